# Optimizing a Trainium2 kernel written in Bass

```python
import math
import jax, jax.numpy as jnp
from jax import lax
import numpy as np

D_MODEL = 1024
BATCH = 4
SEQ = 8192
DEPTH = 2

HGRN_HEADS = 4
HGRN_DK = D_MODEL // (2 * HGRN_HEADS)
HGRN_DV = D_MODEL // (2 * HGRN_HEADS)
RET_HEADS = 4
RET_DK = D_MODEL // (2 * RET_HEADS)
RET_DV = D_MODEL // (2 * RET_HEADS)
LIN_CHUNK = 64
EVEN_COLS = [HGRN_HEADS * HGRN_DK, HGRN_HEADS * HGRN_DK, HGRN_HEADS * HGRN_DV, HGRN_HEADS * HGRN_DV,
             RET_HEADS * RET_DK, RET_HEADS * RET_DK, RET_HEADS * RET_DV, RET_HEADS * RET_DV]
EVEN_MIX_OUT = HGRN_HEADS * HGRN_DV + RET_HEADS * RET_DV

NSA_HD = 64
NSA_HEADS = D_MODEL // NSA_HD
NSA_KV_GROUPS = 2
CMP_BLOCK = 32
CMP_STRIDE = 16
SLC_BLOCK = 64
SLC_TOPN = 16
WINDOW = 512
Q_BLOCK = 128
KVW = NSA_KV_GROUPS * NSA_HD
ODD_COLS = [NSA_HEADS * NSA_HD, KVW, KVW, KVW, KVW, KVW, KVW, 3 * NSA_HEADS]

FFN_HIDDEN = ((8 * D_MODEL // 3 + 255) // 256) * 256
N_EVEN = (DEPTH + 1) // 2
N_ODD = DEPTH // 2
RMS_EPS = 1e-6
NEG_INF = -1e30
FORCE_SCORE = 1e9

kernel_name = "hgrn2_retention_nsa_hybrid"


def rmsnorm(x, g):
    xf = x.astype(jnp.float32)
    y = xf * lax.rsqrt(jnp.mean(xf * xf, axis=-1, keepdims=True) + RMS_EPS)
    return (y * g.astype(jnp.float32)).astype(x.dtype)


def split_cols(a, sizes):
    return jnp.split(a, np.cumsum(sizes)[:-1].tolist(), axis=-1)


def to_chunks(a, c):
    b, t, h, d = a.shape
    return a.reshape(b, t // c, c, h, d).transpose(1, 0, 3, 2, 4)


def from_chunks(a):
    n, b, h, c, d = a.shape
    return a.transpose(1, 0, 3, 2, 4).reshape(b, n * c, h, d)


def hgrn2_chunked(q, k, v, log_f):
    b_, t_, h_, dk = q.shape
    dv = v.shape[-1]
    c = LIN_CHUNK
    causal = jnp.tril(jnp.ones((c, c), dtype=bool))[:, :, None]

    def step(state, inp):
        qi, ki, vi, gi = inp
        cum = jnp.cumsum(gi, axis=2)
        diff = cum[:, :, :, None, :] - cum[:, :, None, :, :]
        decay = jnp.exp(jnp.where(causal, diff, NEG_INF))
        attn = jnp.einsum('bhid,bhjd,bhijd->bhij', qi, ki, decay)
        out = jnp.einsum('bhij,bhjv->bhiv', attn, vi) + jnp.einsum('bhid,bhdv->bhiv', qi * jnp.exp(cum), state)
        last = cum[:, :, -1:, :]
        state = jnp.exp(last[:, :, 0, :])[..., None] * state + jnp.einsum('bhjd,bhjv->bhdv', ki * jnp.exp(last - cum), vi)
        return state, out

    s0 = jnp.zeros((b_, h_, dk, dv), jnp.float32)
    xs = tuple(to_chunks(a.astype(jnp.float32), c) for a in (q, k, v, log_f))
    _, o = lax.scan(step, s0, xs)
    return from_chunks(o)


def retention_chunked(q, k, v, log_gamma):
    b_, t_, h_, dk = q.shape
    dv = v.shape[-1]
    c = LIN_CHUNK
    pos = jnp.arange(c, dtype=jnp.float32)
    rel = pos[:, None] - pos[None, :]
    decay = jnp.where(rel[None] >= 0, jnp.exp(jnp.maximum(rel, 0.0)[None] * log_gamma[:, None, None]), 0.0)
    q_decay = jnp.exp((pos + 1.0)[None, :] * log_gamma[:, None])[..., None]
    k_decay = jnp.exp((c - 1.0 - pos)[None, :] * log_gamma[:, None])[..., None]
    chunk_decay = jnp.exp(c * log_gamma)[:, None, None]

    def step(state, inp):
        qi, ki, vi = inp
        attn = jnp.einsum('bhid,bhjd->bhij', qi, ki) * decay
        out = jnp.einsum('bhij,bhjv->bhiv', attn, vi) + jnp.einsum('bhid,bhdv->bhiv', qi * q_decay, state)
        state = chunk_decay * state + jnp.einsum('bhjd,bhjv->bhdv', ki * k_decay, vi)
        return state, out

    s0 = jnp.zeros((b_, h_, dk, dv), jnp.float32)
    xs = tuple(to_chunks(a.astype(jnp.float32), c) for a in (q, k, v))
    _, o = lax.scan(step, s0, xs)
    return from_chunks(o)


def even_mixer(h, w_in, lower_bound, hgrn_norm, ret_norm, w_out):
    b_, t_, _ = h.shape
    hq, hf, hi, hg, rq, rk, rv, rg = split_cols(h @ w_in, EVEN_COLS)
    heads = lambda a, n: a.reshape(b_, t_, n, -1)
    f = lower_bound + (1.0 - lower_bound) * jax.nn.sigmoid(hf.astype(jnp.float32))
    o_h = hgrn2_chunked(heads(jax.nn.silu(hq), HGRN_HEADS), heads(1.0 - f, HGRN_HEADS),
                        heads(hi, HGRN_HEADS), heads(jnp.log(f), HGRN_HEADS))
    o_h = rmsnorm(o_h, hgrn_norm) * jax.nn.silu(heads(hg, HGRN_HEADS).astype(jnp.float32))
    log_gamma = jnp.log(1.0 - jnp.exp2(-5.0 - jnp.arange(RET_HEADS, dtype=jnp.float32)))
    o_r = retention_chunked(heads(rq, RET_HEADS), heads(rk, RET_HEADS) * (RET_DK ** -0.5),
                            heads(rv, RET_HEADS), log_gamma)
    o_r = rmsnorm(o_r, ret_norm) * jax.nn.silu(heads(rg, RET_HEADS).astype(jnp.float32))
    o = jnp.concatenate([o_h.reshape(b_, t_, -1), o_r.reshape(b_, t_, -1)], axis=-1)
    return o.astype(h.dtype) @ w_out


def alibi_slopes(n):
    return jnp.exp2(-8.0 * jnp.arange(1, n + 1, dtype=jnp.float32) / n)


def compress_blocks(k, pos_emb, w1, w2):
    b_, t_, g_, d = k.shape
    r = CMP_BLOCK // CMP_STRIDE
    ch = k.reshape(b_, t_ // CMP_STRIDE, CMP_STRIDE, g_, d)
    nc = t_ // CMP_STRIDE - r + 1
    blocks = jnp.concatenate([ch[:, j:j + nc] for j in range(r)], axis=2)
    blocks = blocks + pos_emb[None, None, :, None, :]
    flat = blocks.transpose(0, 1, 3, 2, 4).reshape(b_, nc, g_, CMP_BLOCK * d)
    return jax.nn.silu(flat @ w1) @ w2


def cmp_to_slc_overlap(nc, ns):
    c0 = jnp.arange(nc) * CMP_STRIDE
    s0 = jnp.arange(ns) * SLC_BLOCK
    lo = jnp.maximum(c0[:, None], s0[None, :])
    hi = jnp.minimum(c0[:, None] + CMP_BLOCK, s0[None, :] + SLC_BLOCK)
    return (jnp.maximum(hi - lo, 0) / CMP_BLOCK).astype(jnp.float32)


def odd_mixer(h, w_in, cmp_pos_k, cmp_pos_v, cmp_w1_k, cmp_w2_k, cmp_w1_v, cmp_w2_v, w_out):
    b_, t_, _ = h.shape
    H, G, d = NSA_HEADS, NSA_KV_GROUPS, NSA_HD
    R = H // G
    q, kc, vc, ks, vs, kw, vw, gl = split_cols(h @ w_in, ODD_COLS)
    q = q.reshape(b_, t_, H, d) * (d ** -0.5)
    kvh = lambda a: a.reshape(b_, t_, G, d)
    k_cmp = compress_blocks(kvh(kc), cmp_pos_k, cmp_w1_k, cmp_w2_k)
    v_cmp = compress_blocks(kvh(vc), cmp_pos_v, cmp_w1_v, cmp_w2_v)
    nc = k_cmp.shape[1]
    ns = t_ // SLC_BLOCK
    n_sel = min(SLC_TOPN, ns)
    overlap = cmp_to_slc_overlap(nc, ns)
    c_end = jnp.arange(nc) * CMP_STRIDE + CMP_BLOCK - 1
    k_sb = kvh(ks).reshape(b_, ns, SLC_BLOCK, G, d).transpose(0, 3, 1, 2, 4)
    v_sb = kvh(vs).reshape(b_, ns, SLC_BLOCK, G, d).transpose(0, 3, 1, 2, 4)
    kw_pad = jnp.pad(kvh(kw), ((0, 0), (WINDOW, 0), (0, 0), (0, 0)))
    vw_pad = jnp.pad(kvh(vw), ((0, 0), (WINDOW, 0), (0, 0), (0, 0)))
    slopes = alibi_slopes(H).reshape(G, R)
    gates = jax.nn.sigmoid(gl.astype(jnp.float32)).reshape(b_, t_, H, 3)
    gather = jax.vmap(jax.vmap(lambda tbl, ix: tbl[ix]))
    nq = t_ // Q_BLOCK
    f32 = jnp.float32

    def block(args):
        n, qb, gb = args
        qg = qb.reshape(b_, Q_BLOCK, G, R, d)
        t = n * Q_BLOCK + jnp.arange(Q_BLOCK)
        s_c = jnp.einsum('bqgrd,bcgd->bgrqc', qg, k_cmp, preferred_element_type=f32)
        dist_c = (t[:, None] - c_end[None, :]).astype(f32)
        valid_c = dist_c >= 0
        s_c = jnp.where(valid_c, s_c - slopes[None, :, :, None, None] * dist_c, NEG_INF)
        p_c = jax.nn.softmax(s_c, axis=-1) * valid_c
        o_c = jnp.einsum('bgrqc,bcgd->bqgrd', p_c, v_cmp.astype(f32))
        p_slc = jnp.einsum('bgrqc,cj->bgqj', p_c, overlap)
        qblk = t // SLC_BLOCK
        js = jnp.arange(ns)
        forced = (js[None, :] == 0) | (js[None, :] == qblk[:, None]) | (js[None, :] == qblk[:, None] - 1)
        allowed = js[None, :] <= qblk[:, None]
        score = jnp.where(forced, FORCE_SCORE, jnp.where(allowed, p_slc, NEG_INF))
        _, idx = lax.top_k(score, n_sel)
        kg = gather(k_sb, idx)
        vg = gather(v_sb, idx)
        s_s = jnp.einsum('bqgrd,bgqnld->bgrqnl', qg, kg, preferred_element_type=f32)
        spos = idx[..., None] * SLC_BLOCK + jnp.arange(SLC_BLOCK)
        dist_s = (t[None, None, :, None, None] - spos).astype(f32)[:, :, None]
        s_s = jnp.where(dist_s >= 0, s_s - slopes[None, :, :, None, None, None] * dist_s, NEG_INF)
        p_s = jax.nn.softmax(s_s.reshape(b_, G, R, Q_BLOCK, -1), axis=-1).reshape(s_s.shape)
        o_s = jnp.einsum('bgrqnl,bgqnld->bqgrd', p_s, vg.astype(f32))
        kwb = lax.dynamic_slice_in_dim(kw_pad, n * Q_BLOCK, WINDOW + Q_BLOCK, axis=1)
        vwb = lax.dynamic_slice_in_dim(vw_pad, n * Q_BLOCK, WINDOW + Q_BLOCK, axis=1)
        wpos = n * Q_BLOCK - WINDOW + jnp.arange(WINDOW + Q_BLOCK)
        dist_w = t[:, None] - wpos[None, :]
        valid_w = (dist_w >= 0) & (dist_w < WINDOW) & (wpos[None, :] >= 0)
        s_w = jnp.einsum('bqgrd,bkgd->bgrqk', qg, kwb, preferred_element_type=f32)
        s_w = jnp.where(valid_w, s_w - slopes[None, :, :, None, None] * dist_w.astype(f32), NEG_INF)
        p_w = jax.nn.softmax(s_w, axis=-1)
        o_w = jnp.einsum('bgrqk,bkgd->bqgrd', p_w, vwb.astype(f32))
        g5 = gb.reshape(b_, Q_BLOCK, G, R, 3)
        o = g5[..., 0:1] * o_c + g5[..., 1:2] * o_s + g5[..., 2:3] * o_w
        return o.reshape(b_, Q_BLOCK, H * d)

    qblocks = q.reshape(b_, nq, Q_BLOCK, H, d).transpose(1, 0, 2, 3, 4)
    gblocks = gates.reshape(b_, nq, Q_BLOCK, H, 3).transpose(1, 0, 2, 3, 4)
    out = lax.map(block, (jnp.arange(nq), qblocks, gblocks))
    out = out.transpose(1, 0, 2, 3).reshape(b_, t_, H * d)
    return out.astype(h.dtype) @ w_out


def swiglu(h, w_gate_up, w_down):
    g, u = jnp.split(h @ w_gate_up, 2, axis=-1)
    return (jax.nn.silu(g) * u) @ w_down


def setup_inputs(seed: int = 0) -> dict:
    key = jax.random.key(seed)
    ks = jax.random.split(key, 20)
    f32 = jnp.float32
    w = lambda k, shape, fan_in: jax.random.normal(k, shape, f32) * (fan_in ** -0.5)
    gain = lambda k, shape: 1.0 + 0.02 * jax.random.normal(k, shape, f32)
    even_in = sum(EVEN_COLS)
    odd_in = sum(ODD_COLS)
    return {
        'x': jax.random.normal(ks[0], (BATCH, SEQ, D_MODEL), f32),
        'mix_norm': gain(ks[1], (DEPTH, D_MODEL)),
        'ffn_norm': gain(ks[2], (DEPTH, D_MODEL)),
        'final_norm': gain(ks[3], (D_MODEL,)),
        'even_w_in': w(ks[4], (N_EVEN, D_MODEL, even_in), D_MODEL),
        'hgrn_lower_bounds': 0.1 * jax.random.normal(ks[5], (N_EVEN + 1, HGRN_HEADS * HGRN_DK), f32),
        'hgrn_out_norm': gain(ks[6], (N_EVEN, HGRN_DV)),
        'ret_out_norm': gain(ks[7], (N_EVEN, RET_DV)),
        'even_w_out': w(ks[8], (N_EVEN, EVEN_MIX_OUT, D_MODEL), EVEN_MIX_OUT),
        'odd_w_in': w(ks[9], (N_ODD, D_MODEL, odd_in), D_MODEL),
        'cmp_pos_k': 0.1 * jax.random.normal(ks[10], (N_ODD, CMP_BLOCK, NSA_HD), f32),
        'cmp_pos_v': 0.1 * jax.random.normal(ks[11], (N_ODD, CMP_BLOCK, NSA_HD), f32),
        'cmp_w1_k': w(ks[12], (N_ODD, CMP_BLOCK * NSA_HD, NSA_HD), CMP_BLOCK * NSA_HD),
        'cmp_w2_k': w(ks[13], (N_ODD, NSA_HD, NSA_HD), NSA_HD),
        'cmp_w1_v': w(ks[14], (N_ODD, CMP_BLOCK * NSA_HD, NSA_HD), CMP_BLOCK * NSA_HD),
        'cmp_w2_v': w(ks[15], (N_ODD, NSA_HD, NSA_HD), NSA_HD),
        'odd_w_out': w(ks[16], (N_ODD, NSA_HEADS * NSA_HD, D_MODEL), NSA_HEADS * NSA_HD),
        'ffn_w_gate_up': w(ks[17], (DEPTH, D_MODEL, 2 * FFN_HIDDEN), D_MODEL),
        'ffn_w_down': w(ks[18], (DEPTH, FFN_HIDDEN, D_MODEL), FFN_HIDDEN),
    }


def reference(x, mix_norm, ffn_norm, final_norm, even_w_in, hgrn_lower_bounds, hgrn_out_norm,
              ret_out_norm, even_w_out, odd_w_in, cmp_pos_k, cmp_pos_v, cmp_w1_k, cmp_w2_k,
              cmp_w1_v, cmp_w2_v, odd_w_out, ffn_w_gate_up, ffn_w_down):
    lb_all = jnp.cumsum(jax.nn.softmax(hgrn_lower_bounds.astype(jnp.float32), axis=0), axis=0)
    h = x
    for layer in range(DEPTH):
        hn = rmsnorm(h, mix_norm[layer])
        if layer % 2 == 0:
            e = layer // 2
            h = h + even_mixer(hn, even_w_in[e], lb_all[e], hgrn_out_norm[e], ret_out_norm[e], even_w_out[e])
        else:
            o = layer // 2
            h = h + odd_mixer(hn, odd_w_in[o], cmp_pos_k[o], cmp_pos_v[o], cmp_w1_k[o], cmp_w2_k[o],
                              cmp_w1_v[o], cmp_w2_v[o], odd_w_out[o])
        h = h + swiglu(rmsnorm(h, ffn_norm[layer]), ffn_w_gate_up[layer], ffn_w_down[layer])
    return rmsnorm(h, final_norm)
```

```python
import numpy as np
from contextlib import ExitStack
import concourse.bass as bass
import concourse.mybir as mybir
from concourse.bass_utils import run_bass_kernel_spmd
import ml_dtypes

F32 = mybir.dt.float32
BF16 = mybir.dt.bfloat16
AF = mybir.ActivationFunctionType
ALU = mybir.AluOpType
AX = mybir.AxisListType

D = 1024
T = 8192
B = 4
FH = 2816
NKC = D // 128
NHC = FH // 128
EPS = 1e-6
TT_F = 256


class _Op:
    __slots__ = ("fn", "waits", "needs_inc", "dma_slot", "semval")

    def __init__(self, fn, waits, dma_slot):
        self.fn = fn
        self.waits = waits
        self.needs_inc = False
        self.dma_slot = dma_slot
        self.semval = 0


class _Slot:
    __slots__ = ("sem", "count", "inc")

    def __init__(self, sem, inc=16):
        self.sem = sem
        self.count = 0
        self.inc = inc


class Prog:
    ENGS = ("pe", "act", "dve", "pool", "sp")

    def __init__(self, nc, n_dma_slots=12):
        self.nc = nc
        self.es = ExitStack()
        self.ops = {e: [] for e in self.ENGS}
        self.known = {e: {} for e in self.ENGS}
        self.sems = {e: self.es.enter_context(nc.semaphore("sem_" + e)) for e in self.ENGS}
        self.slots = {}
        self.slot_rr = {}
        for q in ("sp", "pool", "act"):
            self.slots[q] = [
                _Slot(self.es.enter_context(nc.semaphore("dma_%s_%d" % (q, i))))
                for i in range(n_dma_slots)
            ]
            self.slot_rr[q] = 0
        self.cc_slot = _Slot(self.es.enter_context(nc.semaphore("cc_sem")), 1)
        self.bar = self.es.enter_context(nc.semaphore("phase_bar"))
        self.phase_no = 0
        self.semcount = {e: 0 for e in self.ENGS}
        self.pes = ExitStack()
        self.res_w = {}
        self.res_r = {}
        self.bank_last = {}
        self._names = 0

    def sbuf(self, shape, dtype, name=None):
        self._names += 1
        return self.pes.enter_context(
            self.nc.sbuf_tensor("%s_%d" % (name or "sb", self._names), list(shape), dtype))

    def psum(self, shape, dtype, name=None):
        self._names += 1
        return self.pes.enter_context(
            self.nc.psum_tensor("%s_%d" % (name or "ps", self._names), list(shape), dtype))

    def _need(self, eng, tok, waits):
        if tok[0] == "E":
            _, e2, idx = tok
            if e2 == eng and eng == "pe":
                return
            k = ("E", e2)
            if self.known[eng].get(k, -1) >= idx:
                return
            self.known[eng][k] = idx
            self.ops[e2][idx].needs_inc = True
            waits.append(tok)
        else:
            _, slot, val = tok
            k = ("D", id(slot))
            if self.known[eng].get(k, 0) >= val:
                return
            self.known[eng][k] = val
            waits.append(tok)

    @staticmethod
    def _banks(*aps):
        out = []
        for a in aps:
            try:
                if str(a.space) == "PSUM":
                    out.append(a.name)
            except AttributeError:
                pass
        return out

    def op(self, eng, fn, reads=(), writes=(), dma=False, banks=(), cc=False):
        deps = []
        for bk in banks:
            for e2, t in self.bank_last.get(bk, {}).items():
                if e2 != eng:
                    deps.append(t)
        for r in reads:
            t = self.res_w.get(r)
            if t is not None:
                deps.append(t)
        for w in writes:
            t = self.res_w.get(w)
            if t is not None:
                deps.append(t)
            rr = self.res_r.get(w)
            if rr:
                deps.extend(rr.values())
        waits = []
        for t in deps:
            self._need(eng, t, waits)
        slot = None
        if cc:
            slot = self.cc_slot
            slot.count += 1
            tok = ("D", slot, slot.count)
            rkey = ("D", id(slot))
        elif dma:
            lst = self.slots[eng]
            slot = lst[self.slot_rr[eng] % len(lst)]
            self.slot_rr[eng] += 1
            if slot.count:
                self._need(eng, ("D", slot, slot.count), waits)
            slot.count += 16
            tok = ("D", slot, slot.count)
            rkey = ("D", id(slot))
        else:
            tok = ("E", eng, len(self.ops[eng]))
            rkey = ("E", eng)
        self.ops[eng].append(_Op(fn, waits, slot))
        for bk in banks:
            self.bank_last.setdefault(bk, {})[eng] = tok
        for r in reads:
            self.res_r.setdefault(r, {})[rkey] = tok
        for w in writes:
            self.res_w[w] = tok
            self.res_r[w] = {}
        return tok

    def mm(self, out, lhsT, rhs, start, stop, reads, writes):
        return self.op("pe", lambda h: h.matmul(out, lhsT, rhs, start=start, stop=stop), reads, writes,
                       banks=self._banks(out))

    def transpose(self, out, in_, ident, reads, writes):
        return self.op("pe", lambda h: h.transpose(out, in_, ident), reads, writes, banks=self._banks(out))

    def act(self, out, in_, func, reads, writes, bias=None, scale=None, eng="act"):
        kw = {}
        if bias is not None:
            kw["bias"] = bias
        if scale is not None:
            kw["scale"] = scale
        return self.op(eng, lambda h: h.activation(out, in_, func, **kw), reads, writes, banks=self._banks(out, in_))

    def tt(self, eng, out, in0, in1, op, reads, writes):
        return self.op(eng, lambda h: h.tensor_tensor(out, in0, in1, op), reads, writes, banks=self._banks(out, in0, in1))

    def ts(self, eng, out, in0, s1, s2, op0, op1, reads, writes):
        if s2 is None:
            return self.op(eng, lambda h: h.tensor_scalar(out, in0, s1, None, op0), reads, writes, banks=self._banks(out, in0, s1))
        return self.op(eng, lambda h: h.tensor_scalar(out, in0, s1, s2, op0, op1), reads, writes, banks=self._banks(out, in0, s1, s2))

    def stt(self, out, in0, scalar, in1, op0, op1, reads, writes):
        return self.op("dve", lambda h: h.scalar_tensor_tensor(out, in0, scalar, in1, op0, op1), reads, writes,
                       banks=self._banks(out, in0, scalar, in1))

    def copy(self, eng, out, in_, reads, writes):
        if eng == "act":
            return self.op(eng, lambda h: h.copy(out, in_), reads, writes, banks=self._banks(out, in_))
        return self.op(eng, lambda h: h.tensor_copy(out, in_), reads, writes, banks=self._banks(out, in_))

    def recip(self, out, in_, reads, writes):
        return self.op("dve", lambda h: h.reciprocal(out, in_), reads, writes, banks=self._banks(out, in_))

    def memset(self, eng, ap, val, writes):
        return self.op(eng, lambda h: h.memset(ap, val), (), writes)

    def dma(self, q, out, in_, reads, writes):
        return self.op(q, lambda h: h.dma_start(out=out, in_=in_), reads, writes, dma=True)

    def allgather(self, out, in_, groups, reads, writes):
        return self.op("pool", lambda h: h.collective_compute("AllGather", ALU.bypass, replica_groups=groups,
                                                              ins=[in_], outs=[out]), reads, writes, cc=True)

    def end_phase(self, last=False):
        nc = self.nc
        self.phase_no += 1
        final = {}
        for e in self.ENGS:
            comp = [o for o in self.ops[e] if o.dma_slot is None]
            if comp:
                comp[-1].needs_inc = True
            c = self.semcount[e]
            for o in self.ops[e]:
                if o.dma_slot is None and o.needs_inc:
                    c += 1
                    o.semval = c
            self.semcount[e] = c
            final[e] = c if comp else None
        fin = []
        for q in self.slots:
            for s_ in self.slots[q]:
                if s_.count:
                    fin.append((s_.sem, s_.count))
        if self.cc_slot.count:
            fin.append((self.cc_slot.sem, self.cc_slot.count))
        phase_no = self.phase_no

        def run(e, h):
            for o in self.ops[e]:
                for t in o.waits:
                    if t[0] == "E":
                        h.wait_ge(self.sems[t[1]], self.ops[t[1]][t[2]].semval)
                    else:
                        h.wait_ge(t[1].sem, t[2])
                ins = o.fn(h)
                if o.dma_slot is not None:
                    if o.dma_slot.inc == 1:
                        ins.then_inc(o.dma_slot.sem)
                    else:
                        ins.then_inc(o.dma_slot.sem, 16)
                elif o.needs_inc:
                    ins.then_inc(self.sems[e], 1)
            if final[e] is not None:
                h.wait_ge(self.sems[e], final[e])
            if e == "sp":
                for sem, v in fin:
                    h.wait_ge(sem, v)
            if not last:
                h.sem_inc(self.bar, 1)
                h.wait_ge(self.bar, 5 * phase_no)

        with nc.Block() as block:
            @block.tensor
            def _(h):
                run("pe", h)

            @block.scalar
            def _(h):
                run("act", h)

            @block.vector
            def _(h):
                run("dve", h)

            @block.gpsimd
            def _(h):
                run("pool", h)

            @block.sync
            def _(h):
                run("sp", h)
        self.ops = {e: [] for e in self.ENGS}
        for e in self.ENGS:
            self.known[e] = {k: v for k, v in self.known[e].items() if k[0] == "D"}
        self.res_w = {}
        self.res_r = {}
        self.bank_last = {}
        self.pes.close()
        self.pes = ExitStack()
        if last:
            self.es.close()

    def emit(self):
        self.end_phase(last=True)


def emit_ffn(P, nc, ntok, mode, dr):
    TT = TT_F
    ntile = ntok // TT
    w_out = P.sbuf([128, NKC, D], BF16, "w_out")
    w_gu = P.sbuf([128, NKC, 2 * FH], BF16, "w_gu")
    w_dn = P.sbuf([128, NHC, D], BF16, "w_dn")
    g_ffn = P.sbuf([128, NKC], F32, "g_ffn")
    g_nxt = P.sbuf([128, NKC], F32, "g_nxt")
    ones = P.sbuf([128, 128], BF16, "ones")
    P.memset("pool", ones[:], 1.0, ["ones"])
    P.dma("sp", g_ffn[:], dr["g_ffn"], [], ["g_ffn"])
    P.dma("sp", g_nxt[:], dr["g_nxt"], [], ["g_nxt"])
    wo_v = dr["w_out"].rearrange("(c p) n -> p c n", p=128)
    wo_perm = dr.get("wo_perm", list(range(NKC)))
    for c in range(NKC):
        P.dma("pool", w_out[:, c, :], wo_v[:, wo_perm[c], :], [], [("w_out", c)])
    wg_v = dr["w_gu"].rearrange("(c p) n -> p c n", p=128)
    for c in range(NKC):
        for hf in range(2):
            P.dma("pool", w_gu[:, c, hf * FH:(hf + 1) * FH], wg_v[:, c, hf * FH:(hf + 1) * FH], [], [("w_gu", c, hf)])
    wd_v = dr["w_dn"].rearrange("(c p) n -> p c n", p=128)
    for c in range(NHC):
        P.dma("pool", w_dn[:, c, :], wd_v[:, c, :], [], [("w_dn", c)])

    NB = 2
    xt = [P.sbuf([128, NKC, TT], F32, "xt") for _ in range(NB)]
    ot = [P.sbuf([128, NKC, TT], BF16, "ot") for _ in range(NB)]
    hn = P.sbuf([128, NKC, TT], BF16, "hn")
    actb = P.sbuf([128, NHC, TT], BF16, "actb")
    sg = [P.sbuf([128, TT], F32, "sg") for _ in range(2)]
    rstd = P.sbuf([128, TT], F32, "rstd")
    ps_acc = [P.psum([128, TT], F32, "ps_acc") for _ in range(2)]
    ps_g = [P.psum([128, TT], F32, "ps_g") for _ in range(2)]
    ps_u = [P.psum([128, TT], F32, "ps_u") for _ in range(2)]
    ps_s = P.psum([128, TT], F32, "ps_s")

    xT_v = dr["xT"].rearrange("(c p) t -> p c t", p=128)
    if "o_srcs" in dr:
        o_srcs = dr["o_srcs"]
    else:
        oT_v = dr["oT"].rearrange("(c p) t -> p c t", p=128)
        o_srcs = [lambda i: oT_v[:, :, i * TT:(i + 1) * TT]]
    if mode == "mid":
        if "h_dst" in dr:
            h_dst = dr["h_dst"]
        else:
            oh_v = dr["out_h"].rearrange("(c p) t -> p c t", p=128)
            h_dst = lambda i, m: oh_v[:, m, i * TT:(i + 1) * TT]
    if "n_dst" in dr:
        n_dst = dr["n_dst"]
    else:
        on_v = dr["out_n"].rearrange("(c p) t -> p c t", p=128)
        n_dst = lambda i, c: on_v[:, c, i * TT:(i + 1) * TT]
    if len(o_srcs) == 2:
        selv = P.sbuf([128, 2], F32, "selv")
        otb = P.sbuf([128, NKC, TT], BF16, "otb")
        P.dma("sp", selv[:], dr["sel"], [], ["selv"])

    def load(i):
        b = i % NB
        sl = slice(i * TT, (i + 1) * TT)
        for c in range(NKC):
            P.dma("sp", xt[b][:, c, :], xT_v[:, c, sl], [], [("xt", b, c)])
        P.dma("sp", ot[b][:], o_srcs[0](i), [], [("ot", b)])
        if len(o_srcs) == 2:
            P.dma("sp", otb[:], o_srcs[1](i), [], ["otb"])
            P.ts("dve", otb[:], otb[:], selv[:, 1:2], None, ALU.mult, None, ["otb", "selv"], ["otb"])
            for c in range(NKC):
                P.stt(ot[b][:, c, :], ot[b][:, c, :], selv[:, 0:1], otb[:, c, :], ALU.mult, ALU.add,
                      [("ot", b), "otb", "selv"], [("ot", b)])

    acc_i = [0]
    gu_i = [0]

    def rms(b, gam, gkey, dst, dkey):
        for c in range(NKC):
            P.act(actb[:, c, :], xt[b][:, c, :], AF.Square, [("xt", b, c)], [("actb", c)])
        for c in range(NKC):
            P.mm(ps_s[:], ones[:], actb[:, c, :], c == 0, c == NKC - 1, ["ones", ("actb", c)], ["ps_s"])
        P.act(rstd[:], ps_s[:], AF.Sqrt, ["ps_s"], ["rstd"], bias=EPS, scale=1.0 / D)
        P.recip(rstd[:], rstd[:], ["rstd"], ["rstd"])
        for c in range(NKC):
            P.stt(dst(c), xt[b][:, c, :], gam[:, c:c + 1], rstd[:], ALU.mult, ALU.mult,
                  [("xt", b, c), "rstd", gkey], [dkey(c)])

    load(0)
    for i in range(ntile):
        b = i % NB
        sl = slice(i * TT, (i + 1) * TT)
        if i + 1 < ntile:
            load(i + 1)
        for m in range(NKC):
            pa = acc_i[0] % 2
            acc_i[0] += 1
            for k in range(NKC):
                P.mm(ps_acc[pa][:], w_out[:, k, m * 128:(m + 1) * 128], ot[b][:, k, :], k == 0, k == NKC - 1,
                     [("w_out", k), ("ot", b)], [("ps_acc", pa)])
            P.tt("dve", xt[b][:, m, :], xt[b][:, m, :], ps_acc[pa][:], ALU.add,
                 [("xt", b, m), ("ps_acc", pa)], [("xt", b, m)])
        rms(b, g_ffn, "g_ffn", lambda c: hn[:, c, :], lambda c: ("hn", c))
        for j in range(NHC):
            pg = gu_i[0] % 2
            gu_i[0] += 1
            for k in range(NKC):
                P.mm(ps_g[pg][:], w_gu[:, k, j * 128:(j + 1) * 128], hn[:, k, :], k == 0, k == NKC - 1,
                     [("w_gu", k, 0), ("hn", k)], [("ps_g", pg)])
            for k in range(NKC):
                P.mm(ps_u[pg][:], w_gu[:, k, FH + j * 128:FH + (j + 1) * 128], hn[:, k, :], k == 0, k == NKC - 1,
                     [("w_gu", k, 1), ("hn", k)], [("ps_u", pg)])
            P.act(sg[pg][:], ps_g[pg][:], AF.Silu, [("ps_g", pg)], [("sg", pg)])
            P.tt("dve", actb[:, j, :], sg[pg][:], ps_u[pg][:], ALU.mult, [("sg", pg), ("ps_u", pg)], [("actb", j)])
        for m in range(NKC):
            pa = acc_i[0] % 2
            acc_i[0] += 1
            for j in range(NHC):
                P.mm(ps_acc[pa][:], w_dn[:, j, m * 128:(m + 1) * 128], actb[:, j, :], j == 0, j == NHC - 1,
                     [("w_dn", j), ("actb", j)], [("ps_acc", pa)])
            P.tt("dve", xt[b][:, m, :], xt[b][:, m, :], ps_acc[pa][:], ALU.add,
                 [("xt", b, m), ("ps_acc", pa)], [("xt", b, m)])
            if mode == "mid":
                P.dma("sp", h_dst(i, m), xt[b][:, m, :], [("xt", b, m)], [("out_h", i, m)])
        if mode == "mid":
            rms(b, g_nxt, "g_nxt", lambda c: hn[:, c, :], lambda c: ("hn", c))
            for c in range(NKC):
                P.dma("sp", n_dst(i, c), hn[:, c, :], [("hn", c)], [("out_n", i, c)])
            if "after_tile" in dr:
                dr["after_tile"](i)
        else:
            rms(b, g_nxt, "g_nxt", lambda c: xt[b][:, c, :], lambda c: ("xt", b, c))
            for c in range(NKC):
                P.dma("sp", n_dst(i, c), xt[b][:, c, :], [("xt", b, c)], [("out_n", i, c)])


def build_ffn(ntok, mode):
    nc = bass.Bass("TRN2", target_bir_lowering=False)
    dr = {}
    dr["oT"] = nc.dram_tensor("oT", [D, ntok], BF16, kind="ExternalInput").ap()
    dr["xT"] = nc.dram_tensor("xT", [D, ntok], F32, kind="ExternalInput").ap()
    dr["w_out"] = nc.dram_tensor("w_out", [D, D], F32, kind="ExternalInput").ap()
    dr["g_ffn"] = nc.dram_tensor("g_ffn", [128, NKC], F32, kind="ExternalInput").ap()
    dr["g_nxt"] = nc.dram_tensor("g_nxt", [128, NKC], F32, kind="ExternalInput").ap()
    dr["w_gu"] = nc.dram_tensor("w_gu", [D, 2 * FH], F32, kind="ExternalInput").ap()
    dr["w_dn"] = nc.dram_tensor("w_dn", [FH, D], F32, kind="ExternalInput").ap()
    if mode == "mid":
        dr["out_h"] = nc.dram_tensor("out_h", [D, ntok], F32, kind="ExternalOutput").ap()
        dr["out_n"] = nc.dram_tensor("out_n", [D, ntok], BF16, kind="ExternalOutput").ap()
    else:
        dr["out_n"] = nc.dram_tensor("out_n", [D, ntok], F32, kind="ExternalOutput").ap()
    P = Prog(nc)
    emit_ffn(P, nc, ntok, mode, dr)
    P.emit()
    return nc


def gvec(g):
    return np.ascontiguousarray(np.asarray(g, np.float32).reshape(NKC, 128).T)


TT1 = 512
LC = 64
C1_TRI = 0
C1_RST = 256
C1_DQ = 768
C1_EXL = 768 + 6 * 512
C1_N = C1_EXL + 16


def emit_mix0(P, nc, ntok, dr):
    ntile = ntok // TT1
    w_in = P.sbuf([128, NKC, 2048], BF16, "w_in")
    g_mix = P.sbuf([128, NKC], F32, "g_mix")
    cst = P.sbuf([128, C1_N], F32, "cst")
    ident = P.sbuf([128, 128], BF16, "ident")
    ones = P.sbuf([128, 128], BF16, "ones")
    lbp = P.sbuf([128, 4], F32, "lbp")
    lb = P.sbuf([128, 2], F32, "lb")
    oml = P.sbuf([128, 2], F32, "oml")
    wn = P.sbuf([128, 2], F32, "wn")
    P.memset("pool", ones[:], 1.0, ["ones"])
    P.dma("sp", g_mix[:], dr["g_mix"], [], ["g_mix"])
    P.dma("sp", cst[:], dr["cst"], [], ["cst"])
    P.dma("sp", lbp[:], dr["lbp"], [], ["lbp"])
    P.dma("sp", wn[:], dr["wn"], [], ["wn"])
    P.dma("pool", ident[:], dr["ident"], [], ["ident"])
    wi_v = dr["w_in"].rearrange("(c p) n -> p c n", p=128)
    for c in range(NKC):
        P.dma("pool", w_in[:, c, :], wi_v[:, c, :], [], [("w_in", c)])
    P.tt("dve", lb[:], lbp[:, 0:2], lbp[:, 2:4], ALU.subtract, ["lbp"], ["lb"])
    P.act(lb[:], lb[:], AF.Sigmoid, ["lb"], ["lb"])
    P.ts("dve", oml[:], lb[:], -1.0, 1.0, ALU.mult, ALU.add, ["lb"], ["oml"])

    tri4 = cst[0:64, C1_TRI:C1_TRI + 256]
    rst = cst[:, C1_RST:C1_RST + 512]

    def dtab(kind, r):
        o = C1_DQ + (kind * 2 + r) * 512
        return cst[:, o:o + 512]

    xt = [P.sbuf([128, NKC, TT1], F32, "xt") for _ in range(2)]
    hn = P.sbuf([128, NKC, TT1], BF16, "hn")
    sq = P.sbuf([128, NKC, TT1], BF16, "sq")
    rstd = P.sbuf([128, TT1], F32, "rstd")
    t_a = P.sbuf([128, TT1], F32, "t_a")
    t_f = P.sbuf([128, TT1], F32, "t_f")
    t_g = P.sbuf([128, TT1], F32, "t_g")
    t_c = P.sbuf([128, TT1], F32, "t_c")
    t_n = P.sbuf([128, TT1], F32, "t_n")
    t_k = P.sbuf([128, TT1], F32, "t_k")
    ecum = [P.sbuf([128, TT1], F32, "ecum") for _ in range(2)]
    qe = [P.sbuf([128, TT1], BF16, "qe") for _ in range(4)]
    ke = [P.sbuf([128, TT1], BF16, "ke") for _ in range(4)]
    kl = [P.sbuf([128, TT1], BF16, "kl") for _ in range(4)]
    gs = [P.sbuf([128, TT1], F32, "gs") for _ in range(4)]
    v_tok = P.sbuf([128, 4, 512], BF16, "v_tok")
    kl_tok = P.sbuf([128, 4, 4, 128], BF16, "kl_tok")
    atm = P.sbuf([128, 4, 64], BF16, "atm")
    S = P.sbuf([128, 4, 128], F32, "S")
    S_bf = [P.sbuf([128, 4, 128], BF16, "S_bf") for _ in range(2)]
    osq = P.sbuf([128, TT1], BF16, "osq")
    ors = P.sbuf([128, TT1], F32, "ors")
    otmp = P.sbuf([128, TT1], F32, "otmp")
    of = [P.sbuf([128, 4, TT1], BF16, "of") for _ in range(2)]

    ps_in = [P.psum([128, TT1], F32, "ps_in") for _ in range(2)]
    ps_o = [P.psum([128, TT1], F32, "ps_o") for _ in range(4)]
    ps_ds = P.psum([128, 4, 128], F32, "ps_ds")
    ps_misc = P.psum([128, 512], F32, "ps_misc")
    ps_at = ps_misc[0:64, 0:256].rearrange("p (u c) -> p u c", c=LC)
    ps_kt = ps_misc[:, 256:512].bitcast(BF16).rearrange("p (s d) -> p s d", d=128)

    P.memset("dve", S[:], 0.0, ["S"])
    P.memset("pool", S_bf[0][:], 0.0, [("S_bf", 0)])

    xT_v = dr["xT"].rearrange("(c p) t -> p c t", p=128)
    if "o_dst" in dr:
        o_dst = dr["o_dst"]
    else:
        o_v = dr["oT_out"].rearrange("(u p) t -> p u t", p=128)
        o_dst = lambda i, u: o_v[:, u, i * TT1:(i + 1) * TT1]

    def load(i):
        b = i % 2
        sl = slice(i * TT1, (i + 1) * TT1)
        for c in range(NKC):
            P.dma("sp", xt[b][:, c, :], xT_v[:, c, sl], [], [("xt", b, c)])

    pin = [0]

    def proj_fm(blk):
        p = pin[0] % 2
        pin[0] += 1
        for k in range(NKC):
            P.mm(ps_in[p][:], w_in[:, k, blk * 128:(blk + 1) * 128], hn[:, k, :], k == 0, k == NKC - 1,
                 [("w_in", k)] + [("hn", k)], [("ps_in", p)])
        return p

    sbi = [0]
    load(0)
    for i in range(ntile):
        b = i % 2
        sl = slice(i * TT1, (i + 1) * TT1)
        if i + 1 < ntile:
            load(i + 1)
        for c in range(NKC):
            P.act(sq[:, c, :], xt[b][:, c, :], AF.Square, [("xt", b, c)], [("sq", c)])
        p = pin[0] % 2
        pin[0] += 1
        for c in range(NKC):
            P.mm(ps_in[p][:], ones[:], sq[:, c, :], c == 0, c == NKC - 1, ["ones", ("sq", c)], [("ps_in", p)])
        P.act(rstd[:], ps_in[p][:], AF.Sqrt, [("ps_in", p)], ["rstd"], bias=EPS, scale=1.0 / D)
        P.recip(rstd[:], rstd[:], ["rstd"], ["rstd"])
        for c in range(NKC):
            P.stt(hn[:, c, :], xt[b][:, c, :], g_mix[:, c:c + 1], rstd[:], ALU.mult, ALU.mult,
                  [("xt", b, c), "rstd", "g_mix"], [("hn", c)])
        for s in range(4):
            p = pin[0] % 2
            pin[0] += 1
            for k in range(NKC):
                P.mm(ps_in[p][:], hn[:, k, s * 128:(s + 1) * 128], w_in[:, k, 1536:2048], k == 0, k == NKC - 1,
                     [("w_in", k), ("hn", k)], [("ps_in", p)])
            P.copy("act", v_tok[:, s, :], ps_in[p][:], [("ps_in", p)], [("v_tok", s)])
        for u in range(4):
            if u < 2:
                p = proj_fm(u)
                P.act(t_a[:], ps_in[p][:], AF.Silu, [("ps_in", p)], ["t_a"])
                p = proj_fm(4 + u)
                P.act(t_f[:], ps_in[p][:], AF.Sigmoid, [("ps_in", p)], ["t_f"])
                P.ts("dve", t_f[:], t_f[:], oml[:, u:u + 1], lb[:, u:u + 1], ALU.mult, ALU.add,
                     ["t_f", "oml", "lb"], ["t_f"])
                P.act(t_g[:], t_f[:], AF.Ln, ["t_f"], ["t_g"])
                P.op("dve", lambda h, o=t_c[:], m=rst, g=t_g[:]: h.tensor_tensor_scan(o, m, g, 0.0, ALU.mult, ALU.add),
                     ["cst", "t_g"], ["t_c"])
                P.act(ecum[u][:], t_c[:], AF.Exp, ["t_c"], [("ecum", u)])
                P.act(t_n[:], t_c[:], AF.Exp, ["t_c"], ["t_n"], scale=-1.0)
                P.ts("dve", t_k[:], t_f[:], -1.0, 1.0, ALU.mult, ALU.add, ["t_f"], ["t_k"])
                P.tt("dve", qe[u][:], t_a[:], ecum[u][:], ALU.mult, ["t_a", ("ecum", u)], [("qe", u)])
                P.tt("dve", ke[u][:], t_k[:], t_n[:], ALU.mult, ["t_k", "t_n"], [("ke", u)])
                ev = ecum[u][:].rearrange("p (n c) -> p n c", c=LC)[:, :, LC - 1:LC].to_broadcast([128, TT1 // LC, LC])
                P.tt("dve", kl[u][:].rearrange("p (n c) -> p n c", c=LC), ke[u][:].rearrange("p (n c) -> p n c", c=LC),
                     ev, ALU.mult, [("ke", u), ("ecum", u)], [("kl", u)])
            else:
                r = u - 2
                p = proj_fm(u)
                P.tt("dve", qe[u][:], ps_in[p][:], dtab(0, r), ALU.mult, [("ps_in", p), "cst"], [("qe", u)])
                p = proj_fm(4 + u)
                P.tt("dve", ke[u][:], ps_in[p][:], dtab(1, r), ALU.mult, [("ps_in", p), "cst"], [("ke", u)])
                P.tt("dve", kl[u][:], ps_in[p][:], dtab(2, r), ALU.mult, [("ps_in", p), "cst"], [("kl", u)])
            p = proj_fm(8 + u)
            P.act(gs[u][:], ps_in[p][:], AF.Silu, [("ps_in", p)], [("gs", u)])
            for s in range(4):
                P.transpose(ps_kt[:, s, :], kl[u][:, s * 128:(s + 1) * 128], ident[:],
                            [("kl", u), "ident"], ["ps_kt"])
            P.copy("act", kl_tok[:, :, u, :], ps_kt, ["ps_kt"], [("kl_tok", u)])
        for n in range(TT1 // LC):
            c0 = n * LC
            s = n // 2
            r0 = (n % 2) * LC
            for u in range(4):
                P.mm(ps_ds[:, u, :], kl_tok[r0:r0 + LC, s, u, :], v_tok[r0:r0 + LC, s, u * 128:(u + 1) * 128], True, True,
                     [("kl_tok", u), ("v_tok", s)], ["ps_ds"])
            for u in range(4):
                P.mm(ps_at[:, u, :], ke[u][:, c0:c0 + LC], qe[u][:, c0:c0 + LC], True, True,
                     [("ke", u), ("qe", u)], ["ps_at"])
            P.tt("dve", atm[r0:r0 + LC], ps_at, cst[r0:r0 + LC, C1_TRI:C1_TRI + 256].rearrange("p (u c) -> p u c", c=LC),
                 ALU.mult, ["ps_at", "cst"], [("atm", n % 2)])
            sb = sbi[0] % 2
            for u in range(4):
                P.mm(ps_o[u][:, c0:c0 + LC], v_tok[r0:r0 + LC, s, u * 128:(u + 1) * 128], atm[r0:r0 + LC, u, :], True, False,
                     [("v_tok", s), ("atm", n % 2)], [("ps_o", u)])
                P.mm(ps_o[u][:, c0:c0 + LC], S_bf[sb][:, u, :], qe[u][:, c0:c0 + LC], False, True,
                     [("S_bf", sb), ("qe", u)], [("ps_o", u)])
            for u in range(4):
                if u < 2:
                    ex = ecum[u][:, c0 + LC - 1:c0 + LC]
                    rk = [("ecum", u)]
                else:
                    ex = cst[:, C1_EXL + (u - 2) * 8:C1_EXL + (u - 2) * 8 + 1]
                    rk = ["cst"]
                P.stt(S[:, u, :], S[:, u, :], ex, ps_ds[:, u, :], ALU.mult, ALU.add, ["S", "ps_ds"] + rk, ["S"])
            sbi[0] += 1
            P.copy("act", S_bf[sbi[0] % 2][:], S[:], ["S"], [("S_bf", sbi[0] % 2)])
        ob = i % 2
        for u in range(4):
            P.act(osq[:], ps_o[u][:], AF.Square, [("ps_o", u)], ["osq"])
            p = pin[0] % 2
            pin[0] += 1
            P.mm(ps_in[p][:], ones[:], osq[:], True, True, ["ones", "osq"], [("ps_in", p)])
            P.act(ors[:], ps_in[p][:], AF.Sqrt, [("ps_in", p)], ["ors"], bias=EPS, scale=1.0 / 128)
            P.recip(ors[:], ors[:], ["ors"], ["ors"])
            P.tt("dve", otmp[:], ps_o[u][:], ors[:], ALU.mult, [("ps_o", u), "ors"], ["otmp"])
            wc = 0 if u < 2 else 1
            P.stt(of[ob][:, u, :], otmp[:], wn[:, wc:wc + 1], gs[u][:], ALU.mult, ALU.mult,
                  ["otmp", "wn", ("gs", u)], [("of", ob, u)])
            P.dma("sp", o_dst(i, u), of[ob][:, u, :], [("of", ob, u)], [("oT_out", i, u)])
        if "after_tile" in dr:
            dr["after_tile"](i)


def build_mix0(ntok):
    nc = bass.Bass("TRN2", target_bir_lowering=False)
    dr = {}
    dr["xT"] = nc.dram_tensor("xT", [D, ntok], F32, kind="ExternalInput").ap()
    dr["w_in"] = nc.dram_tensor("w_in", [D, 2048], F32, kind="ExternalInput").ap()
    dr["g_mix"] = nc.dram_tensor("g_mix", [128, NKC], F32, kind="ExternalInput").ap()
    dr["cst"] = nc.dram_tensor("cst", [128, C1_N], F32, kind="ExternalInput").ap()
    dr["ident"] = nc.dram_tensor("ident", [128, 128], F32, kind="ExternalInput").ap()
    dr["lbp"] = nc.dram_tensor("lbp", [128, 4], F32, kind="ExternalInput").ap()
    dr["wn"] = nc.dram_tensor("wn", [128, 2], F32, kind="ExternalInput").ap()
    dr["oT_out"] = nc.dram_tensor("oT_out", [512, ntok], BF16, kind="ExternalOutput").ap()
    P = Prog(nc)
    emit_mix0(P, nc, ntok, dr)
    P.emit()
    return nc


def mix0_consts(hh):
    c = np.zeros((128, C1_N), np.float32)
    j = np.arange(64)[:, None]
    i = np.arange(64)[None, :]
    tri = (j <= i).astype(np.float32)
    c[0:64, C1_TRI:C1_TRI + 256] = np.tile(tri, (1, 4))
    c[64:128, C1_TRI:C1_TRI + 256] = np.tile(tri, (1, 4))
    pos = np.arange(512) % LC
    c[:, C1_RST:C1_RST + 512] = (pos != 0).astype(np.float32)[None, :]
    for r in range(2):
        hidx = 2 * hh + r
        lg = np.log(np.float32(1.0) - np.exp2(np.float32(-5.0 - hidx))).astype(np.float32)
        qd = np.exp((pos + 1.0).astype(np.float32) * lg).astype(np.float32)
        kd = (np.exp(-(pos + 1.0).astype(np.float32) * lg) * np.float32(128 ** -0.5)).astype(np.float32)
        ld = (np.exp((LC - 1.0 - pos).astype(np.float32) * lg) * np.float32(128 ** -0.5)).astype(np.float32)
        c[:, C1_DQ + (0 * 2 + r) * 512:C1_DQ + (0 * 2 + r) * 512 + 512] = qd[None, :]
        c[:, C1_DQ + (1 * 2 + r) * 512:C1_DQ + (1 * 2 + r) * 512 + 512] = kd[None, :]
        c[:, C1_DQ + (2 * 2 + r) * 512:C1_DQ + (2 * 2 + r) * 512 + 512] = ld[None, :]
        c[:, C1_EXL + r * 8:C1_EXL + r * 8 + 8] = np.exp(np.float32(LC) * lg)
    return c


NEGM = -30000.0
HD = 64
NR = 8
WIN_IN = 512 + 128 + 128 + 128 + 24
C2_BC = 0
C2_BT = 4
C2_AW = 68
C2_DW = 68 + 255
C2_SL = 68 + 510
C2_NT = 68 + 510 + 8
C2_N = 68 + 510 + 8 + 16
B2_ID = 0
B2_TU = 128
B2_TL = 256
B2_CM = 384
OVS = 130
B2_OV = 384 + 2560
B2_SEL = B2_OV + 4 * OVS
B2_N = B2_SEL + 24 * 64


def emit_mix1(P, nc, ntok, dr, slopes, stop=99):
    ntile = ntok // 512
    NKCH = ntok // 128
    NCB = ntok // 16 - 1
    NCC = ntok // 2048
    w_in = P.sbuf([128, NKC, WIN_IN], BF16, "w_in")
    cst = P.sbuf([128, C2_N], F32, "cst")
    cb = P.sbuf([128, B2_N], BF16, "cb")
    w1kv = P.sbuf([128, 32, 64], BF16, "w1kv")
    poskv = P.sbuf([128, 32], BF16, "poskv")
    w2kv = P.sbuf([64, 128], BF16, "w2kv")
    kz = P.sbuf([128, ntok], BF16, "kz")
    ks_x = P.sbuf([67, ntok], BF16, "ks_x")
    kw_x = P.sbuf([67, ntok], BF16, "kw_x")
    vs_tok = P.sbuf([128, NKCH, 128], BF16, "vs_tok")
    vw_tok = P.sbuf([128, NKCH, 128], BF16, "vw_tok")
    kc_x = P.sbuf([67, NCC * 128], BF16, "kc_x")
    vc_tok = P.sbuf([128, NCC, 128], BF16, "vc_tok")
    hid = P.sbuf([64, 2, NCC * 128], BF16, "hid")
    cbias = P.sbuf([64, 2], F32, "cbias")
    hn = [P.sbuf([128, NKC, 512], BF16, "hn") for _ in range(2)]
    q_x = [P.sbuf([67, 512], BF16, "q_x") for _ in range(NR)]
    gsig = P.sbuf([24, 512], F32, "gsig")
    g_hl = P.sbuf([56, 512], BF16, "g_hl")
    bias_c = P.sbuf([128, NR, NCC], F32, "bias_c")
    bias_t = P.sbuf([128, NR, NKCH], F32, "bias_t")
    Ec = [P.sbuf([128, NCC, 512], BF16, "Ec") for _ in range(2)]
    Eb = [P.sbuf([128, 512], BF16, "Eb") for _ in range(4)]
    oacc = P.sbuf([64, NR, 512], F32, "oacc")
    rcp = P.sbuf([64, 512], F32, "rcp")
    wgt = P.sbuf([64, 512], F32, "wgt")
    tmpo = P.sbuf([64, 512], F32, "tmpo")
    obf = [P.sbuf([64, NR, 512], BF16, "obf") for _ in range(2)]
    pacc = P.sbuf([128, 4, 128], F32, "pacc")
    rc1 = P.sbuf([128, 1], F32, "rc1")
    sc = P.sbuf([128, 128], F32, "sc")
    sc2 = P.sbuf([128, 128], F32, "sc2")
    m8 = P.sbuf([128, 16], F32, "m8")
    nm = P.sbuf([128, 128], BF16, "nm")
    negmT = P.sbuf([128, 512], BF16, "negmT")

    NPS = 2
    ps_s = [P.psum([128, 512], F32, "ps_s") for _ in range(NPS)]
    ps_o = [P.psum([128, 512], F32, "ps_o") for _ in range(2)]
    ps_u = [P.psum([128, 512], F32, "ps_u") for _ in range(2)]
    ps_p = [P.psum([128, 512], F32, "ps_p") for _ in range(2)]

    P.dma("sp", cst[:], dr["cst"], [], ["cst"])
    for j0 in range(0, B2_N, 1024):
        j1 = min(B2_N, j0 + 1024)
        P.dma("pool", cb[:, j0:j1], dr["cb"][:, j0:j1], [], ["cb"])
    wi_v = dr["w_in"].rearrange("(c p) n -> p c n", p=128)
    for c in range(NKC):
        P.dma("pool", w_in[:, c, :], wi_v[:, c, :], [], [("w_in", c)])
    P.dma("pool", w1kv[:].rearrange("p a b -> p (a b)"), dr["w1kv"], [], ["w1kv"])
    P.dma("pool", poskv[:], dr["poskv"], [], ["poskv"])
    P.dma("pool", w2kv[:], dr["w2kv"], [], ["w2kv"])
    for r in range(NR):
        P.dma("pool", q_x[r][64:67, :], dr["qbias"][:, r, :], [], [("q_b", r)])
    ident = cb[:, B2_ID:B2_ID + 128]
    tri_u = cb[:, B2_TU:B2_TU + 128]
    tri_l = cb[:, B2_TL:B2_TL + 128]
    P.memset("dve", ks_x[64:67, :], 1.0, ["ks_b"])
    P.memset("dve", kw_x[64:67, :], 1.0, ["kw_b"])
    P.memset("dve", kc_x[:], 0.0, ["kc_x"])
    P.memset("dve", kc_x[64:67, :], 1.0, ["kc_x"])
    P.memset("pool", vs_tok[:, :, 64:128], 1.0, ["vs_ones"])
    P.memset("pool", vw_tok[:, :, 64:128], 1.0, ["vw_ones"])
    P.memset("pool", vc_tok[:], 0.0, ["vc_tok"])
    P.memset("pool", vc_tok[:, :, 64:128], 1.0, ["vc_tok"])
    P.memset("pool", g_hl[:], 0.0, ["g_hl"])
    P.memset("pool", hid[:], 0.0, ["hid"])

    if stop <= 0:
        return
    if "hn_src" in dr:
        hn_src = dr["hn_src"]
    else:
        hn_v = dr["hnT"].rearrange("(c p) t -> p c t", p=128)
        hn_src = lambda i, c: hn_v[:, c, i * 512:(i + 1) * 512]

    def load(slot, i):
        for c in range(NKC):
            P.dma("sp", hn[slot][:, c, :], hn_src(i, c), [], [("hn", slot, c)])

    ppi = [0]

    def nextp():
        p = ppi[0] % 2
        ppi[0] += 1
        return p

    nload = [0]
    load(0, 0)
    for i in range(ntile):
        b = nload[0] % 2
        nload[0] += 1
        if i + 1 < ntile:
            load(nload[0] % 2, i + 1)
        else:
            load(nload[0] % 2, 0)
        sl = slice(i * 512, (i + 1) * 512)
        hk = [("hn", b, c) for c in range(NKC)]
        p = nextp()
        for k in range(NKC):
            P.mm(ps_p[p][:], w_in[:, k, 512:640], hn[b][:, k, :], k == 0, k == NKC - 1, [("w_in", k), ("hn", b, k)], [("ps_p", p)])
        P.copy("act", kz[:, sl], ps_p[p][:], [("ps_p", p)], [("kz", i)])
        if stop <= 0.3:
            continue
        p = nextp()
        for k in range(NKC):
            P.mm(ps_p[p][:], w_in[:, k, 640:768], hn[b][:, k, :], k == 0, k == NKC - 1, [("w_in", k), ("hn", b, k)], [("ps_p", p)])
        P.copy("dve", ks_x[0:64, sl], ps_p[p][0:64, :], [("ps_p", p)], [("ks_x", i)])
        P.copy("dve", kw_x[0:64, sl], ps_p[p][64:128, :], [("ps_p", p)], [("kw_x", i)])
        if stop <= 0.6:
            continue
        p = nextp()
        for s in range(4):
            for k in range(NKC):
                P.mm(ps_p[p][:, s * 128:(s + 1) * 128], hn[b][:, k, s * 128:(s + 1) * 128], w_in[:, k, 768:896], k == 0, k == NKC - 1,
                     [("w_in", k), ("hn", b, k)], [("ps_p", p)])
        pv = ps_p[p][:].rearrange("p (s c) -> p s c", c=128)
        P.copy("dve", vs_tok[:, 4 * i:4 * i + 4, 0:64], pv[:, :, 0:64], [("ps_p", p)], [("vs_tok", i)])
        P.copy("dve", vw_tok[:, 4 * i:4 * i + 4, 0:64], pv[:, :, 64:128], [("ps_p", p)], [("vw_tok", i)])

    if stop <= 1:
        return
    kzall = [("kz", i) for i in range(ntile)]
    for kv in range(2):
        base = 64 * kv
        p = nextp()
        for pp in range(32):
            P.mm(ps_p[p][0:64, 0:1], w1kv[base:base + 64, pp, :], poskv[base:base + 64, pp:pp + 1], pp == 0, pp == 31,
                 ["w1kv", "poskv"], [("ps_p", p)])
        P.copy("dve", cbias[:, kv:kv + 1], ps_p[p][0:64, 0:1], [("ps_p", p)], [("cbias", kv)])
        for c0 in range(0, NCB, 512):
            cn = min(512, NCB - c0)
            p = nextp()
            for pp in range(32):
                rhs = kz[base:base + 64, pp + 16 * c0: pp + 16 * c0 + 16 * (cn - 1) + 1: 16]
                P.mm(ps_p[p][0:64, 0:cn], w1kv[base:base + 64, pp, :], rhs, pp == 0, pp == 31, ["w1kv"] + kzall, [("ps_p", p)])
            P.act(hid[:, kv, c0:c0 + cn], ps_p[p][0:64, 0:cn], AF.Silu, [("ps_p", p), ("cbias", kv)], ["hid"],
                  bias=cbias[:, kv:kv + 1])
    for c0 in range(0, NCB, 512):
        cn = min(512, NCB - c0)
        p = nextp()
        P.mm(ps_p[p][0:64, 0:cn], w2kv[:, 0:64], hid[:, 0, c0:c0 + cn], True, True, ["w2kv", "hid"], [("ps_p", p)])
        P.copy("dve", kc_x[0:64, c0:c0 + cn], ps_p[p][0:64, 0:cn], [("ps_p", p)], ["kc_x"])
    for m in range(NCC):
        p = nextp()
        P.mm(ps_p[p][:, 0:64], hid[:, 1, m * 128:(m + 1) * 128], w2kv[:, 64:128], True, True, ["w2kv", "hid"], [("ps_p", p)])
        rows = 128 if (m + 1) * 128 <= NCB else NCB - m * 128
        P.copy("dve", vc_tok[0:rows, m, 0:64], ps_p[p][0:rows, 0:64], [("ps_p", p)], ["vc_tok"])
    for j0 in range(0, ntok, 2048):
        P.dma("pool", kz[:, j0:j0 + 2048], dr["zexp"][:, j0:j0 + 2048], [], kzall)

    if stop <= 2:
        return
    sp_c = P.sbuf([128, NR, NCC], F32, "sp_c")
    sp_t = P.sbuf([128, NR, NKCH], F32, "sp_t")
    nt0 = P.sbuf([128, NR, 16], F32, "nt0")
    for r in range(NR):
        slp = cst[:, C2_SL + r:C2_SL + r + 1]
        P.ts("dve", sp_c[:, r, :], cst[:, C2_BC:C2_BC + NCC], slp, None, ALU.mult, None, ["cst"], ["sp_c"])
        P.ts("dve", sp_t[:, r, :], cst[:, C2_BT:C2_BT + NKCH], slp, None, ALU.mult, None, ["cst"], ["sp_t"])
        P.ts("dve", nt0[:, r, :], cst[:, C2_NT:C2_NT + 16], slp, None, ALU.mult, None, ["cst"], ["nt0"])
    if "o_dst" in dr:
        o_dst1 = dr["o_dst"]
    else:
        o_v = dr["oT_out"].rearrange("(r p) t -> p r t", p=64)
        o_dst1 = lambda n_: o_v[:, :, n_ * 512:(n_ + 1) * 512]
    psi = [0]
    poi = [0]
    ebi = [0]

    def gate_w(r, br, po):
        P.ts("dve", rcp[:], ps_o[po][64:128, :], 1e-30, None, ALU.add, None, [("ps_o", po)], ["rcp"])
        P.recip(rcp[:], rcp[:], ["rcp"], ["rcp"])
        p = nextp()
        P.mm(ps_p[p][0:64, :], cb[0:56, B2_SEL + (3 * r + br) * 64:B2_SEL + (3 * r + br + 1) * 64], g_hl[:], True, True,
             ["cb", "g_hl"], [("ps_p", p)])
        P.tt("dve", wgt[:], rcp[:], ps_p[p][0:64, :], ALU.mult, ["rcp", ("ps_p", p)], ["wgt"])

    import os
    dbg_lo, dbg_hi = [int(v) for v in os.environ.get('DBG_TILES', '0,99').split(',')]
    for n in range(ntile):
        b = nload[0] % 2
        nload[0] += 1
        if n + 1 < ntile:
            load(nload[0] % 2, n + 1)
        if n < dbg_lo or n >= dbg_hi:
            continue
        t0 = 512 * n
        sl = slice(t0, t0 + 512)
        for r in range(NR):
            p = nextp()
            for k in range(NKC):
                P.mm(ps_p[p][0:64, :], w_in[:, k, r * 64:(r + 1) * 64], hn[b][:, k, :], k == 0, k == NKC - 1,
                     [("w_in", k), ("hn", b, k)], [("ps_p", p)])
            P.op("act", lambda h, o=q_x[r][0:64, :], i_=ps_p[p][0:64, :]: h.mul(o, i_, HD ** -0.5), [("ps_p", p)], [("q_x", r)],
                 banks=P._banks(ps_p[p][0:64, :]))
        if stop <= 2.3:
            continue
        p = nextp()
        for k in range(NKC):
            P.mm(ps_p[p][0:24, :], w_in[:, k, 896:920], hn[b][:, k, :], k == 0, k == NKC - 1,
                 [("w_in", k), ("hn", b, k)], [("ps_p", p)])
        P.act(gsig[:], ps_p[p][0:24, :], AF.Sigmoid, [("ps_p", p)], ["gsig"])
        P.copy("dve", g_hl[0:24, :], gsig[:], ["gsig"], ["g_hl"])
        P.tt("dve", g_hl[32:56, :], gsig[:], g_hl[0:24, :], ALU.subtract, ["gsig", "g_hl"], ["g_hl"])
        if stop <= 2.6:
            continue
        P.tt("dve", bias_c[:], sp_c[:], nt0[:, :, n:n + 1].to_broadcast([128, NR, NCC]), ALU.add, ["sp_c", "nt0"],
             [("bias_c", r) for r in range(NR)])
        P.tt("dve", bias_t[:], sp_t[:], nt0[:, :, n:n + 1].to_broadcast([128, NR, NKCH]), ALU.add, ["sp_t", "nt0"],
             [("bias_t", r) for r in range(NR)])
        if stop <= 3:
            continue
        mlist = [m for m in range(NCC) if 2048 * m + 31 <= t0 + 511]
        for r in range(NR):
            e = Ec[r % 2]
            po = poi[0] % 2
            poi[0] += 1
            for mi, m in enumerate(mlist):
                ps = psi[0] % NPS
                psi[0] += 1
                o = n - 4 * m
                mixed = 0 <= o <= 4
                P.mm(ps_s[ps][:], kc_x[:, m * 128:(m + 1) * 128], q_x[r][:], True, not mixed,
                     ["kc_x", ("q_x", r), ("q_b", r)], [("ps_s", ps)])
                if mixed:
                    P.mm(ps_s[ps][:], ident, cb[:, B2_CM + o * 512:B2_CM + (o + 1) * 512], False, True, ["cb"], [("ps_s", ps)])
                P.act(e[:, m, :], ps_s[ps][:], AF.Exp, [("ps_s", ps), ("bias_c", r)], [("Ec", r % 2, m)],
                      bias=bias_c[:, r, m:m + 1])
                P.mm(ps_o[po][:], vc_tok[:, m, :], e[:, m, :], mi == 0, mi == len(mlist) - 1,
                     ["vc_tok", ("Ec", r % 2, m)], [("ps_o", po)])
            if stop <= 3.3:
                continue
            gate_w(r, 0, po)
            P.tt("dve", oacc[:, r, :], ps_o[po][0:64, :], wgt[:], ALU.mult, [("ps_o", po), "wgt"], [("oacc", r)])
            if stop <= 3.6:
                continue
            dbg_u = int(os.environ.get('DBG_U', '0'))
            for qs in range(4):
                ub = ps_u[qs // 2][:, (qs % 2) * OVS:(qs % 2) * OVS + OVS]
                for mi, m in enumerate(mlist):
                    if dbg_u == 2:
                        continue
                    P.mm(ub, e[:, m, qs * 128:(qs + 1) * 128], cb[:, B2_OV + m * OVS:B2_OV + m * OVS + OVS],
                         mi == 0, mi == len(mlist) - 1, [("Ec", r % 2, m), "cb"], [("ps_u", qs)])
                if dbg_u == 1:
                    continue
                P.ts("dve", rc1[:], ub[:, 128:129], 1e-30, None, ALU.add, None, [("ps_u", qs)], ["rc1"])
                P.recip(rc1[:], rc1[:], ["rc1"], ["rc1"])
                if r == 0:
                    P.ts("dve", pacc[:, qs, :], ub[:, 0:128], rc1[:, 0:1], None, ALU.mult, None,
                         [("ps_u", qs), "rc1"], [("pacc", qs)])
                else:
                    P.stt(pacc[:, qs, :], ub[:, 0:128], rc1[:, 0:1], pacc[:, qs, :], ALU.mult, ALU.add,
                          [("ps_u", qs), "rc1", ("pacc", qs)], [("pacc", qs)])
        if stop <= 4:
            continue
        for qs in range(4):
            off = 8 * n + 2 * qs
            aw = cst[:, C2_AW + 127 - off:C2_AW + 255 - off]
            dw = cst[:, C2_DW + 127 - off:C2_DW + 255 - off]
            P.tt("dve", sc[:], pacc[:, qs, :], aw, ALU.mult, [("pacc", qs), "cst"], ["sc"])
            P.tt("dve", sc[:], sc[:], dw, ALU.add, ["sc", "cst"], ["sc"])
            P.memset("dve", sc[:, 0:1], 1e9, ["sc"])
            P.op("dve", lambda h, o=m8[:, 0:8], i_=sc[:]: h.max(o, i_), ["sc"], ["m8"])
            P.op("dve", lambda h, o=sc2[:], a=m8[:, 0:8], v=sc[:]: h.match_replace(o, a, v, -3.0e38), ["sc", "m8"], ["sc2"])
            P.op("dve", lambda h, o=m8[:, 8:16], i_=sc2[:]: h.max(o, i_), ["sc2"], ["m8"])
            P.ts("dve", sc2[:], sc[:], m8[:, 15:16], None, ALU.is_ge, None, ["sc", "m8"], ["sc2"])
            P.ts("dve", nm[:], sc2[:], -1.0, -NEGM, ALU.add, ALU.mult, ["sc2"], ["nm"])
            P.transpose(ps_u[qs // 2][:, 264:392].bitcast(BF16)[:, (qs % 2) * 128:(qs % 2) * 128 + 128], nm[:], ident,
                        ["nm", "cb"], [("ps_nm", qs)])
        for hb_ in range(2):
            P.copy("act", negmT[:, hb_ * 256:(hb_ + 1) * 256], ps_u[hb_][:, 264:392].bitcast(BF16),
                   [("ps_nm", 2 * hb_), ("ps_nm", 2 * hb_ + 1)], [("negmT", hb_)])
        if stop <= 5:
            continue
        for br in (2, 1):
            for r in range(NR):
                po = poi[0] % 2
                poi[0] += 1
                if br == 2:
                    klist = [kc for kc in ([4 * n + a for a in range(4)] + [4 * n - 4 + a for a in range(4)]) if kc >= 0]
                    kx, vt, kkey, vkey, vones = kw_x, vw_tok, "kw_x", "vw_tok", "vw_ones"
                else:
                    klist = list(range(4 * n + 4))
                    kx, vt, kkey, vkey, vones = ks_x, vs_tok, "ks_x", "vs_tok", "vs_ones"
                for ki, kc in enumerate(klist):
                    ps = psi[0] % NPS
                    psi[0] += 1
                    eb = ebi[0] % 4
                    ebi[0] += 1
                    if kc >= 4 * n:
                        a = kc - 4 * n
                        c_lo, c_hi = 128 * a, 512
                        dmask = (tri_u, c_lo)
                    else:
                        a = kc - (4 * n - 4)
                        if br == 2:
                            c_lo, c_hi = 0, 128 * a + 128
                            dmask = (tri_l, 128 * a)
                        else:
                            c_lo, c_hi = 0, 512
                            dmask = None
                    cs = slice(c_lo, c_hi)
                    last = (dmask is None) and (br == 2)
                    P.mm(ps_s[ps][:, cs], kx[:, kc * 128:(kc + 1) * 128], q_x[r][:, cs], True, last,
                         [(kkey, kc // 4), kkey[:2] + "_b", ("q_x", r), ("q_b", r)], [("ps_s", ps)])
                    if br == 1:
                        P.mm(ps_s[ps][:, cs], kz[:, kc * 128:(kc + 1) * 128], negmT[:, cs], False, dmask is None,
                             kzall + [("negmT", 0), ("negmT", 1)], [("ps_s", ps)])
                    if dmask is not None:
                        P.mm(ps_s[ps][:, dmask[1]:dmask[1] + 128], ident, dmask[0], False, True, ["cb"], [("ps_s", ps)])
                    P.act(Eb[eb][:, cs], ps_s[ps][:, cs], AF.Exp, [("ps_s", ps), ("bias_t", r)], [("Eb", eb)],
                          bias=bias_t[:, r, kc:kc + 1])
                    P.mm(ps_o[po][:, cs], vt[:, kc, :], Eb[eb][:, cs], ki == 0, ki == len(klist) - 1,
                         [(vkey, kc // 4), vones, ("Eb", eb)], [("ps_o", po)])
                gate_w(r, br, po)
                P.tt("dve", tmpo[:], ps_o[po][0:64, :], wgt[:], ALU.mult, [("ps_o", po), "wgt"], ["tmpo"])
                P.tt("dve", oacc[:, r, :], oacc[:, r, :], tmpo[:], ALU.add, [("oacc", r), "tmpo"], [("oacc", r)])
        ob = n % 2
        for r in range(NR):
            P.copy("act", obf[ob][:, r, :], oacc[:, r, :], [("oacc", r)], [("obf", ob, r)])
        P.dma("sp", o_dst1(n), obf[ob][:], [("obf", ob, r) for r in range(NR)], [("oT_out", n)])
        if "after_tile" in dr:
            dr["after_tile"](n)


def build_mix1(ntok, slopes, stop=99):
    nc = bass.Bass("TRN2", target_bir_lowering=False)
    dr = {}
    dr["hnT"] = nc.dram_tensor("hnT", [D, ntok], BF16, kind="ExternalInput").ap()
    dr["w_in"] = nc.dram_tensor("w_in", [D, WIN_IN], F32, kind="ExternalInput").ap()
    dr["cst"] = nc.dram_tensor("cst", [128, C2_N], F32, kind="ExternalInput").ap()
    dr["cb"] = nc.dram_tensor("cb", [128, B2_N], F32, kind="ExternalInput").ap()
    dr["w1kv"] = nc.dram_tensor("w1kv", [128, 2048], F32, kind="ExternalInput").ap()
    dr["poskv"] = nc.dram_tensor("poskv", [128, 32], F32, kind="ExternalInput").ap()
    dr["w2kv"] = nc.dram_tensor("w2kv", [64, 128], F32, kind="ExternalInput").ap()
    dr["qbias"] = nc.dram_tensor("qbias", [3, NR, 512], F32, kind="ExternalInput").ap()
    dr["zexp"] = nc.dram_tensor("zexp", [128, ntok], F32, kind="ExternalInput").ap()
    dr["oT_out"] = nc.dram_tensor("oT_out", [512, ntok], BF16, kind="ExternalOutput").ap()
    P = Prog(nc)
    emit_mix1(P, nc, ntok, dr, slopes, stop)
    P.emit()
    return nc


def _bf(x):
    return np.asarray(x, np.float32).astype(ml_dtypes.bfloat16).astype(np.float32)


def mix1_slopes(g):
    h = np.arange(8 * g, 8 * g + 8, dtype=np.float32)
    return np.exp2(np.float32(-8.0) * (h + np.float32(1.0)) / np.float32(16.0)).astype(np.float32)


def mix1_consts(g, ntok):
    slopes = mix1_slopes(g)
    cst = np.zeros((128, C2_N), np.float32)
    ki = np.arange(128)
    ncc = ntok // 2048
    for m in range(ncc):
        cst[:, C2_BC + m] = 16.0 * (128 * m + ki) + 31.0
    for kc in range(ntok // 128):
        cst[:, C2_BT + kc] = 128.0 * kc + ki
    hb = (ki // 64)[:, None]
    jj = (np.arange(255) - 127)[None, :]
    allowed = jj <= hb
    forced = (jj == hb) | (jj == hb - 1)
    cst[:, C2_AW:C2_AW + 255] = allowed.astype(np.float32)
    cst[:, C2_DW:C2_DW + 255] = np.where(forced, np.float32(1e9), np.where(allowed, np.float32(0.0), np.float32(-1e30)))
    cst[:, C2_SL:C2_SL + 8] = slopes[None, :]
    cst[:, C2_NT:C2_NT + 16] = -512.0 * np.arange(16)[None, :]
    cb = np.zeros((128, B2_N), np.float32)
    cb[:, B2_ID:B2_ID + 128] = np.eye(128)
    k = ki[:, None]
    q = np.arange(128)[None, :]
    cb[:, B2_TU:B2_TU + 128] = np.where(q >= k, 0.0, NEGM)
    cb[:, B2_TL:B2_TL + 128] = np.where(q < k, 0.0, NEGM)
    qi = np.arange(512)[None, :]
    for o in range(5):
        cb[:, B2_CM + o * 512:B2_CM + (o + 1) * 512] = np.where(512 * o + qi >= 16 * k + 31, 0.0, NEGM)
    ncb = ntok // 16 - 1
    ns = ntok // 64
    for m in range(ncc):
        c = 128 * m + ki
        c0 = c * 16
        for j in range(128):
            if j >= ns:
                continue
            lo = np.maximum(c0, j * 64)
            hi = np.minimum(c0 + 32, j * 64 + 64)
            cb[:, B2_OV + m * OVS + j] = np.where(c < ncb, np.maximum(hi - lo, 0) / 32.0, 0.0)
        cb[:, B2_OV + m * OVS + 128] = (c < ncb).astype(np.float32)
    for idx in range(24):
        cb[idx, B2_SEL + idx * 64:B2_SEL + (idx + 1) * 64] = 1.0
        cb[32 + idx, B2_SEL + idx * 64:B2_SEL + (idx + 1) * 64] = 1.0
    cb = _bf(cb)
    i = np.arange(512, dtype=np.float64)
    qb = np.zeros((3, NR, 512), np.float32)
    for r in range(NR):
        a = -np.float64(slopes[r]) * i
        a1 = _bf(a)
        a2 = _bf(a - a1)
        a3 = _bf(a - a1 - a2)
        qb[0, r], qb[1, r], qb[2, r] = a1, a2, a3
    z = (np.arange(ntok)[None, :] // 64 == np.arange(128)[:, None]).astype(np.float32)
    return {"cst": cst, "cb": cb, "qbias": qb, "zexp": z}, slopes


def mix1_weights(w_in, cpk, cpv, w1k, w2k, w1v, w2v, g):
    cols = [w_in[:, 512 * g:512 * g + 512]]
    for base in (1024, 1152, 1280, 1536, 1408, 1664):
        cols.append(w_in[:, base + 64 * g: base + 64 * g + 64])
    cols.append(w_in[:, 1792 + 24 * g:1792 + 24 * g + 24])
    wc = np.ascontiguousarray(np.concatenate(cols, axis=1))
    w1 = np.concatenate([w1k.reshape(32, 64, 64).transpose(1, 0, 2).reshape(64, 2048),
                         w1v.reshape(32, 64, 64).transpose(1, 0, 2).reshape(64, 2048)], axis=0)
    pos = np.concatenate([cpk.T, cpv.T], axis=0)
    w2 = np.concatenate([w2k, w2v], axis=1)
    return {"w_in": wc, "w1kv": np.ascontiguousarray(w1), "poskv": np.ascontiguousarray(pos), "w2kv": np.ascontiguousarray(w2)}


_PROGS = {}


def _prog(key, fn):
    if key not in _PROGS:
        _PROGS[key] = fn()
    return _PROGS[key]


def _mix0_inputs(xT, w_in, lbs, hno, rno, g_mix, hh):
    hsel = [2 * hh, 2 * hh + 1]
    col = lambda grp, h: w_in[:, grp * 512 + h * 128: grp * 512 + (h + 1) * 128]
    Q = [col(0, h) for h in hsel] + [col(4, h) for h in hsel]
    K = [col(1, h) for h in hsel] + [col(5, h) for h in hsel]
    G = [col(3, h) for h in hsel] + [col(7, h) for h in hsel]
    V = [col(2, h) for h in hsel] + [col(6, h) for h in hsel]
    wc = np.ascontiguousarray(np.concatenate(Q + K + G + V, axis=1))
    lbp = np.stack([lbs[0, hsel[0] * 128:(hsel[0] + 1) * 128], lbs[0, hsel[1] * 128:(hsel[1] + 1) * 128],
                    lbs[1, hsel[0] * 128:(hsel[0] + 1) * 128], lbs[1, hsel[1] * 128:(hsel[1] + 1) * 128]], axis=1)
    return {"xT": xT, "w_in": wc, "g_mix": gvec(g_mix), "cst": mix0_consts(hh), "ident": np.eye(128, dtype=np.float32),
            "lbp": np.ascontiguousarray(lbp.astype(np.float32)),
            "wn": np.ascontiguousarray(np.stack([hno, rno], axis=1).astype(np.float32))}


def kernel_unfused(x, mix_norm, ffn_norm, final_norm, even_w_in, hgrn_lower_bounds, hgrn_out_norm,
           ret_out_norm, even_w_out, odd_w_in, cmp_pos_k, cmp_pos_v, cmp_w1_k, cmp_w2_k,
           cmp_w1_v, cmp_w2_v, odd_w_out, ffn_w_gate_up, ffn_w_down):
    f = lambda a: np.asarray(a, np.float32)
    x = f(x)
    cores = list(range(8))
    xT = [np.ascontiguousarray(x[b].T) for b in range(B)]
    TH = T // 2
    ncA = _prog("mix0", lambda: build_mix0(T))
    inA = [_mix0_inputs(xT[c // 2], f(even_w_in)[0], f(hgrn_lower_bounds), f(hgrn_out_norm)[0], f(ret_out_norm)[0],
                        f(mix_norm)[0], c % 2) for c in cores]
    rA = run_bass_kernel_spmd(ncA, inA, core_ids=cores).results
    o0T = []
    for b in range(B):
        full = np.empty((D, T), ml_dtypes.bfloat16)
        for hh in range(2):
            o = rA[2 * b + hh]["oT_out"]
            full[256 * hh:256 * hh + 256] = o[0:256]
            full[512 + 256 * hh:512 + 256 * hh + 256] = o[256:512]
        o0T.append(full)
    ncB = _prog("ffn_mid", lambda: build_ffn(TH, "mid"))
    inB = []
    for c in cores:
        b, s = c // 2, c % 2
        inB.append({"oT": np.ascontiguousarray(o0T[b][:, s * TH:(s + 1) * TH]),
                    "xT": np.ascontiguousarray(xT[b][:, s * TH:(s + 1) * TH]),
                    "w_out": f(even_w_out)[0], "g_ffn": gvec(f(ffn_norm)[0]), "g_nxt": gvec(f(mix_norm)[1]),
                    "w_gu": f(ffn_w_gate_up)[0], "w_dn": f(ffn_w_down)[0]})
    rB = run_bass_kernel_spmd(ncB, inB, core_ids=cores).results
    inC = []
    slopes = None
    for c in cores:
        b, g = c // 2, c % 2
        consts, sl = mix1_consts(g, T)
        d = dict(consts)
        d.update(mix1_weights(f(odd_w_in)[0], f(cmp_pos_k)[0], f(cmp_pos_v)[0], f(cmp_w1_k)[0], f(cmp_w2_k)[0],
                              f(cmp_w1_v)[0], f(cmp_w2_v)[0], g))
        d["hnT"] = np.ascontiguousarray(np.concatenate([rB[2 * b]["out_n"], rB[2 * b + 1]["out_n"]], axis=1))
        inC.append(d)
    ncC = _prog("mix1", lambda: build_mix1(T, None))
    rC = run_bass_kernel_spmd(ncC, inC, core_ids=cores).results
    ncD = _prog("ffn_fin", lambda: build_ffn(TH, "fin"))
    inD = []
    for c in cores:
        b, s = c // 2, c % 2
        o1 = np.concatenate([rC[2 * b]["oT_out"][:, s * TH:(s + 1) * TH], rC[2 * b + 1]["oT_out"][:, s * TH:(s + 1) * TH]], axis=0)
        inD.append({"oT": np.ascontiguousarray(o1), "xT": rB[c]["out_h"],
                    "w_out": f(odd_w_out)[0], "g_ffn": gvec(f(ffn_norm)[1]), "g_nxt": gvec(f(final_norm)),
                    "w_gu": f(ffn_w_gate_up)[1], "w_dn": f(ffn_w_down)[1]})
    rD = run_bass_kernel_spmd(ncD, inD, core_ids=cores).results
    out = np.empty((B, T, D), np.float32)
    for c in cores:
        b, s = c // 2, c % 2
        out[b, s * TH:(s + 1) * TH, :] = rD[c]["out_n"].T
    return out


PAIRS = [[0, 1], [2, 3], [4, 5], [6, 7]]
TH = T // 2


def build_fused(nph=4):
    nc = bass.Bass("TRN2", target_bir_lowering=False)
    ext = lambda name, shape, dt=F32: nc.dram_tensor(name, list(shape), dt, kind="ExternalInput").ap()
    P = Prog(nc)
    NCH = 8
    o0c = [nc.dram_tensor("o0c%d" % k, [512, 1024], BF16) for k in range(NCH)]
    g0c = [nc.dram_tensor("g0c%d" % k, [1024, 1024], BF16) for k in range(NCH)]
    d0 = {"xT": ext("xT", [D, T]), "w_in": ext("w_in0", [D, 2048]), "g_mix": ext("g_mix0", [128, NKC]),
          "cst": ext("cst0", [128, C1_N]), "ident": ext("ident", [128, 128]), "lbp": ext("lbp", [128, 4]),
          "wn": ext("wn", [128, 2])}
    o0v = [t.ap().rearrange("(u p) t -> p u t", p=128) for t in o0c]
    d0["o_dst"] = lambda i, u: o0v[i // 2][:, u, (i % 2) * 512:(i % 2) * 512 + 512]

    def after0(i):
        if i % 2 == 1:
            k = i // 2
            P.allgather(g0c[k].ap().opt(), o0c[k].ap().opt(), PAIRS,
                        [("oT_out", ii, u) for ii in (2 * k, 2 * k + 1) for u in range(4)], [("g0c", k)])
    d0["after_tile"] = after0
    emit_mix0(P, nc, T, d0)
    if nph == 1:
        dbg = nc.dram_tensor("dbg", [1024, 1024], BF16, kind="ExternalOutput").ap()
        P.dma("sp", dbg, g0c[7].ap(), [("g0c", 7)], ["dbg"])
        P.end_phase(last=True)
        return nc
    P.end_phase()
    h1 = nc.dram_tensor("h1", [D, TH], F32)
    hn1c = [nc.dram_tensor("hn1c%d" % k, [D, 512], BF16) for k in range(NCH)]
    ghn = [nc.dram_tensor("ghn%d" % k, [2 * D, 512], BF16) for k in range(NCH)]
    d1 = {"xT": ext("xh", [D, TH]), "w_out": ext("w_out0", [D, D]), "g_ffn": ext("g_ffn0", [128, NKC]),
          "g_nxt": ext("g_nxt0", [128, NKC]), "w_gu": ext("w_gu0", [D, 2 * FH]), "w_dn": ext("w_dn0", [FH, D]),
          "sel": ext("sel", [128, 2])}
    g0v = [t.ap().rearrange("(c p) t -> p c t", p=128) for t in g0c]

    def osrc(half):
        def f(i):
            tg = half * TH + i * TT_F
            return g0v[tg // 1024][:, :, tg % 1024:tg % 1024 + TT_F]
        return f
    d1["o_srcs"] = [osrc(0), osrc(1)]
    d1["wo_perm"] = [(2 * (c // 4) + (c % 4)) if (c % 4) < 2 else (4 + 2 * (c // 4) + (c % 4) - 2) for c in range(8)]
    h1v = h1.ap().rearrange("(c p) t -> p c t", p=128)
    d1["h_dst"] = lambda i, m: h1v[:, m, i * TT_F:(i + 1) * TT_F]
    hn1v = [t.ap().rearrange("(c p) t -> p c t", p=128) for t in hn1c]
    d1["n_dst"] = lambda i, c: hn1v[i // 2][:, c, (i % 2) * TT_F:(i % 2) * TT_F + TT_F]

    def after1(i):
        if i % 2 == 1:
            k = i // 2
            P.allgather(ghn[k].ap().opt(), hn1c[k].ap().opt(), PAIRS,
                        [("out_n", ii, c) for ii in (2 * k, 2 * k + 1) for c in range(NKC)], [("ghn", k)])
    d1["after_tile"] = after1
    emit_ffn(P, nc, TH, "mid", d1)
    if nph == 2:
        dbg = nc.dram_tensor("dbg", [2 * D, 512], BF16, kind="ExternalOutput").ap()
        P.dma("sp", dbg, ghn[7].ap(), [("ghn", 7)], ["dbg"])
        P.end_phase(last=True)
        return nc
    P.end_phase()
    o1c = [nc.dram_tensor("o1c%d" % k, [512, 1024], BF16) for k in range(NCH)]
    g1c = [nc.dram_tensor("g1c%d" % k, [1024, 1024], BF16) for k in range(NCH)]
    d2 = {"w_in": ext("w_in1", [D, WIN_IN]), "cst": ext("cst1", [128, C2_N]), "cb": ext("cb1", [128, B2_N]),
          "w1kv": ext("w1kv", [128, 2048]), "poskv": ext("poskv", [128, 32]), "w2kv": ext("w2kv", [64, 128]),
          "qbias": ext("qbias", [3, NR, 512]), "zexp": ext("zexp", [128, T])}
    ghv = [t.ap().rearrange("(r c p) t -> p r c t", p=128, c=NKC) for t in ghn]
    d2["hn_src"] = lambda i, c: ghv[i % 8][:, i // 8, c, :]
    o1v = [t.ap().rearrange("(r p) t -> p r t", p=64) for t in o1c]
    d2["o_dst"] = lambda n: o1v[n // 2][:, :, (n % 2) * 512:(n % 2) * 512 + 512]

    def after2(n):
        if n % 2 == 1:
            k = n // 2
            P.allgather(g1c[k].ap().opt(), o1c[k].ap().opt(), PAIRS, [("oT_out", 2 * k), ("oT_out", 2 * k + 1)], [("g1c", k)])
    d2["after_tile"] = after2
    emit_mix1(P, nc, T, d2, None)
    if nph == 3:
        dbg = nc.dram_tensor("dbg", [1024, 1024], BF16, kind="ExternalOutput").ap()
        P.dma("sp", dbg, g1c[7].ap(), [("g1c", 7)], ["dbg"])
        P.end_phase(last=True)
        return nc
    P.end_phase()
    d3 = {"xT": h1.ap(), "w_out": ext("w_out1", [D, D]), "g_ffn": ext("g_ffn1", [128, NKC]),
          "g_nxt": ext("g_fin", [128, NKC]), "w_gu": ext("w_gu1", [D, 2 * FH]), "w_dn": ext("w_dn1", [FH, D]),
          "sel": d1["sel"]}
    g1v = [t.ap().rearrange("(c p) t -> p c t", p=128) for t in g1c]

    def osrc1(half):
        def f(i):
            tg = half * TH + i * TT_F
            return g1v[tg // 1024][:, :, tg % 1024:tg % 1024 + TT_F]
        return f
    d3["o_srcs"] = [osrc1(0), osrc1(1)]
    d3["out_n"] = nc.dram_tensor("out_n", [D, TH], F32, kind="ExternalOutput").ap()
    emit_ffn(P, nc, TH, "fin", d3)
    P.end_phase(last=True)
    return nc


def kernel(x, mix_norm, ffn_norm, final_norm, even_w_in, hgrn_lower_bounds, hgrn_out_norm,
           ret_out_norm, even_w_out, odd_w_in, cmp_pos_k, cmp_pos_v, cmp_w1_k, cmp_w2_k,
           cmp_w1_v, cmp_w2_v, odd_w_out, ffn_w_gate_up, ffn_w_down):
    f = lambda a: np.asarray(a, np.float32)
    x = f(x)
    cores = list(range(8))
    xT = [np.ascontiguousarray(x[b].T) for b in range(B)]
    nc = _prog("fused", build_fused)
    ins = _fused_inputs(x, xT, mix_norm, ffn_norm, final_norm, even_w_in, hgrn_lower_bounds, hgrn_out_norm,
                        ret_out_norm, even_w_out, odd_w_in, cmp_pos_k, cmp_pos_v, cmp_w1_k, cmp_w2_k,
                        cmp_w1_v, cmp_w2_v, odd_w_out, ffn_w_gate_up, ffn_w_down)
    res = run_bass_kernel_spmd(nc, ins, core_ids=cores).results
    out = np.empty((B, T, D), np.float32)
    for c in cores:
        b, j = c // 2, c % 2
        out[b, j * TH:(j + 1) * TH, :] = res[c]["out_n"].T
    return out


def _fused_inputs(x, xT, mix_norm, ffn_norm, final_norm, even_w_in, hgrn_lower_bounds, hgrn_out_norm,
                  ret_out_norm, even_w_out, odd_w_in, cmp_pos_k, cmp_pos_v, cmp_w1_k, cmp_w2_k,
                  cmp_w1_v, cmp_w2_v, odd_w_out, ffn_w_gate_up, ffn_w_down):
    f = lambda a: np.asarray(a, np.float32)
    cores = list(range(8))
    ins = []
    for c in cores:
        b, j = c // 2, c % 2
        m0 = _mix0_inputs(xT[b], f(even_w_in)[0], f(hgrn_lower_bounds), f(hgrn_out_norm)[0], f(ret_out_norm)[0],
                          f(mix_norm)[0], j)
        d = {"xT": m0["xT"], "w_in0": m0["w_in"], "g_mix0": m0["g_mix"], "cst0": m0["cst"], "ident": m0["ident"],
             "lbp": m0["lbp"], "wn": m0["wn"]}
        d.update({"xh": np.ascontiguousarray(xT[b][:, j * TH:(j + 1) * TH]), "w_out0": f(even_w_out)[0],
                  "g_ffn0": gvec(f(ffn_norm)[0]), "g_nxt0": gvec(f(mix_norm)[1]), "w_gu0": f(ffn_w_gate_up)[0],
                  "w_dn0": f(ffn_w_down)[0],
                  "sel": np.ascontiguousarray(np.tile(np.array([[1.0 - j, float(j)]], np.float32), (128, 1)))})
        consts, _ = mix1_consts(j, T)
        mw = mix1_weights(f(odd_w_in)[0], f(cmp_pos_k)[0], f(cmp_pos_v)[0], f(cmp_w1_k)[0], f(cmp_w2_k)[0],
                          f(cmp_w1_v)[0], f(cmp_w2_v)[0], j)
        d.update({"w_in1": mw["w_in"], "cst1": consts["cst"], "cb1": consts["cb"], "w1kv": mw["w1kv"], "poskv": mw["poskv"],
                  "w2kv": mw["w2kv"], "qbias": consts["qbias"], "zexp": consts["zexp"]})
        d.update({"w_out1": f(odd_w_out)[0], "g_ffn1": gvec(f(ffn_norm)[1]), "g_fin": gvec(f(final_norm)),
                  "w_gu1": f(ffn_w_gate_up)[1], "w_dn1": f(ffn_w_down)[1]})
        ins.append(d)
    return ins
```

```python
import numpy as np
from contextlib import ExitStack
import concourse.bass as bass
import concourse.mybir as mybir
from concourse.bass_utils import run_bass_kernel_spmd
import ml_dtypes

F32 = mybir.dt.float32
BF16 = mybir.dt.bfloat16
AF = mybir.ActivationFunctionType
ALU = mybir.AluOpType
AX = mybir.AxisListType

D = 1024
T = 8192
B = 4
FH = 2816
NKC = D // 128
NHC = FH // 128
EPS = 1e-6
TT_F = 256


class _Op:
    __slots__ = ("fn", "waits", "needs_inc", "dma_slot", "semval")

    def __init__(self, fn, waits, dma_slot):
        self.fn = fn
        self.waits = waits
        self.needs_inc = False
        self.dma_slot = dma_slot
        self.semval = 0


class _Slot:
    __slots__ = ("sem", "count", "inc")

    def __init__(self, sem, inc=16):
        self.sem = sem
        self.count = 0
        self.inc = inc


class Prog:
    ENGS = ("pe", "act", "dve", "pool", "sp")

    def __init__(self, nc, n_dma_slots=12):
        self.nc = nc
        self.es = ExitStack()
        self.ops = {e: [] for e in self.ENGS}
        self.known = {e: {} for e in self.ENGS}
        self.sems = {e: self.es.enter_context(nc.semaphore("sem_" + e)) for e in self.ENGS}
        self.slots = {}
        self.slot_rr = {}
        for q in ("sp", "pool", "act"):
            self.slots[q] = [
                _Slot(self.es.enter_context(nc.semaphore("dma_%s_%d" % (q, i))))
                for i in range(n_dma_slots)
            ]
            self.slot_rr[q] = 0
        self.cc_slot = _Slot(self.es.enter_context(nc.semaphore("cc_sem")), 1)
        self.bar = self.es.enter_context(nc.semaphore("phase_bar"))
        self.phase_no = 0
        self.semcount = {e: 0 for e in self.ENGS}
        self.pes = ExitStack()
        self.res_w = {}
        self.res_r = {}
        self.bank_last = {}
        self._names = 0

    def sbuf(self, shape, dtype, name=None):
        self._names += 1
        return self.pes.enter_context(
            self.nc.sbuf_tensor("%s_%d" % (name or "sb", self._names), list(shape), dtype))

    def psum(self, shape, dtype, name=None):
        self._names += 1
        return self.pes.enter_context(
            self.nc.psum_tensor("%s_%d" % (name or "ps", self._names), list(shape), dtype))

    def _need(self, eng, tok, waits):
        if tok[0] == "E":
            _, e2, idx = tok
            if e2 == eng and eng == "pe":
                return
            k = ("E", e2)
            if self.known[eng].get(k, -1) >= idx:
                return
            self.known[eng][k] = idx
            self.ops[e2][idx].needs_inc = True
            waits.append(tok)
        else:
            _, slot, val = tok
            k = ("D", id(slot))
            if self.known[eng].get(k, 0) >= val:
                return
            self.known[eng][k] = val
            waits.append(tok)

    @staticmethod
    def _banks(*aps):
        out = []
        for a in aps:
            try:
                if str(a.space) == "PSUM":
                    out.append(a.name)
            except AttributeError:
                pass
        return out

    def op(self, eng, fn, reads=(), writes=(), dma=False, banks=(), cc=False):
        deps = []
        for bk in banks:
            for e2, t in self.bank_last.get(bk, {}).items():
                if e2 != eng:
                    deps.append(t)
        for r in reads:
            t = self.res_w.get(r)
            if t is not None:
                deps.append(t)
        for w in writes:
            t = self.res_w.get(w)
            if t is not None:
                deps.append(t)
            rr = self.res_r.get(w)
            if rr:
                deps.extend(rr.values())
        waits = []
        for t in deps:
            self._need(eng, t, waits)
        slot = None
        if cc:
            slot = self.cc_slot
            slot.count += 1
            tok = ("D", slot, slot.count)
            rkey = ("D", id(slot))
        elif dma:
            lst = self.slots[eng]
            slot = lst[self.slot_rr[eng] % len(lst)]
            self.slot_rr[eng] += 1
            if slot.count:
                self._need(eng, ("D", slot, slot.count), waits)
            slot.count += 16
            tok = ("D", slot, slot.count)
            rkey = ("D", id(slot))
        else:
            tok = ("E", eng, len(self.ops[eng]))
            rkey = ("E", eng)
        self.ops[eng].append(_Op(fn, waits, slot))
        for bk in banks:
            self.bank_last.setdefault(bk, {})[eng] = tok
        for r in reads:
            self.res_r.setdefault(r, {})[rkey] = tok
        for w in writes:
            self.res_w[w] = tok
            self.res_r[w] = {}
        return tok

    def mm(self, out, lhsT, rhs, start, stop, reads, writes):
        return self.op("pe", lambda h: h.matmul(out, lhsT, rhs, start=start, stop=stop), reads, writes,
                       banks=self._banks(out))

    def transpose(self, out, in_, ident, reads, writes):
        return self.op("pe", lambda h: h.transpose(out, in_, ident), reads, writes, banks=self._banks(out))

    def act(self, out, in_, func, reads, writes, bias=None, scale=None, eng="act"):
        kw = {}
        if bias is not None:
            kw["bias"] = bias
        if scale is not None:
            kw["scale"] = scale
        return self.op(eng, lambda h: h.activation(out, in_, func, **kw), reads, writes, banks=self._banks(out, in_))

    def tt(self, eng, out, in0, in1, op, reads, writes):
        return self.op(eng, lambda h: h.tensor_tensor(out, in0, in1, op), reads, writes, banks=self._banks(out, in0, in1))

    def ts(self, eng, out, in0, s1, s2, op0, op1, reads, writes):
        if s2 is None:
            return self.op(eng, lambda h: h.tensor_scalar(out, in0, s1, None, op0), reads, writes, banks=self._banks(out, in0, s1))
        return self.op(eng, lambda h: h.tensor_scalar(out, in0, s1, s2, op0, op1), reads, writes, banks=self._banks(out, in0, s1, s2))

    def stt(self, out, in0, scalar, in1, op0, op1, reads, writes):
        return self.op("dve", lambda h: h.scalar_tensor_tensor(out, in0, scalar, in1, op0, op1), reads, writes,
                       banks=self._banks(out, in0, scalar, in1))

    def copy(self, eng, out, in_, reads, writes):
        if eng == "act":
            return self.op(eng, lambda h: h.copy(out, in_), reads, writes, banks=self._banks(out, in_))
        return self.op(eng, lambda h: h.tensor_copy(out, in_), reads, writes, banks=self._banks(out, in_))

    def recip(self, out, in_, reads, writes):
        return self.op("dve", lambda h: h.reciprocal(out, in_), reads, writes, banks=self._banks(out, in_))

    def memset(self, eng, ap, val, writes):
        return self.op(eng, lambda h: h.memset(ap, val), (), writes)

    def dma(self, q, out, in_, reads, writes):
        return self.op(q, lambda h: h.dma_start(out=out, in_=in_), reads, writes, dma=True)

    def allgather(self, out, in_, groups, reads, writes):
        return self.op("pool", lambda h: h.collective_compute("AllGather", ALU.bypass, replica_groups=groups,
                                                              ins=[in_], outs=[out]), reads, writes, cc=True)

    def end_phase(self, last=False):
        nc = self.nc
        self.phase_no += 1
        final = {}
        for e in self.ENGS:
            comp = [o for o in self.ops[e] if o.dma_slot is None]
            if comp:
                comp[-1].needs_inc = True
            c = self.semcount[e]
            for o in self.ops[e]:
                if o.dma_slot is None and o.needs_inc:
                    c += 1
                    o.semval = c
            self.semcount[e] = c
            final[e] = c if comp else None
        fin = []
        for q in self.slots:
            for s_ in self.slots[q]:
                if s_.count:
                    fin.append((s_.sem, s_.count))
        if self.cc_slot.count:
            fin.append((self.cc_slot.sem, self.cc_slot.count))
        phase_no = self.phase_no

        def run(e, h):
            for o in self.ops[e]:
                for t in o.waits:
                    if t[0] == "E":
                        h.wait_ge(self.sems[t[1]], self.ops[t[1]][t[2]].semval)
                    else:
                        h.wait_ge(t[1].sem, t[2])
                ins = o.fn(h)
                if o.dma_slot is not None:
                    if o.dma_slot.inc == 1:
                        ins.then_inc(o.dma_slot.sem)
                    else:
                        ins.then_inc(o.dma_slot.sem, 16)
                elif o.needs_inc:
                    ins.then_inc(self.sems[e], 1)
            if final[e] is not None:
                h.wait_ge(self.sems[e], final[e])
            if e == "sp":
                for sem, v in fin:
                    h.wait_ge(sem, v)
            if not last:
                h.sem_inc(self.bar, 1)
                h.wait_ge(self.bar, 5 * phase_no)

        with nc.Block() as block:
            @block.tensor
            def _(h):
                run("pe", h)

            @block.scalar
            def _(h):
                run("act", h)

            @block.vector
            def _(h):
                run("dve", h)

            @block.gpsimd
            def _(h):
                run("pool", h)

            @block.sync
            def _(h):
                run("sp", h)
        self.ops = {e: [] for e in self.ENGS}
        for e in self.ENGS:
            self.known[e] = {k: v for k, v in self.known[e].items() if k[0] == "D"}
        self.res_w = {}
        self.res_r = {}
        self.bank_last = {}
        self.pes.close()
        self.pes = ExitStack()
        if last:
            self.es.close()

    def emit(self):
        self.end_phase(last=True)


def emit_ffn(P, nc, ntok, mode, dr):
    TT = TT_F
    ntile = ntok // TT
    w_out = P.sbuf([128, NKC, D], BF16, "w_out")
    w_gu = P.sbuf([128, NKC, 2 * FH], BF16, "w_gu")
    w_dn = P.sbuf([128, NHC, D], BF16, "w_dn")
    g_ffn = P.sbuf([128, NKC], F32, "g_ffn")
    g_nxt = P.sbuf([128, NKC], F32, "g_nxt")
    ones = P.sbuf([128, 128], BF16, "ones")
    P.memset("pool", ones[:], 1.0, ["ones"])
    P.dma("sp", g_ffn[:], dr["g_ffn"], [], ["g_ffn"])
    P.dma("sp", g_nxt[:], dr["g_nxt"], [], ["g_nxt"])
    wo_v = dr["w_out"].rearrange("(c p) n -> p c n", p=128)
    wo_perm = dr.get("wo_perm", list(range(NKC)))
    for c in range(NKC):
        P.dma("pool", w_out[:, c, :], wo_v[:, wo_perm[c], :], [], [("w_out", c)])
    wg_v = dr["w_gu"].rearrange("(c p) n -> p c n", p=128)
    for c in range(NKC):
        for hf in range(2):
            P.dma("pool", w_gu[:, c, hf * FH:(hf + 1) * FH], wg_v[:, c, hf * FH:(hf + 1) * FH], [], [("w_gu", c, hf)])
    wd_v = dr["w_dn"].rearrange("(c p) n -> p c n", p=128)
    for c in range(NHC):
        P.dma("pool", w_dn[:, c, :], wd_v[:, c, :], [], [("w_dn", c)])

    NB = 2
    xt = [P.sbuf([128, NKC, TT], F32, "xt") for _ in range(NB)]
    ot = [P.sbuf([128, NKC, TT], BF16, "ot") for _ in range(NB)]
    hn = P.sbuf([128, NKC, TT], BF16, "hn")
    actb = P.sbuf([128, NHC, TT], BF16, "actb")
    sg = [P.sbuf([128, TT], F32, "sg") for _ in range(2)]
    rstd = P.sbuf([128, TT], F32, "rstd")
    ps_acc = [P.psum([128, TT], F32, "ps_acc") for _ in range(2)]
    ps_g = [P.psum([128, TT], F32, "ps_g") for _ in range(2)]
    ps_u = [P.psum([128, TT], F32, "ps_u") for _ in range(2)]
    ps_s = P.psum([128, TT], F32, "ps_s")

    xT_v = dr["xT"].rearrange("(c p) t -> p c t", p=128)
    if "o_srcs" in dr:
        o_srcs = dr["o_srcs"]
    else:
        oT_v = dr["oT"].rearrange("(c p) t -> p c t", p=128)
        o_srcs = [lambda i: oT_v[:, :, i * TT:(i + 1) * TT]]
    if mode == "mid":
        if "h_dst" in dr:
            h_dst = dr["h_dst"]
        else:
            oh_v = dr["out_h"].rearrange("(c p) t -> p c t", p=128)
            h_dst = lambda i, m: oh_v[:, m, i * TT:(i + 1) * TT]
    if "n_dst" in dr:
        n_dst = dr["n_dst"]
    else:
        on_v = dr["out_n"].rearrange("(c p) t -> p c t", p=128)
        n_dst = lambda i, c: on_v[:, c, i * TT:(i + 1) * TT]
    if len(o_srcs) == 2:
        selv = P.sbuf([128, 2], F32, "selv")
        otb = P.sbuf([128, NKC, TT], BF16, "otb")
        P.dma("sp", selv[:], dr["sel"], [], ["selv"])

    def load(i):
        b = i % NB
        sl = slice(i * TT, (i + 1) * TT)
        for c in range(NKC):
            P.dma("sp", xt[b][:, c, :], xT_v[:, c, sl], [], [("xt", b, c)])
        P.dma("sp", ot[b][:], o_srcs[0](i), [], [("ot", b)])
        if len(o_srcs) == 2:
            P.dma("sp", otb[:], o_srcs[1](i), [], ["otb"])
            P.ts("dve", otb[:], otb[:], selv[:, 1:2], None, ALU.mult, None, ["otb", "selv"], ["otb"])
            for c in range(NKC):
                P.stt(ot[b][:, c, :], ot[b][:, c, :], selv[:, 0:1], otb[:, c, :], ALU.mult, ALU.add,
                      [("ot", b), "otb", "selv"], [("ot", b)])

    acc_i = [0]
    gu_i = [0]

    def rms(b, gam, gkey, dst, dkey):
        for c in range(NKC):
            P.act(actb[:, c, :], xt[b][:, c, :], AF.Square, [("xt", b, c)], [("actb", c)])
        for c in range(NKC):
            P.mm(ps_s[:], ones[:], actb[:, c, :], c == 0, c == NKC - 1, ["ones", ("actb", c)], ["ps_s"])
        P.act(rstd[:], ps_s[:], AF.Sqrt, ["ps_s"], ["rstd"], bias=EPS, scale=1.0 / D)
        P.recip(rstd[:], rstd[:], ["rstd"], ["rstd"])
        for c in range(NKC):
            P.stt(dst(c), xt[b][:, c, :], gam[:, c:c + 1], rstd[:], ALU.mult, ALU.mult,
                  [("xt", b, c), "rstd", gkey], [dkey(c)])

    load(0)
    for i in range(ntile):
        b = i % NB
        sl = slice(i * TT, (i + 1) * TT)
        if i + 1 < ntile:
            load(i + 1)
        for m in range(NKC):
            pa = acc_i[0] % 2
            acc_i[0] += 1
            for k in range(NKC):
                P.mm(ps_acc[pa][:], w_out[:, k, m * 128:(m + 1) * 128], ot[b][:, k, :], k == 0, k == NKC - 1,
                     [("w_out", k), ("ot", b)], [("ps_acc", pa)])
            P.tt("dve", xt[b][:, m, :], xt[b][:, m, :], ps_acc[pa][:], ALU.add,
                 [("xt", b, m), ("ps_acc", pa)], [("xt", b, m)])
        rms(b, g_ffn, "g_ffn", lambda c: hn[:, c, :], lambda c: ("hn", c))
        for j in range(NHC):
            pg = gu_i[0] % 2
            gu_i[0] += 1
            for k in range(NKC):
                P.mm(ps_g[pg][:], w_gu[:, k, j * 128:(j + 1) * 128], hn[:, k, :], k == 0, k == NKC - 1,
                     [("w_gu", k, 0), ("hn", k)], [("ps_g", pg)])
            for k in range(NKC):
                P.mm(ps_u[pg][:], w_gu[:, k, FH + j * 128:FH + (j + 1) * 128], hn[:, k, :], k == 0, k == NKC - 1,
                     [("w_gu", k, 1), ("hn", k)], [("ps_u", pg)])
            P.act(sg[pg][:], ps_g[pg][:], AF.Silu, [("ps_g", pg)], [("sg", pg)])
            P.tt("dve", actb[:, j, :], sg[pg][:], ps_u[pg][:], ALU.mult, [("sg", pg), ("ps_u", pg)], [("actb", j)])
        for m in range(NKC):
            pa = acc_i[0] % 2
            acc_i[0] += 1
            for j in range(NHC):
                P.mm(ps_acc[pa][:], w_dn[:, j, m * 128:(m + 1) * 128], actb[:, j, :], j == 0, j == NHC - 1,
                     [("w_dn", j), ("actb", j)], [("ps_acc", pa)])
            P.tt("dve", xt[b][:, m, :], xt[b][:, m, :], ps_acc[pa][:], ALU.add,
                 [("xt", b, m), ("ps_acc", pa)], [("xt", b, m)])
            if mode == "mid":
                P.dma("sp", h_dst(i, m), xt[b][:, m, :], [("xt", b, m)], [("out_h", i, m)])
        if mode == "mid":
            rms(b, g_nxt, "g_nxt", lambda c: hn[:, c, :], lambda c: ("hn", c))
            for c in range(NKC):
                P.dma("sp", n_dst(i, c), hn[:, c, :], [("hn", c)], [("out_n", i, c)])
            if "after_tile" in dr:
                dr["after_tile"](i)
        else:
            rms(b, g_nxt, "g_nxt", lambda c: xt[b][:, c, :], lambda c: ("xt", b, c))
            for c in range(NKC):
                P.dma("sp", n_dst(i, c), xt[b][:, c, :], [("xt", b, c)], [("out_n", i, c)])


def build_ffn(ntok, mode):
    nc = bass.Bass("TRN2", target_bir_lowering=False)
    dr = {}
    dr["oT"] = nc.dram_tensor("oT", [D, ntok], BF16, kind="ExternalInput").ap()
    dr["xT"] = nc.dram_tensor("xT", [D, ntok], F32, kind="ExternalInput").ap()
    dr["w_out"] = nc.dram_tensor("w_out", [D, D], F32, kind="ExternalInput").ap()
    dr["g_ffn"] = nc.dram_tensor("g_ffn", [128, NKC], F32, kind="ExternalInput").ap()
    dr["g_nxt"] = nc.dram_tensor("g_nxt", [128, NKC], F32, kind="ExternalInput").ap()
    dr["w_gu"] = nc.dram_tensor("w_gu", [D, 2 * FH], F32, kind="ExternalInput").ap()
    dr["w_dn"] = nc.dram_tensor("w_dn", [FH, D], F32, kind="ExternalInput").ap()
    if mode == "mid":
        dr["out_h"] = nc.dram_tensor("out_h", [D, ntok], F32, kind="ExternalOutput").ap()
        dr["out_n"] = nc.dram_tensor("out_n", [D, ntok], BF16, kind="ExternalOutput").ap()
    else:
        dr["out_n"] = nc.dram_tensor("out_n", [D, ntok], F32, kind="ExternalOutput").ap()
    P = Prog(nc)
    emit_ffn(P, nc, ntok, mode, dr)
    P.emit()
    return nc


def gvec(g):
    return np.ascontiguousarray(np.asarray(g, np.float32).reshape(NKC, 128).T)


TT1 = 512
LC = 64
C1_TRI = 0
C1_RST = 256
C1_DQ = 768
C1_EXL = 768 + 6 * 512
C1_N = C1_EXL + 16


def emit_mix0(P, nc, ntok, dr):
    ntile = ntok // TT1
    w_in = P.sbuf([128, NKC, 2048], BF16, "w_in")
    g_mix = P.sbuf([128, NKC], F32, "g_mix")
    cst = P.sbuf([128, C1_N], F32, "cst")
    ident = P.sbuf([128, 128], BF16, "ident")
    ones = P.sbuf([128, 128], BF16, "ones")
    lbp = P.sbuf([128, 4], F32, "lbp")
    lb = P.sbuf([128, 2], F32, "lb")
    oml = P.sbuf([128, 2], F32, "oml")
    wn = P.sbuf([128, 2], F32, "wn")
    P.memset("pool", ones[:], 1.0, ["ones"])
    P.dma("sp", g_mix[:], dr["g_mix"], [], ["g_mix"])
    P.dma("sp", cst[:], dr["cst"], [], ["cst"])
    P.dma("sp", lbp[:], dr["lbp"], [], ["lbp"])
    P.dma("sp", wn[:], dr["wn"], [], ["wn"])
    P.dma("pool", ident[:], dr["ident"], [], ["ident"])
    wi_v = dr["w_in"].rearrange("(c p) n -> p c n", p=128)
    for c in range(NKC):
        P.dma("pool", w_in[:, c, :], wi_v[:, c, :], [], [("w_in", c)])
    P.tt("dve", lb[:], lbp[:, 0:2], lbp[:, 2:4], ALU.subtract, ["lbp"], ["lb"])
    P.act(lb[:], lb[:], AF.Sigmoid, ["lb"], ["lb"])
    P.ts("dve", oml[:], lb[:], -1.0, 1.0, ALU.mult, ALU.add, ["lb"], ["oml"])

    tri4 = cst[0:64, C1_TRI:C1_TRI + 256]
    rst = cst[:, C1_RST:C1_RST + 512]

    def dtab(kind, r):
        o = C1_DQ + (kind * 2 + r) * 512
        return cst[:, o:o + 512]

    xt = [P.sbuf([128, NKC, TT1], F32, "xt") for _ in range(2)]
    hn = P.sbuf([128, NKC, TT1], BF16, "hn")
    sq = P.sbuf([128, NKC, TT1], BF16, "sq")
    rstd = P.sbuf([128, TT1], F32, "rstd")
    t_a = P.sbuf([128, TT1], F32, "t_a")
    t_f = P.sbuf([128, TT1], F32, "t_f")
    t_g = P.sbuf([128, TT1], F32, "t_g")
    t_c = P.sbuf([128, TT1], F32, "t_c")
    t_n = P.sbuf([128, TT1], F32, "t_n")
    t_k = P.sbuf([128, TT1], F32, "t_k")
    ecum = [P.sbuf([128, TT1], F32, "ecum") for _ in range(2)]
    qe = [P.sbuf([128, TT1], BF16, "qe") for _ in range(4)]
    ke = [P.sbuf([128, TT1], BF16, "ke") for _ in range(4)]
    kl = [P.sbuf([128, TT1], BF16, "kl") for _ in range(4)]
    gs = [P.sbuf([128, TT1], F32, "gs") for _ in range(4)]
    v_tok = P.sbuf([128, 4, 512], BF16, "v_tok")
    kl_tok = P.sbuf([128, 4, 4, 128], BF16, "kl_tok")
    atm = P.sbuf([128, 4, 64], BF16, "atm")
    S = P.sbuf([128, 4, 128], F32, "S")
    S_bf = [P.sbuf([128, 4, 128], BF16, "S_bf") for _ in range(2)]
    osq = P.sbuf([128, TT1], BF16, "osq")
    ors = P.sbuf([128, TT1], F32, "ors")
    otmp = P.sbuf([128, TT1], F32, "otmp")
    of = [P.sbuf([128, 4, TT1], BF16, "of") for _ in range(2)]

    ps_in = [P.psum([128, TT1], F32, "ps_in") for _ in range(2)]
    ps_o = [P.psum([128, TT1], F32, "ps_o") for _ in range(4)]
    ps_ds = P.psum([128, 4, 128], F32, "ps_ds")
    ps_misc = P.psum([128, 512], F32, "ps_misc")
    ps_at = ps_misc[0:64, 0:256].rearrange("p (u c) -> p u c", c=LC)
    ps_kt = ps_misc[:, 256:512].bitcast(BF16).rearrange("p (s d) -> p s d", d=128)

    P.memset("dve", S[:], 0.0, ["S"])
    P.memset("pool", S_bf[0][:], 0.0, [("S_bf", 0)])

    xT_v = dr["xT"].rearrange("(c p) t -> p c t", p=128)
    if "o_dst" in dr:
        o_dst = dr["o_dst"]
    else:
        o_v = dr["oT_out"].rearrange("(u p) t -> p u t", p=128)
        o_dst = lambda i, u: o_v[:, u, i * TT1:(i + 1) * TT1]

    def load(i):
        b = i % 2
        sl = slice(i * TT1, (i + 1) * TT1)
        for c in range(NKC):
            P.dma("sp", xt[b][:, c, :], xT_v[:, c, sl], [], [("xt", b, c)])

    pin = [0]

    def proj_fm(blk):
        p = pin[0] % 2
        pin[0] += 1
        for k in range(NKC):
            P.mm(ps_in[p][:], w_in[:, k, blk * 128:(blk + 1) * 128], hn[:, k, :], k == 0, k == NKC - 1,
                 [("w_in", k)] + [("hn", k)], [("ps_in", p)])
        return p

    sbi = [0]
    load(0)
    for i in range(ntile):
        b = i % 2
        sl = slice(i * TT1, (i + 1) * TT1)
        if i + 1 < ntile:
            load(i + 1)
        for c in range(NKC):
            P.act(sq[:, c, :], xt[b][:, c, :], AF.Square, [("xt", b, c)], [("sq", c)])
        p = pin[0] % 2
        pin[0] += 1
        for c in range(NKC):
            P.mm(ps_in[p][:], ones[:], sq[:, c, :], c == 0, c == NKC - 1, ["ones", ("sq", c)], [("ps_in", p)])
        P.act(rstd[:], ps_in[p][:], AF.Sqrt, [("ps_in", p)], ["rstd"], bias=EPS, scale=1.0 / D)
        P.recip(rstd[:], rstd[:], ["rstd"], ["rstd"])
        for c in range(NKC):
            P.stt(hn[:, c, :], xt[b][:, c, :], g_mix[:, c:c + 1], rstd[:], ALU.mult, ALU.mult,
                  [("xt", b, c), "rstd", "g_mix"], [("hn", c)])
        for s in range(4):
            p = pin[0] % 2
            pin[0] += 1
            for k in range(NKC):
                P.mm(ps_in[p][:], hn[:, k, s * 128:(s + 1) * 128], w_in[:, k, 1536:2048], k == 0, k == NKC - 1,
                     [("w_in", k), ("hn", k)], [("ps_in", p)])
            P.copy("act", v_tok[:, s, :], ps_in[p][:], [("ps_in", p)], [("v_tok", s)])
        for u in range(4):
            if u < 2:
                p = proj_fm(u)
                P.act(t_a[:], ps_in[p][:], AF.Silu, [("ps_in", p)], ["t_a"])
                p = proj_fm(4 + u)
                P.act(t_f[:], ps_in[p][:], AF.Sigmoid, [("ps_in", p)], ["t_f"])
                P.ts("dve", t_f[:], t_f[:], oml[:, u:u + 1], lb[:, u:u + 1], ALU.mult, ALU.add,
                     ["t_f", "oml", "lb"], ["t_f"])
                P.act(t_g[:], t_f[:], AF.Ln, ["t_f"], ["t_g"])
                P.op("dve", lambda h, o=t_c[:], m=rst, g=t_g[:]: h.tensor_tensor_scan(o, m, g, 0.0, ALU.mult, ALU.add),
                     ["cst", "t_g"], ["t_c"])
                P.act(ecum[u][:], t_c[:], AF.Exp, ["t_c"], [("ecum", u)])
                P.act(t_n[:], t_c[:], AF.Exp, ["t_c"], ["t_n"], scale=-1.0)
                P.ts("dve", t_k[:], t_f[:], -1.0, 1.0, ALU.mult, ALU.add, ["t_f"], ["t_k"])
                P.tt("dve", qe[u][:], t_a[:], ecum[u][:], ALU.mult, ["t_a", ("ecum", u)], [("qe", u)])
                P.tt("dve", ke[u][:], t_k[:], t_n[:], ALU.mult, ["t_k", "t_n"], [("ke", u)])
                ev = ecum[u][:].rearrange("p (n c) -> p n c", c=LC)[:, :, LC - 1:LC].to_broadcast([128, TT1 // LC, LC])
                P.tt("dve", kl[u][:].rearrange("p (n c) -> p n c", c=LC), ke[u][:].rearrange("p (n c) -> p n c", c=LC),
                     ev, ALU.mult, [("ke", u), ("ecum", u)], [("kl", u)])
            else:
                r = u - 2
                p = proj_fm(u)
                P.tt("dve", qe[u][:], ps_in[p][:], dtab(0, r), ALU.mult, [("ps_in", p), "cst"], [("qe", u)])
                p = proj_fm(4 + u)
                P.tt("dve", ke[u][:], ps_in[p][:], dtab(1, r), ALU.mult, [("ps_in", p), "cst"], [("ke", u)])
                P.tt("dve", kl[u][:], ps_in[p][:], dtab(2, r), ALU.mult, [("ps_in", p), "cst"], [("kl", u)])
            p = proj_fm(8 + u)
            P.act(gs[u][:], ps_in[p][:], AF.Silu, [("ps_in", p)], [("gs", u)])
            for s in range(4):
                P.transpose(ps_kt[:, s, :], kl[u][:, s * 128:(s + 1) * 128], ident[:],
                            [("kl", u), "ident"], ["ps_kt"])
            P.copy("act", kl_tok[:, :, u, :], ps_kt, ["ps_kt"], [("kl_tok", u)])
        for n in range(TT1 // LC):
            c0 = n * LC
            s = n // 2
            r0 = (n % 2) * LC
            for u in range(4):
                P.mm(ps_ds[:, u, :], kl_tok[r0:r0 + LC, s, u, :], v_tok[r0:r0 + LC, s, u * 128:(u + 1) * 128], True, True,
                     [("kl_tok", u), ("v_tok", s)], ["ps_ds"])
            for u in range(4):
                P.mm(ps_at[:, u, :], ke[u][:, c0:c0 + LC], qe[u][:, c0:c0 + LC], True, True,
                     [("ke", u), ("qe", u)], ["ps_at"])
            P.tt("dve", atm[r0:r0 + LC], ps_at, cst[r0:r0 + LC, C1_TRI:C1_TRI + 256].rearrange("p (u c) -> p u c", c=LC),
                 ALU.mult, ["ps_at", "cst"], [("atm", n % 2)])
            sb = sbi[0] % 2
            for u in range(4):
                P.mm(ps_o[u][:, c0:c0 + LC], v_tok[r0:r0 + LC, s, u * 128:(u + 1) * 128], atm[r0:r0 + LC, u, :], True, False,
                     [("v_tok", s), ("atm", n % 2)], [("ps_o", u)])
                P.mm(ps_o[u][:, c0:c0 + LC], S_bf[sb][:, u, :], qe[u][:, c0:c0 + LC], False, True,
                     [("S_bf", sb), ("qe", u)], [("ps_o", u)])
            for u in range(4):
                if u < 2:
                    ex = ecum[u][:, c0 + LC - 1:c0 + LC]
                    rk = [("ecum", u)]
                else:
                    ex = cst[:, C1_EXL + (u - 2) * 8:C1_EXL + (u - 2) * 8 + 1]
                    rk = ["cst"]
                P.stt(S[:, u, :], S[:, u, :], ex, ps_ds[:, u, :], ALU.mult, ALU.add, ["S", "ps_ds"] + rk, ["S"])
            sbi[0] += 1
            P.copy("act", S_bf[sbi[0] % 2][:], S[:], ["S"], [("S_bf", sbi[0] % 2)])
        ob = i % 2
        for u in range(4):
            P.act(osq[:], ps_o[u][:], AF.Square, [("ps_o", u)], ["osq"])
            p = pin[0] % 2
            pin[0] += 1
            P.mm(ps_in[p][:], ones[:], osq[:], True, True, ["ones", "osq"], [("ps_in", p)])
            P.act(ors[:], ps_in[p][:], AF.Sqrt, [("ps_in", p)], ["ors"], bias=EPS, scale=1.0 / 128)
            P.recip(ors[:], ors[:], ["ors"], ["ors"])
            P.tt("dve", otmp[:], ps_o[u][:], ors[:], ALU.mult, [("ps_o", u), "ors"], ["otmp"])
            wc = 0 if u < 2 else 1
            P.stt(of[ob][:, u, :], otmp[:], wn[:, wc:wc + 1], gs[u][:], ALU.mult, ALU.mult,
                  ["otmp", "wn", ("gs", u)], [("of", ob, u)])
            P.dma("sp", o_dst(i, u), of[ob][:, u, :], [("of", ob, u)], [("oT_out", i, u)])
        if "after_tile" in dr:
            dr["after_tile"](i)


def build_mix0(ntok):
    nc = bass.Bass("TRN2", target_bir_lowering=False)
    dr = {}
    dr["xT"] = nc.dram_tensor("xT", [D, ntok], F32, kind="ExternalInput").ap()
    dr["w_in"] = nc.dram_tensor("w_in", [D, 2048], F32, kind="ExternalInput").ap()
    dr["g_mix"] = nc.dram_tensor("g_mix", [128, NKC], F32, kind="ExternalInput").ap()
    dr["cst"] = nc.dram_tensor("cst", [128, C1_N], F32, kind="ExternalInput").ap()
    dr["ident"] = nc.dram_tensor("ident", [128, 128], F32, kind="ExternalInput").ap()
    dr["lbp"] = nc.dram_tensor("lbp", [128, 4], F32, kind="ExternalInput").ap()
    dr["wn"] = nc.dram_tensor("wn", [128, 2], F32, kind="ExternalInput").ap()
    dr["oT_out"] = nc.dram_tensor("oT_out", [512, ntok], BF16, kind="ExternalOutput").ap()
    P = Prog(nc)
    emit_mix0(P, nc, ntok, dr)
    P.emit()
    return nc


def mix0_consts(hh):
    c = np.zeros((128, C1_N), np.float32)
    j = np.arange(64)[:, None]
    i = np.arange(64)[None, :]
    tri = (j <= i).astype(np.float32)
    c[0:64, C1_TRI:C1_TRI + 256] = np.tile(tri, (1, 4))
    c[64:128, C1_TRI:C1_TRI + 256] = np.tile(tri, (1, 4))
    pos = np.arange(512) % LC
    c[:, C1_RST:C1_RST + 512] = (pos != 0).astype(np.float32)[None, :]
    for r in range(2):
        hidx = 2 * hh + r
        lg = np.log(np.float32(1.0) - np.exp2(np.float32(-5.0 - hidx))).astype(np.float32)
        qd = np.exp((pos + 1.0).astype(np.float32) * lg).astype(np.float32)
        kd = (np.exp(-(pos + 1.0).astype(np.float32) * lg) * np.float32(128 ** -0.5)).astype(np.float32)
        ld = (np.exp((LC - 1.0 - pos).astype(np.float32) * lg) * np.float32(128 ** -0.5)).astype(np.float32)
        c[:, C1_DQ + (0 * 2 + r) * 512:C1_DQ + (0 * 2 + r) * 512 + 512] = qd[None, :]
        c[:, C1_DQ + (1 * 2 + r) * 512:C1_DQ + (1 * 2 + r) * 512 + 512] = kd[None, :]
        c[:, C1_DQ + (2 * 2 + r) * 512:C1_DQ + (2 * 2 + r) * 512 + 512] = ld[None, :]
        c[:, C1_EXL + r * 8:C1_EXL + r * 8 + 8] = np.exp(np.float32(LC) * lg)
    return c


NEGM = -30000.0
DUM_SLC = 1
DUM_WIN = 2
HD = 64
NR = 8
WIN_IN = 512 + 128 + 128 + 128 + 24
C2_BC = 0
C2_BT = 4
C2_AW = 68
C2_DW = 68 + 255
C2_SL = 68 + 510
C2_NT = 68 + 510 + 8
C2_N = 68 + 510 + 8 + 16
B2_ID = 0
B2_TU = 128
B2_TL = 256
B2_CM = 384
OVS = 130
B2_OV = 384 + 2560
B2_SEL = B2_OV + 4 * OVS
B2_N = B2_SEL + 24 * 64


def emit_mix1(P, nc, ntok, dr, slopes, stop=99):
    ntile = ntok // 512
    NKCH = ntok // 128
    NCB = ntok // 16 - 1
    NCC = ntok // 2048
    w_in = P.sbuf([128, NKC, WIN_IN], BF16, "w_in")
    cst = P.sbuf([128, C2_N], F32, "cst")
    cb = P.sbuf([128, B2_N], BF16, "cb")
    w1kv = P.sbuf([128, 32, 64], BF16, "w1kv")
    poskv = P.sbuf([128, 32], BF16, "poskv")
    w2kv = P.sbuf([64, 128], BF16, "w2kv")
    kz = P.sbuf([128, ntok], BF16, "kz")
    ks_x = P.sbuf([67, ntok], BF16, "ks_x")
    kw_x = P.sbuf([67, ntok], BF16, "kw_x")
    vs_tok = P.sbuf([128, NKCH, 128], BF16, "vs_tok")
    vw_tok = P.sbuf([128, NKCH, 128], BF16, "vw_tok")
    kc_x = P.sbuf([67, NCC * 128], BF16, "kc_x")
    vc_tok = P.sbuf([128, NCC, 128], BF16, "vc_tok")
    hid = P.sbuf([64, 2, NCC * 128], BF16, "hid")
    cbias = P.sbuf([64, 2], F32, "cbias")
    hn = [P.sbuf([128, NKC, 512], BF16, "hn") for _ in range(2)]
    q_x = [P.sbuf([67, 512], BF16, "q_x") for _ in range(NR)]
    gsig = P.sbuf([24, 512], F32, "gsig")
    g_hl = P.sbuf([56, 512], BF16, "g_hl")
    bias_c = P.sbuf([128, NR, NCC], F32, "bias_c")
    bias_t = P.sbuf([128, NR, NKCH], F32, "bias_t")
    Ec = [P.sbuf([128, NCC, 512], BF16, "Ec") for _ in range(2)]
    NEB = 6
    Eb = [P.sbuf([128, 512], BF16, "Eb") for _ in range(NEB)]
    oacc = P.sbuf([64, NR, 512], F32, "oacc")
    rcp = P.sbuf([64, 512], F32, "rcp")
    wgt = P.sbuf([64, 512], F32, "wgt")
    tmpo = P.sbuf([64, 512], F32, "tmpo")
    obf = [P.sbuf([64, NR, 512], BF16, "obf") for _ in range(2)]
    pacc = P.sbuf([128, 4, 128], F32, "pacc")
    rc1 = P.sbuf([128, 1], F32, "rc1")
    sc = P.sbuf([128, 128], F32, "sc")
    sc2 = P.sbuf([128, 128], F32, "sc2")
    m8 = P.sbuf([128, 16], F32, "m8")
    nm = P.sbuf([128, 128], BF16, "nm")
    negmT = P.sbuf([128, 512], BF16, "negmT")

    NPS = 4
    ps_p = [P.psum([128, 512], F32, "ps_g") for _ in range(NPS)]
    ps_s = ps_p
    ps_o = [P.psum([128, 512], F32, "ps_o") for _ in range(2)]
    ps_u = [P.psum([128, 512], F32, "ps_u") for _ in range(2)]

    P.dma("sp", cst[:], dr["cst"], [], ["cst"])
    for j0 in range(0, B2_N, 1024):
        j1 = min(B2_N, j0 + 1024)
        P.dma("pool", cb[:, j0:j1], dr["cb"][:, j0:j1], [], ["cb"])
    wi_v = dr["w_in"].rearrange("(c p) n -> p c n", p=128)
    for c in range(NKC):
        P.dma("pool", w_in[:, c, :], wi_v[:, c, :], [], [("w_in", c)])
    P.dma("pool", w1kv[:].rearrange("p a b -> p (a b)"), dr["w1kv"], [], ["w1kv"])
    P.dma("pool", poskv[:], dr["poskv"], [], ["poskv"])
    P.dma("pool", w2kv[:], dr["w2kv"], [], ["w2kv"])
    for r in range(NR):
        P.dma("pool", q_x[r][64:67, :], dr["qbias"][:, r, :], [], [("q_b", r)])
    ident = cb[:, B2_ID:B2_ID + 128]
    tri_u = cb[:, B2_TU:B2_TU + 128]
    tri_l = cb[:, B2_TL:B2_TL + 128]
    P.memset("dve", ks_x[64:67, :], 1.0, ["ks_b"])
    P.memset("dve", kw_x[64:67, :], 1.0, ["kw_b"])
    P.memset("dve", kc_x[:], 0.0, ["kc_x"])
    P.memset("dve", kc_x[64:67, :], 1.0, ["kc_x"])
    P.memset("pool", vs_tok[:, :, 64:128], 1.0, ["vs_ones"])
    P.memset("pool", vw_tok[:, :, 64:128], 1.0, ["vw_ones"])
    P.memset("pool", vc_tok[:], 0.0, ["vc_tok"])
    P.memset("pool", vc_tok[:, :, 64:128], 1.0, ["vc_tok"])
    P.memset("pool", g_hl[:], 0.0, ["g_hl"])
    P.memset("pool", hid[:], 0.0, ["hid"])

    if stop <= 0:
        return
    if "hn_src" in dr:
        hn_src = dr["hn_src"]
    else:
        hn_v = dr["hnT"].rearrange("(c p) t -> p c t", p=128)
        hn_src = lambda i, c: hn_v[:, c, i * 512:(i + 1) * 512]

    def load(slot, i):
        for c in range(NKC):
            P.dma("sp", hn[slot][:, c, :], hn_src(i, c), [], [("hn", slot, c)])

    ppi = [0]

    def nextp():
        p = ppi[0] % NPS
        ppi[0] += 1
        return p

    nload = [0]
    load(0, 0)
    for i in range(ntile):
        b = nload[0] % 2
        nload[0] += 1
        if i + 1 < ntile:
            load(nload[0] % 2, i + 1)
        else:
            load(nload[0] % 2, 0)
        sl = slice(i * 512, (i + 1) * 512)
        hk = [("hn", b, c) for c in range(NKC)]
        p = nextp()
        for k in range(NKC):
            P.mm(ps_p[p][:], w_in[:, k, 512:640], hn[b][:, k, :], k == 0, k == NKC - 1, [("w_in", k), ("hn", b, k)], [("ps_p", p)])
        P.copy("act", kz[:, sl], ps_p[p][:], [("ps_p", p)], [("kz", i)])
        if stop <= 0.3:
            continue
        p = nextp()
        for k in range(NKC):
            P.mm(ps_p[p][:], w_in[:, k, 640:768], hn[b][:, k, :], k == 0, k == NKC - 1, [("w_in", k), ("hn", b, k)], [("ps_p", p)])
        P.copy("dve", ks_x[0:64, sl], ps_p[p][0:64, :], [("ps_p", p)], [("ks_x", i)])
        P.copy("dve", kw_x[0:64, sl], ps_p[p][64:128, :], [("ps_p", p)], [("kw_x", i)])
        if stop <= 0.6:
            continue
        p = nextp()
        for s in range(4):
            for k in range(NKC):
                P.mm(ps_p[p][:, s * 128:(s + 1) * 128], hn[b][:, k, s * 128:(s + 1) * 128], w_in[:, k, 768:896], k == 0, k == NKC - 1,
                     [("w_in", k), ("hn", b, k)], [("ps_p", p)])
        pv = ps_p[p][:].rearrange("p (s c) -> p s c", c=128)
        P.copy("dve", vs_tok[:, 4 * i:4 * i + 4, 0:64], pv[:, :, 0:64], [("ps_p", p)], [("vs_tok", i)])
        P.copy("dve", vw_tok[:, 4 * i:4 * i + 4, 0:64], pv[:, :, 64:128], [("ps_p", p)], [("vw_tok", i)])

    if stop <= 1:
        return
    kzall = [("kz", i) for i in range(ntile)]
    for kv in range(2):
        base = 64 * kv
        p = nextp()
        for pp in range(32):
            P.mm(ps_p[p][0:64, 0:1], w1kv[base:base + 64, pp, :], poskv[base:base + 64, pp:pp + 1], pp == 0, pp == 31,
                 ["w1kv", "poskv"], [("ps_p", p)])
        P.copy("dve", cbias[:, kv:kv + 1], ps_p[p][0:64, 0:1], [("ps_p", p)], [("cbias", kv)])
        for c0 in range(0, NCB, 512):
            cn = min(512, NCB - c0)
            p = nextp()
            for pp in range(32):
                rhs = kz[base:base + 64, pp + 16 * c0: pp + 16 * c0 + 16 * (cn - 1) + 1: 16]
                P.mm(ps_p[p][0:64, 0:cn], w1kv[base:base + 64, pp, :], rhs, pp == 0, pp == 31, ["w1kv"] + kzall, [("ps_p", p)])
            P.act(hid[:, kv, c0:c0 + cn], ps_p[p][0:64, 0:cn], AF.Silu, [("ps_p", p), ("cbias", kv)], ["hid"],
                  bias=cbias[:, kv:kv + 1])
    for c0 in range(0, NCB, 512):
        cn = min(512, NCB - c0)
        p = nextp()
        P.mm(ps_p[p][0:64, 0:cn], w2kv[:, 0:64], hid[:, 0, c0:c0 + cn], True, True, ["w2kv", "hid"], [("ps_p", p)])
        P.copy("dve", kc_x[0:64, c0:c0 + cn], ps_p[p][0:64, 0:cn], [("ps_p", p)], ["kc_x"])
    for m in range(NCC):
        p = nextp()
        P.mm(ps_p[p][:, 0:64], hid[:, 1, m * 128:(m + 1) * 128], w2kv[:, 64:128], True, True, ["w2kv", "hid"], [("ps_p", p)])
        rows = 128 if (m + 1) * 128 <= NCB else NCB - m * 128
        P.copy("dve", vc_tok[0:rows, m, 0:64], ps_p[p][0:rows, 0:64], [("ps_p", p)], ["vc_tok"])
    for j0 in range(0, ntok, 2048):
        P.dma("pool", kz[:, j0:j0 + 2048], dr["zexp"][:, j0:j0 + 2048], [], kzall)

    if stop <= 2:
        return
    sp_c = P.sbuf([128, NR, NCC], F32, "sp_c")
    sp_t = P.sbuf([128, NR, NKCH], F32, "sp_t")
    nt0 = P.sbuf([128, NR, 16], F32, "nt0")
    for r in range(NR):
        slp = cst[:, C2_SL + r:C2_SL + r + 1]
        P.ts("dve", sp_c[:, r, :], cst[:, C2_BC:C2_BC + NCC], slp, None, ALU.mult, None, ["cst"], ["sp_c"])
        P.ts("dve", sp_t[:, r, :], cst[:, C2_BT:C2_BT + NKCH], slp, None, ALU.mult, None, ["cst"], ["sp_t"])
        P.ts("dve", nt0[:, r, :], cst[:, C2_NT:C2_NT + 16], slp, None, ALU.mult, None, ["cst"], ["nt0"])
    if "o_dst" in dr:
        o_dst1 = dr["o_dst"]
    else:
        o_v = dr["oT_out"].rearrange("(r p) t -> p r t", p=64)
        o_dst1 = lambda n_: o_v[:, :, n_ * 512:(n_ + 1) * 512]
    psi = [0]
    poi = [0]
    ebi = [0]

    def gate_w(r, br, po):
        P.ts("dve", rcp[:], ps_o[po][64:128, :], 1e-30, None, ALU.add, None, [("ps_o", po)], ["rcp"])
        P.recip(rcp[:], rcp[:], ["rcp"], ["rcp"])
        p = nextp()
        P.mm(ps_p[p][0:64, :], cb[0:56, B2_SEL + (3 * r + br) * 64:B2_SEL + (3 * r + br + 1) * 64], g_hl[:], True, True,
             ["cb", "g_hl"], [("ps_p", p)])
        P.tt("dve", wgt[:], rcp[:], ps_p[p][0:64, :], ALU.mult, ["rcp", ("ps_p", p)], ["wgt"])

    import os
    dbg_lo, dbg_hi = [int(v) for v in os.environ.get('DBG_TILES', '0,99').split(',')]
    for n in range(ntile):
        b = nload[0] % 2
        nload[0] += 1
        if n + 1 < ntile:
            load(nload[0] % 2, n + 1)
        if n < dbg_lo or n >= dbg_hi:
            continue
        t0 = 512 * n
        sl = slice(t0, t0 + 512)
        for r in range(NR):
            p = nextp()
            for k in range(NKC):
                P.mm(ps_p[p][0:64, :], w_in[:, k, r * 64:(r + 1) * 64], hn[b][:, k, :], k == 0, k == NKC - 1,
                     [("w_in", k), ("hn", b, k)], [("ps_p", p)])
            P.op("act", lambda h, o=q_x[r][0:64, :], i_=ps_p[p][0:64, :]: h.mul(o, i_, HD ** -0.5), [("ps_p", p)], [("q_x", r)],
                 banks=P._banks(ps_p[p][0:64, :]))
        if stop <= 2.3:
            continue
        p = nextp()
        for k in range(NKC):
            P.mm(ps_p[p][0:24, :], w_in[:, k, 896:920], hn[b][:, k, :], k == 0, k == NKC - 1,
                 [("w_in", k), ("hn", b, k)], [("ps_p", p)])
        P.act(gsig[:], ps_p[p][0:24, :], AF.Sigmoid, [("ps_p", p)], ["gsig"])
        P.copy("dve", g_hl[0:24, :], gsig[:], ["gsig"], ["g_hl"])
        P.tt("dve", g_hl[32:56, :], gsig[:], g_hl[0:24, :], ALU.subtract, ["gsig", "g_hl"], ["g_hl"])
        if stop <= 2.6:
            continue
        P.tt("dve", bias_c[:], sp_c[:], nt0[:, :, n:n + 1].to_broadcast([128, NR, NCC]), ALU.add, ["sp_c", "nt0"],
             [("bias_c", r) for r in range(NR)])
        P.tt("dve", bias_t[:], sp_t[:], nt0[:, :, n:n + 1].to_broadcast([128, NR, NKCH]), ALU.add, ["sp_t", "nt0"],
             [("bias_t", r) for r in range(NR)])
        if stop <= 3:
            continue
        mlist = [m for m in range(NCC) if 2048 * m + 31 <= t0 + 511]
        for r in range(NR):
            e = Ec[r % 2]
            po = poi[0] % 2
            poi[0] += 1
            for mi, m in enumerate(mlist):
                ps = nextp()
                o = n - 4 * m
                mixed = 0 <= o <= 4
                P.mm(ps_s[ps][:], kc_x[:, m * 128:(m + 1) * 128], q_x[r][:], True, not mixed,
                     ["kc_x", ("q_x", r), ("q_b", r)], [("ps_p", ps)])
                if mixed:
                    P.mm(ps_s[ps][:], ident, cb[:, B2_CM + o * 512:B2_CM + (o + 1) * 512], False, True, ["cb"], [("ps_p", ps)])
                P.act(e[:, m, :], ps_s[ps][:], AF.Exp, [("ps_p", ps), ("bias_c", r)], [("Ec", r % 2, m)],
                      bias=bias_c[:, r, m:m + 1])
                P.mm(ps_o[po][:], vc_tok[:, m, :], e[:, m, :], mi == 0, mi == len(mlist) - 1,
                     ["vc_tok", ("Ec", r % 2, m)], [("ps_o", po)])
            if stop <= 3.3:
                continue
            gate_w(r, 0, po)
            P.tt("dve", oacc[:, r, :], ps_o[po][0:64, :], wgt[:], ALU.mult, [("ps_o", po), "wgt"], [("oacc", r)])
            if stop <= 3.6:
                continue
            dbg_u = int(os.environ.get('DBG_U', '0'))
            for qs in range(4):
                ub = ps_u[qs // 2][:, (qs % 2) * OVS:(qs % 2) * OVS + OVS]
                for mi, m in enumerate(mlist):
                    if dbg_u == 2:
                        continue
                    P.mm(ub, e[:, m, qs * 128:(qs + 1) * 128], cb[:, B2_OV + m * OVS:B2_OV + m * OVS + OVS],
                         mi == 0, mi == len(mlist) - 1, [("Ec", r % 2, m), "cb"], [("ps_u", qs)])
                if dbg_u == 1:
                    continue
                P.ts("dve", rc1[:], ub[:, 128:129], 1e-30, None, ALU.add, None, [("ps_u", qs)], ["rc1"])
                P.recip(rc1[:], rc1[:], ["rc1"], ["rc1"])
                if r == 0:
                    P.ts("dve", pacc[:, qs, :], ub[:, 0:128], rc1[:, 0:1], None, ALU.mult, None,
                         [("ps_u", qs), "rc1"], [("pacc", qs)])
                else:
                    P.stt(pacc[:, qs, :], ub[:, 0:128], rc1[:, 0:1], pacc[:, qs, :], ALU.mult, ALU.add,
                          [("ps_u", qs), "rc1", ("pacc", qs)], [("pacc", qs)])
        if stop <= 4:
            continue
        for qs in range(4):
            off = 8 * n + 2 * qs
            aw = cst[:, C2_AW + 127 - off:C2_AW + 255 - off]
            dw = cst[:, C2_DW + 127 - off:C2_DW + 255 - off]
            P.tt("dve", sc[:], pacc[:, qs, :], aw, ALU.mult, [("pacc", qs), "cst"], ["sc"])
            P.tt("dve", sc[:], sc[:], dw, ALU.add, ["sc", "cst"], ["sc"])
            P.memset("dve", sc[:, 0:1], 1e9, ["sc"])
            P.op("dve", lambda h, o=m8[:, 0:8], i_=sc[:]: h.max(o, i_), ["sc"], ["m8"])
            P.op("dve", lambda h, o=sc2[:], a=m8[:, 0:8], v=sc[:]: h.match_replace(o, a, v, -3.0e38), ["sc", "m8"], ["sc2"])
            P.op("dve", lambda h, o=m8[:, 8:16], i_=sc2[:]: h.max(o, i_), ["sc2"], ["m8"])
            P.ts("dve", sc2[:], sc[:], m8[:, 15:16], None, ALU.is_ge, None, ["sc", "m8"], ["sc2"])
            P.ts("dve", nm[:], sc2[:], -1.0, -NEGM, ALU.add, ALU.mult, ["sc2"], ["nm"])
            P.transpose(ps_u[qs // 2][:, 264:392].bitcast(BF16)[:, (qs % 2) * 128:(qs % 2) * 128 + 128], nm[:], ident,
                        ["nm", "cb"], [("ps_nm", qs)])
        for hb_ in range(2):
            P.copy("act", negmT[:, hb_ * 256:(hb_ + 1) * 256], ps_u[hb_][:, 264:392].bitcast(BF16),
                   [("ps_nm", 2 * hb_), ("ps_nm", 2 * hb_ + 1)], [("negmT", hb_)])
        if stop <= 5:
            continue
        for br in (2, 1):
            if br == 2:
                klist = [kc for kc in ([4 * n + a for a in range(4)] + [4 * n - 4 + a for a in range(4)]) if kc >= 0]
                kx, vt, kkey, vkey, vones = kw_x, vw_tok, "kw_x", "vw_tok", "vw_ones"
            else:
                klist = list(range(4 * n + 4))
                kx, vt, kkey, vkey, vones = ks_x, vs_tok, "ks_x", "vs_tok", "vs_ones"
            for rp in range(0, NR, 2):
                heads = (rp, rp + 1)
                pend = None

                def emit_pv(pp):
                    ki_, kc_, cs_, ebs_ = pp
                    for r in heads:
                        po = r % 2
                        P.mm(ps_o[po][:, cs_], vt[:, kc_, :], Eb[ebs_[r]][:, cs_], ki_ == 0, ki_ == len(klist) - 1,
                             [(vkey, kc_ // 4), vones, ("Eb", ebs_[r])], [("ps_o", po)])

                for ki, kc in enumerate(klist):
                    if kc >= 4 * n:
                        a = kc - 4 * n
                        c_lo, c_hi = 128 * a, 512
                        dmask = (tri_u, c_lo)
                    else:
                        a = kc - (4 * n - 4)
                        if br == 2:
                            c_lo, c_hi = 0, 128 * a + 128
                            dmask = (tri_l, 128 * a)
                        else:
                            c_lo, c_hi = 0, 512
                            dmask = None
                    cs = slice(c_lo, c_hi)
                    ebs = {}
                    for r in heads:
                        ps = nextp()
                        eb = ebi[0] % NEB
                        ebi[0] += 1
                        ebs[r] = eb
                        P.mm(ps_s[ps][:, cs], kx[:, kc * 128:(kc + 1) * 128], q_x[r][:, cs], True, False,
                             [(kkey, kc // 4), kkey[:2] + "_b", ("q_x", r), ("q_b", r)], [("ps_p", ps)])
                        if br == 1:
                            P.mm(ps_s[ps][:, cs], kz[:, kc * 128:(kc + 1) * 128], negmT[:, cs], False, dmask is None,
                                 kzall + [("negmT", 0), ("negmT", 1)], [("ps_p", ps)])
                        if dmask is not None:
                            P.mm(ps_s[ps][:, dmask[1]:dmask[1] + 128], ident, dmask[0], False, True, ["cb"], [("ps_p", ps)])
                        P.act(Eb[eb][:, cs], ps_s[ps][:, cs], AF.Exp, [("ps_p", ps), ("bias_t", r)], [("Eb", eb)],
                              bias=bias_t[:, r, kc:kc + 1])
                    for _ in range(DUM_WIN if br == 2 else DUM_SLC):
                        P.mm(ps_u[0][:, :], ident, cb[:, B2_CM:B2_CM + 512], True, True, ["cb"], ["ps_junk"])
                    if pend is not None:
                        emit_pv(pend)
                    pend = (ki, kc, cs, ebs)
                emit_pv(pend)
                for r in heads:
                    po = r % 2
                    gate_w(r, br, po)
                    P.tt("dve", tmpo[:], ps_o[po][0:64, :], wgt[:], ALU.mult, [("ps_o", po), "wgt"], ["tmpo"])
                    P.tt("dve", oacc[:, r, :], oacc[:, r, :], tmpo[:], ALU.add, [("oacc", r), "tmpo"], [("oacc", r)])
        ob = n % 2
        for r in range(NR):
            P.copy("act", obf[ob][:, r, :], oacc[:, r, :], [("oacc", r)], [("obf", ob, r)])
        P.dma("sp", o_dst1(n), obf[ob][:], [("obf", ob, r) for r in range(NR)], [("oT_out", n)])
        if "after_tile" in dr:
            dr["after_tile"](n)


def build_mix1(ntok, slopes, stop=99):
    nc = bass.Bass("TRN2", target_bir_lowering=False)
    dr = {}
    dr["hnT"] = nc.dram_tensor("hnT", [D, ntok], BF16, kind="ExternalInput").ap()
    dr["w_in"] = nc.dram_tensor("w_in", [D, WIN_IN], F32, kind="ExternalInput").ap()
    dr["cst"] = nc.dram_tensor("cst", [128, C2_N], F32, kind="ExternalInput").ap()
    dr["cb"] = nc.dram_tensor("cb", [128, B2_N], F32, kind="ExternalInput").ap()
    dr["w1kv"] = nc.dram_tensor("w1kv", [128, 2048], F32, kind="ExternalInput").ap()
    dr["poskv"] = nc.dram_tensor("poskv", [128, 32], F32, kind="ExternalInput").ap()
    dr["w2kv"] = nc.dram_tensor("w2kv", [64, 128], F32, kind="ExternalInput").ap()
    dr["qbias"] = nc.dram_tensor("qbias", [3, NR, 512], F32, kind="ExternalInput").ap()
    dr["zexp"] = nc.dram_tensor("zexp", [128, ntok], F32, kind="ExternalInput").ap()
    dr["oT_out"] = nc.dram_tensor("oT_out", [512, ntok], BF16, kind="ExternalOutput").ap()
    P = Prog(nc)
    emit_mix1(P, nc, ntok, dr, slopes, stop)
    P.emit()
    return nc


def _bf(x):
    return np.asarray(x, np.float32).astype(ml_dtypes.bfloat16).astype(np.float32)


def mix1_slopes(g):
    h = np.arange(8 * g, 8 * g + 8, dtype=np.float32)
    return np.exp2(np.float32(-8.0) * (h + np.float32(1.0)) / np.float32(16.0)).astype(np.float32)


def mix1_consts(g, ntok):
    slopes = mix1_slopes(g)
    cst = np.zeros((128, C2_N), np.float32)
    ki = np.arange(128)
    ncc = ntok // 2048
    for m in range(ncc):
        cst[:, C2_BC + m] = 16.0 * (128 * m + ki) + 31.0
    for kc in range(ntok // 128):
        cst[:, C2_BT + kc] = 128.0 * kc + ki
    hb = (ki // 64)[:, None]
    jj = (np.arange(255) - 127)[None, :]
    allowed = jj <= hb
    forced = (jj == hb) | (jj == hb - 1)
    cst[:, C2_AW:C2_AW + 255] = allowed.astype(np.float32)
    cst[:, C2_DW:C2_DW + 255] = np.where(forced, np.float32(1e9), np.where(allowed, np.float32(0.0), np.float32(-1e30)))
    cst[:, C2_SL:C2_SL + 8] = slopes[None, :]
    cst[:, C2_NT:C2_NT + 16] = -512.0 * np.arange(16)[None, :]
    cb = np.zeros((128, B2_N), np.float32)
    cb[:, B2_ID:B2_ID + 128] = np.eye(128)
    k = ki[:, None]
    q = np.arange(128)[None, :]
    cb[:, B2_TU:B2_TU + 128] = np.where(q >= k, 0.0, NEGM)
    cb[:, B2_TL:B2_TL + 128] = np.where(q < k, 0.0, NEGM)
    qi = np.arange(512)[None, :]
    for o in range(5):
        cb[:, B2_CM + o * 512:B2_CM + (o + 1) * 512] = np.where(512 * o + qi >= 16 * k + 31, 0.0, NEGM)
    ncb = ntok // 16 - 1
    ns = ntok // 64
    for m in range(ncc):
        c = 128 * m + ki
        c0 = c * 16
        for j in range(128):
            if j >= ns:
                continue
            lo = np.maximum(c0, j * 64)
            hi = np.minimum(c0 + 32, j * 64 + 64)
            cb[:, B2_OV + m * OVS + j] = np.where(c < ncb, np.maximum(hi - lo, 0) / 32.0, 0.0)
        cb[:, B2_OV + m * OVS + 128] = (c < ncb).astype(np.float32)
    for idx in range(24):
        cb[idx, B2_SEL + idx * 64:B2_SEL + (idx + 1) * 64] = 1.0
        cb[32 + idx, B2_SEL + idx * 64:B2_SEL + (idx + 1) * 64] = 1.0
    cb = _bf(cb)
    i = np.arange(512, dtype=np.float64)
    qb = np.zeros((3, NR, 512), np.float32)
    for r in range(NR):
        a = -np.float64(slopes[r]) * i
        a1 = _bf(a)
        a2 = _bf(a - a1)
        a3 = _bf(a - a1 - a2)
        qb[0, r], qb[1, r], qb[2, r] = a1, a2, a3
    z = (np.arange(ntok)[None, :] // 64 == np.arange(128)[:, None]).astype(np.float32)
    return {"cst": cst, "cb": cb, "qbias": qb, "zexp": z}, slopes


def mix1_weights(w_in, cpk, cpv, w1k, w2k, w1v, w2v, g):
    cols = [w_in[:, 512 * g:512 * g + 512]]
    for base in (1024, 1152, 1280, 1536, 1408, 1664):
        cols.append(w_in[:, base + 64 * g: base + 64 * g + 64])
    cols.append(w_in[:, 1792 + 24 * g:1792 + 24 * g + 24])
    wc = np.ascontiguousarray(np.concatenate(cols, axis=1))
    w1 = np.concatenate([w1k.reshape(32, 64, 64).transpose(1, 0, 2).reshape(64, 2048),
                         w1v.reshape(32, 64, 64).transpose(1, 0, 2).reshape(64, 2048)], axis=0)
    pos = np.concatenate([cpk.T, cpv.T], axis=0)
    w2 = np.concatenate([w2k, w2v], axis=1)
    return {"w_in": wc, "w1kv": np.ascontiguousarray(w1), "poskv": np.ascontiguousarray(pos), "w2kv": np.ascontiguousarray(w2)}


_PROGS = {}


def _prog(key, fn):
    if key not in _PROGS:
        _PROGS[key] = fn()
    return _PROGS[key]


def _mix0_inputs(xT, w_in, lbs, hno, rno, g_mix, hh):
    hsel = [2 * hh, 2 * hh + 1]
    col = lambda grp, h: w_in[:, grp * 512 + h * 128: grp * 512 + (h + 1) * 128]
    Q = [col(0, h) for h in hsel] + [col(4, h) for h in hsel]
    K = [col(1, h) for h in hsel] + [col(5, h) for h in hsel]
    G = [col(3, h) for h in hsel] + [col(7, h) for h in hsel]
    V = [col(2, h) for h in hsel] + [col(6, h) for h in hsel]
    wc = np.ascontiguousarray(np.concatenate(Q + K + G + V, axis=1))
    lbp = np.stack([lbs[0, hsel[0] * 128:(hsel[0] + 1) * 128], lbs[0, hsel[1] * 128:(hsel[1] + 1) * 128],
                    lbs[1, hsel[0] * 128:(hsel[0] + 1) * 128], lbs[1, hsel[1] * 128:(hsel[1] + 1) * 128]], axis=1)
    return {"xT": xT, "w_in": wc, "g_mix": gvec(g_mix), "cst": mix0_consts(hh), "ident": np.eye(128, dtype=np.float32),
            "lbp": np.ascontiguousarray(lbp.astype(np.float32)),
            "wn": np.ascontiguousarray(np.stack([hno, rno], axis=1).astype(np.float32))}


def kernel_unfused(x, mix_norm, ffn_norm, final_norm, even_w_in, hgrn_lower_bounds, hgrn_out_norm,
           ret_out_norm, even_w_out, odd_w_in, cmp_pos_k, cmp_pos_v, cmp_w1_k, cmp_w2_k,
           cmp_w1_v, cmp_w2_v, odd_w_out, ffn_w_gate_up, ffn_w_down):
    f = lambda a: np.asarray(a, np.float32)
    x = f(x)
    cores = list(range(8))
    xT = [np.ascontiguousarray(x[b].T) for b in range(B)]
    TH = T // 2
    ncA = _prog("mix0", lambda: build_mix0(T))
    inA = [_mix0_inputs(xT[c // 2], f(even_w_in)[0], f(hgrn_lower_bounds), f(hgrn_out_norm)[0], f(ret_out_norm)[0],
                        f(mix_norm)[0], c % 2) for c in cores]
    rA = run_bass_kernel_spmd(ncA, inA, core_ids=cores).results
    o0T = []
    for b in range(B):
        full = np.empty((D, T), ml_dtypes.bfloat16)
        for hh in range(2):
            o = rA[2 * b + hh]["oT_out"]
            full[256 * hh:256 * hh + 256] = o[0:256]
            full[512 + 256 * hh:512 + 256 * hh + 256] = o[256:512]
        o0T.append(full)
    ncB = _prog("ffn_mid", lambda: build_ffn(TH, "mid"))
    inB = []
    for c in cores:
        b, s = c // 2, c % 2
        inB.append({"oT": np.ascontiguousarray(o0T[b][:, s * TH:(s + 1) * TH]),
                    "xT": np.ascontiguousarray(xT[b][:, s * TH:(s + 1) * TH]),
                    "w_out": f(even_w_out)[0], "g_ffn": gvec(f(ffn_norm)[0]), "g_nxt": gvec(f(mix_norm)[1]),
                    "w_gu": f(ffn_w_gate_up)[0], "w_dn": f(ffn_w_down)[0]})
    rB = run_bass_kernel_spmd(ncB, inB, core_ids=cores).results
    inC = []
    slopes = None
    for c in cores:
        b, g = c // 2, c % 2
        consts, sl = mix1_consts(g, T)
        d = dict(consts)
        d.update(mix1_weights(f(odd_w_in)[0], f(cmp_pos_k)[0], f(cmp_pos_v)[0], f(cmp_w1_k)[0], f(cmp_w2_k)[0],
                              f(cmp_w1_v)[0], f(cmp_w2_v)[0], g))
        d["hnT"] = np.ascontiguousarray(np.concatenate([rB[2 * b]["out_n"], rB[2 * b + 1]["out_n"]], axis=1))
        inC.append(d)
    ncC = _prog("mix1", lambda: build_mix1(T, None))
    rC = run_bass_kernel_spmd(ncC, inC, core_ids=cores).results
    ncD = _prog("ffn_fin", lambda: build_ffn(TH, "fin"))
    inD = []
    for c in cores:
        b, s = c // 2, c % 2
        o1 = np.concatenate([rC[2 * b]["oT_out"][:, s * TH:(s + 1) * TH], rC[2 * b + 1]["oT_out"][:, s * TH:(s + 1) * TH]], axis=0)
        inD.append({"oT": np.ascontiguousarray(o1), "xT": rB[c]["out_h"],
                    "w_out": f(odd_w_out)[0], "g_ffn": gvec(f(ffn_norm)[1]), "g_nxt": gvec(f(final_norm)),
                    "w_gu": f(ffn_w_gate_up)[1], "w_dn": f(ffn_w_down)[1]})
    rD = run_bass_kernel_spmd(ncD, inD, core_ids=cores).results
    out = np.empty((B, T, D), np.float32)
    for c in cores:
        b, s = c // 2, c % 2
        out[b, s * TH:(s + 1) * TH, :] = rD[c]["out_n"].T
    return out


PAIRS = [[0, 1], [2, 3], [4, 5], [6, 7]]
TH = T // 2


def build_fused(nph=4):
    nc = bass.Bass("TRN2", target_bir_lowering=False)
    ext = lambda name, shape, dt=F32: nc.dram_tensor(name, list(shape), dt, kind="ExternalInput").ap()
    P = Prog(nc)
    NCH = 8
    o0c = [nc.dram_tensor("o0c%d" % k, [512, 1024], BF16) for k in range(NCH)]
    g0c = [nc.dram_tensor("g0c%d" % k, [1024, 1024], BF16) for k in range(NCH)]
    d0 = {"xT": ext("xT", [D, T]), "w_in": ext("w_in0", [D, 2048]), "g_mix": ext("g_mix0", [128, NKC]),
          "cst": ext("cst0", [128, C1_N]), "ident": ext("ident", [128, 128]), "lbp": ext("lbp", [128, 4]),
          "wn": ext("wn", [128, 2])}
    o0v = [t.ap().rearrange("(u p) t -> p u t", p=128) for t in o0c]
    d0["o_dst"] = lambda i, u: o0v[i // 2][:, u, (i % 2) * 512:(i % 2) * 512 + 512]

    def after0(i):
        if i % 2 == 1:
            k = i // 2
            P.allgather(g0c[k].ap().opt(), o0c[k].ap().opt(), PAIRS,
                        [("oT_out", ii, u) for ii in (2 * k, 2 * k + 1) for u in range(4)], [("g0c", k)])
    d0["after_tile"] = after0
    emit_mix0(P, nc, T, d0)
    if nph == 1:
        dbg = nc.dram_tensor("dbg", [1024, 1024], BF16, kind="ExternalOutput").ap()
        P.dma("sp", dbg, g0c[7].ap(), [("g0c", 7)], ["dbg"])
        P.end_phase(last=True)
        return nc
    P.end_phase()
    h1 = nc.dram_tensor("h1", [D, TH], F32)
    hn1c = [nc.dram_tensor("hn1c%d" % k, [D, 512], BF16) for k in range(NCH)]
    ghn = [nc.dram_tensor("ghn%d" % k, [2 * D, 512], BF16) for k in range(NCH)]
    d1 = {"xT": ext("xh", [D, TH]), "w_out": ext("w_out0", [D, D]), "g_ffn": ext("g_ffn0", [128, NKC]),
          "g_nxt": ext("g_nxt0", [128, NKC]), "w_gu": ext("w_gu0", [D, 2 * FH]), "w_dn": ext("w_dn0", [FH, D]),
          "sel": ext("sel", [128, 2])}
    g0v = [t.ap().rearrange("(c p) t -> p c t", p=128) for t in g0c]

    def osrc(half):
        def f(i):
            tg = half * TH + i * TT_F
            return g0v[tg // 1024][:, :, tg % 1024:tg % 1024 + TT_F]
        return f
    d1["o_srcs"] = [osrc(0), osrc(1)]
    d1["wo_perm"] = [(2 * (c // 4) + (c % 4)) if (c % 4) < 2 else (4 + 2 * (c // 4) + (c % 4) - 2) for c in range(8)]
    h1v = h1.ap().rearrange("(c p) t -> p c t", p=128)
    d1["h_dst"] = lambda i, m: h1v[:, m, i * TT_F:(i + 1) * TT_F]
    hn1v = [t.ap().rearrange("(c p) t -> p c t", p=128) for t in hn1c]
    d1["n_dst"] = lambda i, c: hn1v[i // 2][:, c, (i % 2) * TT_F:(i % 2) * TT_F + TT_F]

    def after1(i):
        if i % 2 == 1:
            k = i // 2
            P.allgather(ghn[k].ap().opt(), hn1c[k].ap().opt(), PAIRS,
                        [("out_n", ii, c) for ii in (2 * k, 2 * k + 1) for c in range(NKC)], [("ghn", k)])
    d1["after_tile"] = after1
    emit_ffn(P, nc, TH, "mid", d1)
    if nph == 2:
        dbg = nc.dram_tensor("dbg", [2 * D, 512], BF16, kind="ExternalOutput").ap()
        P.dma("sp", dbg, ghn[7].ap(), [("ghn", 7)], ["dbg"])
        P.end_phase(last=True)
        return nc
    P.end_phase()
    o1c = [nc.dram_tensor("o1c%d" % k, [512, 1024], BF16) for k in range(NCH)]
    g1c = [nc.dram_tensor("g1c%d" % k, [1024, 1024], BF16) for k in range(NCH)]
    d2 = {"w_in": ext("w_in1", [D, WIN_IN]), "cst": ext("cst1", [128, C2_N]), "cb": ext("cb1", [128, B2_N]),
          "w1kv": ext("w1kv", [128, 2048]), "poskv": ext("poskv", [128, 32]), "w2kv": ext("w2kv", [64, 128]),
          "qbias": ext("qbias", [3, NR, 512]), "zexp": ext("zexp", [128, T])}
    ghv = [t.ap().rearrange("(r c p) t -> p r c t", p=128, c=NKC) for t in ghn]
    d2["hn_src"] = lambda i, c: ghv[i % 8][:, i // 8, c, :]
    o1v = [t.ap().rearrange("(r p) t -> p r t", p=64) for t in o1c]
    d2["o_dst"] = lambda n: o1v[n // 2][:, :, (n % 2) * 512:(n % 2) * 512 + 512]

    def after2(n):
        if n % 2 == 1:
            k = n // 2
            P.allgather(g1c[k].ap().opt(), o1c[k].ap().opt(), PAIRS, [("oT_out", 2 * k), ("oT_out", 2 * k + 1)], [("g1c", k)])
    d2["after_tile"] = after2
    emit_mix1(P, nc, T, d2, None)
    if nph == 3:
        dbg = nc.dram_tensor("dbg", [1024, 1024], BF16, kind="ExternalOutput").ap()
        P.dma("sp", dbg, g1c[7].ap(), [("g1c", 7)], ["dbg"])
        P.end_phase(last=True)
        return nc
    P.end_phase()
    d3 = {"xT": h1.ap(), "w_out": ext("w_out1", [D, D]), "g_ffn": ext("g_ffn1", [128, NKC]),
          "g_nxt": ext("g_fin", [128, NKC]), "w_gu": ext("w_gu1", [D, 2 * FH]), "w_dn": ext("w_dn1", [FH, D]),
          "sel": d1["sel"]}
    g1v = [t.ap().rearrange("(c p) t -> p c t", p=128) for t in g1c]

    def osrc1(half):
        def f(i):
            tg = half * TH + i * TT_F
            return g1v[tg // 1024][:, :, tg % 1024:tg % 1024 + TT_F]
        return f
    d3["o_srcs"] = [osrc1(0), osrc1(1)]
    d3["out_n"] = nc.dram_tensor("out_n", [D, TH], F32, kind="ExternalOutput").ap()
    emit_ffn(P, nc, TH, "fin", d3)
    P.end_phase(last=True)
    return nc


def kernel(x, mix_norm, ffn_norm, final_norm, even_w_in, hgrn_lower_bounds, hgrn_out_norm,
           ret_out_norm, even_w_out, odd_w_in, cmp_pos_k, cmp_pos_v, cmp_w1_k, cmp_w2_k,
           cmp_w1_v, cmp_w2_v, odd_w_out, ffn_w_gate_up, ffn_w_down):
    f = lambda a: np.asarray(a, np.float32)
    x = f(x)
    cores = list(range(8))
    xT = [np.ascontiguousarray(x[b].T) for b in range(B)]
    nc = _prog("fused", build_fused)
    ins = _fused_inputs(x, xT, mix_norm, ffn_norm, final_norm, even_w_in, hgrn_lower_bounds, hgrn_out_norm,
                        ret_out_norm, even_w_out, odd_w_in, cmp_pos_k, cmp_pos_v, cmp_w1_k, cmp_w2_k,
                        cmp_w1_v, cmp_w2_v, odd_w_out, ffn_w_gate_up, ffn_w_down)
    res = run_bass_kernel_spmd(nc, ins, core_ids=cores).results
    out = np.empty((B, T, D), np.float32)
    for c in cores:
        b, j = c // 2, c % 2
        out[b, j * TH:(j + 1) * TH, :] = res[c]["out_n"].T
    return out


def _fused_inputs(x, xT, mix_norm, ffn_norm, final_norm, even_w_in, hgrn_lower_bounds, hgrn_out_norm,
                  ret_out_norm, even_w_out, odd_w_in, cmp_pos_k, cmp_pos_v, cmp_w1_k, cmp_w2_k,
                  cmp_w1_v, cmp_w2_v, odd_w_out, ffn_w_gate_up, ffn_w_down):
    f = lambda a: np.asarray(a, np.float32)
    cores = list(range(8))
    ins = []
    for c in cores:
        b, j = c // 2, c % 2
        m0 = _mix0_inputs(xT[b], f(even_w_in)[0], f(hgrn_lower_bounds), f(hgrn_out_norm)[0], f(ret_out_norm)[0],
                          f(mix_norm)[0], j)
        d = {"xT": m0["xT"], "w_in0": m0["w_in"], "g_mix0": m0["g_mix"], "cst0": m0["cst"], "ident": m0["ident"],
             "lbp": m0["lbp"], "wn": m0["wn"]}
        d.update({"xh": np.ascontiguousarray(xT[b][:, j * TH:(j + 1) * TH]), "w_out0": f(even_w_out)[0],
                  "g_ffn0": gvec(f(ffn_norm)[0]), "g_nxt0": gvec(f(mix_norm)[1]), "w_gu0": f(ffn_w_gate_up)[0],
                  "w_dn0": f(ffn_w_down)[0],
                  "sel": np.ascontiguousarray(np.tile(np.array([[1.0 - j, float(j)]], np.float32), (128, 1)))})
        consts, _ = mix1_consts(j, T)
        mw = mix1_weights(f(odd_w_in)[0], f(cmp_pos_k)[0], f(cmp_pos_v)[0], f(cmp_w1_k)[0], f(cmp_w2_k)[0],
                          f(cmp_w1_v)[0], f(cmp_w2_v)[0], j)
        d.update({"w_in1": mw["w_in"], "cst1": consts["cst"], "cb1": consts["cb"], "w1kv": mw["w1kv"], "poskv": mw["poskv"],
                  "w2kv": mw["w2kv"], "qbias": consts["qbias"], "zexp": consts["zexp"]})
        d.update({"w_out1": f(odd_w_out)[0], "g_ffn1": gvec(f(ffn_norm)[1]), "g_fin": gvec(f(final_norm)),
                  "w_gu1": f(ffn_w_gate_up)[1], "w_dn1": f(ffn_w_down)[1]})
        ins.append(d)
    return ins
```

```python
import numpy as np
from contextlib import ExitStack
import concourse.bass as bass
import concourse.mybir as mybir
from concourse.bass_utils import run_bass_kernel_spmd
import ml_dtypes

F32 = mybir.dt.float32
BF16 = mybir.dt.bfloat16
AF = mybir.ActivationFunctionType
ALU = mybir.AluOpType
AX = mybir.AxisListType

D = 1024
T = 8192
B = 4
FH = 2816
NKC = D // 128
NHC = FH // 128
EPS = 1e-6
TT_F = 256


class _Op:
    __slots__ = ("fn", "waits", "needs_inc", "dma_slot", "semval")

    def __init__(self, fn, waits, dma_slot):
        self.fn = fn
        self.waits = waits
        self.needs_inc = False
        self.dma_slot = dma_slot
        self.semval = 0


class _Slot:
    __slots__ = ("sem", "count", "inc")

    def __init__(self, sem, inc=16):
        self.sem = sem
        self.count = 0
        self.inc = inc


class Prog:
    ENGS = ("pe", "act", "dve", "pool", "sp")

    def __init__(self, nc, n_dma_slots=12):
        self.nc = nc
        self.es = ExitStack()
        self.ops = {e: [] for e in self.ENGS}
        self.known = {e: {} for e in self.ENGS}
        self.sems = {e: self.es.enter_context(nc.semaphore("sem_" + e)) for e in self.ENGS}
        self.slots = {}
        self.slot_rr = {}
        for q in ("sp", "pool", "act"):
            self.slots[q] = [
                _Slot(self.es.enter_context(nc.semaphore("dma_%s_%d" % (q, i))))
                for i in range(n_dma_slots)
            ]
            self.slot_rr[q] = 0
        self.cc_slot = _Slot(self.es.enter_context(nc.semaphore("cc_sem")), 1)
        self.bar = self.es.enter_context(nc.semaphore("phase_bar"))
        self.phase_no = 0
        self.semcount = {e: 0 for e in self.ENGS}
        self.pes = ExitStack()
        self.res_w = {}
        self.res_r = {}
        self.bank_last = {}
        self._names = 0

    def sbuf(self, shape, dtype, name=None):
        self._names += 1
        return self.pes.enter_context(
            self.nc.sbuf_tensor("%s_%d" % (name or "sb", self._names), list(shape), dtype))

    def psum(self, shape, dtype, name=None):
        self._names += 1
        return self.pes.enter_context(
            self.nc.psum_tensor("%s_%d" % (name or "ps", self._names), list(shape), dtype))

    def _need(self, eng, tok, waits):
        if tok[0] == "E":
            _, e2, idx = tok
            if e2 == eng and eng == "pe":
                return
            k = ("E", e2)
            if self.known[eng].get(k, -1) >= idx:
                return
            self.known[eng][k] = idx
            self.ops[e2][idx].needs_inc = True
            waits.append(tok)
        else:
            _, slot, val = tok
            k = ("D", id(slot))
            if self.known[eng].get(k, 0) >= val:
                return
            self.known[eng][k] = val
            waits.append(tok)

    @staticmethod
    def _banks(*aps):
        out = []
        for a in aps:
            try:
                if str(a.space) == "PSUM":
                    out.append(a.name)
            except AttributeError:
                pass
        return out

    def op(self, eng, fn, reads=(), writes=(), dma=False, banks=(), cc=False):
        deps = []
        for bk in banks:
            for e2, t in self.bank_last.get(bk, {}).items():
                if e2 != eng:
                    deps.append(t)
        for r in reads:
            t = self.res_w.get(r)
            if t is not None:
                deps.append(t)
        for w in writes:
            t = self.res_w.get(w)
            if t is not None:
                deps.append(t)
            rr = self.res_r.get(w)
            if rr:
                deps.extend(rr.values())
        waits = []
        for t in deps:
            self._need(eng, t, waits)
        slot = None
        if cc:
            slot = self.cc_slot
            slot.count += 1
            tok = ("D", slot, slot.count)
            rkey = ("D", id(slot))
        elif dma:
            lst = self.slots[eng]
            slot = lst[self.slot_rr[eng] % len(lst)]
            self.slot_rr[eng] += 1
            if slot.count:
                self._need(eng, ("D", slot, slot.count), waits)
            slot.count += 16
            tok = ("D", slot, slot.count)
            rkey = ("D", id(slot))
        else:
            tok = ("E", eng, len(self.ops[eng]))
            rkey = ("E", eng)
        self.ops[eng].append(_Op(fn, waits, slot))
        for bk in banks:
            self.bank_last.setdefault(bk, {})[eng] = tok
        for r in reads:
            self.res_r.setdefault(r, {})[rkey] = tok
        for w in writes:
            self.res_w[w] = tok
            self.res_r[w] = {}
        return tok

    def mm(self, out, lhsT, rhs, start, stop, reads, writes):
        return self.op("pe", lambda h: h.matmul(out, lhsT, rhs, start=start, stop=stop), reads, writes,
                       banks=self._banks(out))

    def transpose(self, out, in_, ident, reads, writes):
        return self.op("pe", lambda h: h.transpose(out, in_, ident), reads, writes, banks=self._banks(out))

    def act(self, out, in_, func, reads, writes, bias=None, scale=None, eng="act"):
        kw = {}
        if bias is not None:
            kw["bias"] = bias
        if scale is not None:
            kw["scale"] = scale
        return self.op(eng, lambda h: h.activation(out, in_, func, **kw), reads, writes, banks=self._banks(out, in_))

    def tt(self, eng, out, in0, in1, op, reads, writes):
        return self.op(eng, lambda h: h.tensor_tensor(out, in0, in1, op), reads, writes, banks=self._banks(out, in0, in1))

    def ts(self, eng, out, in0, s1, s2, op0, op1, reads, writes):
        if s2 is None:
            return self.op(eng, lambda h: h.tensor_scalar(out, in0, s1, None, op0), reads, writes, banks=self._banks(out, in0, s1))
        return self.op(eng, lambda h: h.tensor_scalar(out, in0, s1, s2, op0, op1), reads, writes, banks=self._banks(out, in0, s1, s2))

    def stt(self, out, in0, scalar, in1, op0, op1, reads, writes):
        return self.op("dve", lambda h: h.scalar_tensor_tensor(out, in0, scalar, in1, op0, op1), reads, writes,
                       banks=self._banks(out, in0, scalar, in1))

    def copy(self, eng, out, in_, reads, writes):
        if eng == "act":
            return self.op(eng, lambda h: h.copy(out, in_), reads, writes, banks=self._banks(out, in_))
        return self.op(eng, lambda h: h.tensor_copy(out, in_), reads, writes, banks=self._banks(out, in_))

    def recip(self, out, in_, reads, writes, scratch=None):
        if scratch is not None:
            return self.op("dve", lambda h: h.reciprocal_approx_accurate(out, in_, scratch), reads, writes,
                           banks=self._banks(out, in_))
        return self.op("dve", lambda h: h.reciprocal(out, in_), reads, writes, banks=self._banks(out, in_))

    def memset(self, eng, ap, val, writes):
        return self.op(eng, lambda h: h.memset(ap, val), (), writes)

    def dma(self, q, out, in_, reads, writes):
        return self.op(q, lambda h: h.dma_start(out=out, in_=in_), reads, writes, dma=True)

    def allgather(self, out, in_, groups, reads, writes):
        return self.op("pool", lambda h: h.collective_compute("AllGather", ALU.bypass, replica_groups=groups,
                                                              ins=[in_], outs=[out]), reads, writes, cc=True)

    def end_phase(self, last=False):
        nc = self.nc
        self.phase_no += 1
        final = {}
        for e in self.ENGS:
            comp = [o for o in self.ops[e] if o.dma_slot is None]
            if comp:
                comp[-1].needs_inc = True
            c = self.semcount[e]
            for o in self.ops[e]:
                if o.dma_slot is None and o.needs_inc:
                    c += 1
                    o.semval = c
            self.semcount[e] = c
            final[e] = c if comp else None
        fin = []
        for q in self.slots:
            for s_ in self.slots[q]:
                if s_.count:
                    fin.append((s_.sem, s_.count))
        if self.cc_slot.count:
            fin.append((self.cc_slot.sem, self.cc_slot.count))
        phase_no = self.phase_no

        def run(e, h):
            for o in self.ops[e]:
                for t in o.waits:
                    if t[0] == "E":
                        h.wait_ge(self.sems[t[1]], self.ops[t[1]][t[2]].semval)
                    else:
                        h.wait_ge(t[1].sem, t[2])
                ins = o.fn(h)
                if o.dma_slot is not None:
                    if o.dma_slot.inc == 1:
                        ins.then_inc(o.dma_slot.sem)
                    else:
                        ins.then_inc(o.dma_slot.sem, 16)
                elif o.needs_inc:
                    ins.then_inc(self.sems[e], 1)
            if final[e] is not None:
                h.wait_ge(self.sems[e], final[e])
            if e == "sp":
                for sem, v in fin:
                    h.wait_ge(sem, v)
            if not last:
                h.sem_inc(self.bar, 1)
                h.wait_ge(self.bar, 5 * phase_no)

        with nc.Block() as block:
            @block.tensor
            def _(h):
                run("pe", h)

            @block.scalar
            def _(h):
                run("act", h)

            @block.vector
            def _(h):
                run("dve", h)

            @block.gpsimd
            def _(h):
                run("pool", h)

            @block.sync
            def _(h):
                run("sp", h)
        self.ops = {e: [] for e in self.ENGS}
        for e in self.ENGS:
            self.known[e] = {k: v for k, v in self.known[e].items() if k[0] == "D"}
        self.res_w = {}
        self.res_r = {}
        self.bank_last = {}
        self.pes.close()
        self.pes = ExitStack()
        if last:
            self.es.close()

    def emit(self):
        self.end_phase(last=True)


def emit_ffn(P, nc, ntok, mode, dr):
    TT = TT_F
    ntile = ntok // TT
    w_out = P.sbuf([128, NKC, D], BF16, "w_out")
    w_gu = P.sbuf([128, NKC, 2 * FH], BF16, "w_gu")
    w_dn = P.sbuf([128, NHC, D], BF16, "w_dn")
    g_ffn = P.sbuf([128, NKC], F32, "g_ffn")
    g_nxt = P.sbuf([128, NKC], F32, "g_nxt")
    ones = P.sbuf([128, 128], BF16, "ones")
    P.memset("pool", ones[:], 1.0, ["ones"])
    P.dma("sp", g_ffn[:], dr["g_ffn"], [], ["g_ffn"])
    P.dma("sp", g_nxt[:], dr["g_nxt"], [], ["g_nxt"])
    wo_v = dr["w_out"].rearrange("(c p) n -> p c n", p=128)
    wo_perm = dr.get("wo_perm", list(range(NKC)))
    for c in range(NKC):
        P.dma("pool", w_out[:, c, :], wo_v[:, wo_perm[c], :], [], [("w_out", c)])
    wg_v = dr["w_gu"].rearrange("(c p) n -> p c n", p=128)
    for c in range(NKC):
        for hf in range(2):
            P.dma("pool", w_gu[:, c, hf * FH:(hf + 1) * FH], wg_v[:, c, hf * FH:(hf + 1) * FH], [], [("w_gu", c, hf)])
    wd_v = dr["w_dn"].rearrange("(c p) n -> p c n", p=128)
    for c in range(NHC):
        P.dma("pool", w_dn[:, c, :], wd_v[:, c, :], [], [("w_dn", c)])

    NB = 2
    xt = [P.sbuf([128, NKC, TT], F32, "xt") for _ in range(NB)]
    ot = [P.sbuf([128, NKC, TT], BF16, "ot") for _ in range(NB)]
    hn = P.sbuf([128, NKC, TT], BF16, "hn")
    actb = P.sbuf([128, NHC, TT], BF16, "actb")
    sg = [P.sbuf([128, TT], F32, "sg") for _ in range(2)]
    rstd = P.sbuf([128, TT], F32, "rstd")
    ps_acc = [P.psum([128, TT], F32, "ps_acc") for _ in range(2)]
    ps_g = [P.psum([128, TT], F32, "ps_g") for _ in range(2)]
    ps_u = [P.psum([128, TT], F32, "ps_u") for _ in range(2)]
    ps_s = P.psum([128, TT], F32, "ps_s")

    xT_v = dr["xT"].rearrange("(c p) t -> p c t", p=128)
    if "o_srcs" in dr:
        o_srcs = dr["o_srcs"]
    else:
        oT_v = dr["oT"].rearrange("(c p) t -> p c t", p=128)
        o_srcs = [lambda i: oT_v[:, :, i * TT:(i + 1) * TT]]
    if mode == "mid":
        if "h_dst" in dr:
            h_dst = dr["h_dst"]
        else:
            oh_v = dr["out_h"].rearrange("(c p) t -> p c t", p=128)
            h_dst = lambda i, m: oh_v[:, m, i * TT:(i + 1) * TT]
    if "n_dst" in dr:
        n_dst = dr["n_dst"]
    else:
        on_v = dr["out_n"].rearrange("(c p) t -> p c t", p=128)
        n_dst = lambda i, c: on_v[:, c, i * TT:(i + 1) * TT]
    if len(o_srcs) == 2:
        selv = P.sbuf([128, 2], F32, "selv")
        otb = P.sbuf([128, NKC, TT], BF16, "otb")
        P.dma("sp", selv[:], dr["sel"], [], ["selv"])

    def load(i):
        b = i % NB
        sl = slice(i * TT, (i + 1) * TT)
        for c in range(NKC):
            P.dma("sp", xt[b][:, c, :], xT_v[:, c, sl], [], [("xt", b, c)])
        P.dma("sp", ot[b][:], o_srcs[0](i), [], [("ot", b)])
        if len(o_srcs) == 2:
            P.dma("sp", otb[:], o_srcs[1](i), [], ["otb"])
            P.ts("dve", otb[:], otb[:], selv[:, 1:2], None, ALU.mult, None, ["otb", "selv"], ["otb"])
            for c in range(NKC):
                P.stt(ot[b][:, c, :], ot[b][:, c, :], selv[:, 0:1], otb[:, c, :], ALU.mult, ALU.add,
                      [("ot", b), "otb", "selv"], [("ot", b)])

    acc_i = [0]
    gu_i = [0]

    def rms(b, gam, gkey, dst, dkey):
        for c in range(NKC):
            P.act(actb[:, c, :], xt[b][:, c, :], AF.Square, [("xt", b, c)], [("actb", c)])
        for c in range(NKC):
            P.mm(ps_s[:], ones[:], actb[:, c, :], c == 0, c == NKC - 1, ["ones", ("actb", c)], ["ps_s"])
        P.act(rstd[:], ps_s[:], AF.Sqrt, ["ps_s"], ["rstd"], bias=EPS, scale=1.0 / D)
        P.recip(rstd[:], rstd[:], ["rstd"], ["rstd"])
        for c in range(NKC):
            P.stt(dst(c), xt[b][:, c, :], gam[:, c:c + 1], rstd[:], ALU.mult, ALU.mult,
                  [("xt", b, c), "rstd", gkey], [dkey(c)])

    load(0)
    for i in range(ntile):
        b = i % NB
        sl = slice(i * TT, (i + 1) * TT)
        if i + 1 < ntile:
            load(i + 1)
        for m in range(NKC):
            pa = acc_i[0] % 2
            acc_i[0] += 1
            for k in range(NKC):
                P.mm(ps_acc[pa][:], w_out[:, k, m * 128:(m + 1) * 128], ot[b][:, k, :], k == 0, k == NKC - 1,
                     [("w_out", k), ("ot", b)], [("ps_acc", pa)])
            P.tt("dve", xt[b][:, m, :], xt[b][:, m, :], ps_acc[pa][:], ALU.add,
                 [("xt", b, m), ("ps_acc", pa)], [("xt", b, m)])
        rms(b, g_ffn, "g_ffn", lambda c: hn[:, c, :], lambda c: ("hn", c))
        for j in range(NHC):
            pg = gu_i[0] % 2
            gu_i[0] += 1
            for k in range(NKC):
                P.mm(ps_g[pg][:], w_gu[:, k, j * 128:(j + 1) * 128], hn[:, k, :], k == 0, k == NKC - 1,
                     [("w_gu", k, 0), ("hn", k)], [("ps_g", pg)])
            for k in range(NKC):
                P.mm(ps_u[pg][:], w_gu[:, k, FH + j * 128:FH + (j + 1) * 128], hn[:, k, :], k == 0, k == NKC - 1,
                     [("w_gu", k, 1), ("hn", k)], [("ps_u", pg)])
            P.act(sg[pg][:], ps_g[pg][:], AF.Silu, [("ps_g", pg)], [("sg", pg)])
            P.tt("dve", actb[:, j, :], sg[pg][:], ps_u[pg][:], ALU.mult, [("sg", pg), ("ps_u", pg)], [("actb", j)])
        for m in range(NKC):
            pa = acc_i[0] % 2
            acc_i[0] += 1
            for j in range(NHC):
                P.mm(ps_acc[pa][:], w_dn[:, j, m * 128:(m + 1) * 128], actb[:, j, :], j == 0, j == NHC - 1,
                     [("w_dn", j), ("actb", j)], [("ps_acc", pa)])
            P.tt("dve", xt[b][:, m, :], xt[b][:, m, :], ps_acc[pa][:], ALU.add,
                 [("xt", b, m), ("ps_acc", pa)], [("xt", b, m)])
            if mode == "mid":
                P.dma("sp", h_dst(i, m), xt[b][:, m, :], [("xt", b, m)], [("out_h", i, m)])
        if mode == "mid":
            rms(b, g_nxt, "g_nxt", lambda c: hn[:, c, :], lambda c: ("hn", c))
            for c in range(NKC):
                P.dma("sp", n_dst(i, c), hn[:, c, :], [("hn", c)], [("out_n", i, c)])
            if "after_tile" in dr:
                dr["after_tile"](i)
        else:
            rms(b, g_nxt, "g_nxt", lambda c: xt[b][:, c, :], lambda c: ("xt", b, c))
            for c in range(NKC):
                P.dma("sp", n_dst(i, c), xt[b][:, c, :], [("xt", b, c)], [("out_n", i, c)])


def build_ffn(ntok, mode):
    nc = bass.Bass("TRN2", target_bir_lowering=False)
    dr = {}
    dr["oT"] = nc.dram_tensor("oT", [D, ntok], BF16, kind="ExternalInput").ap()
    dr["xT"] = nc.dram_tensor("xT", [D, ntok], F32, kind="ExternalInput").ap()
    dr["w_out"] = nc.dram_tensor("w_out", [D, D], F32, kind="ExternalInput").ap()
    dr["g_ffn"] = nc.dram_tensor("g_ffn", [128, NKC], F32, kind="ExternalInput").ap()
    dr["g_nxt"] = nc.dram_tensor("g_nxt", [128, NKC], F32, kind="ExternalInput").ap()
    dr["w_gu"] = nc.dram_tensor("w_gu", [D, 2 * FH], F32, kind="ExternalInput").ap()
    dr["w_dn"] = nc.dram_tensor("w_dn", [FH, D], F32, kind="ExternalInput").ap()
    if mode == "mid":
        dr["out_h"] = nc.dram_tensor("out_h", [D, ntok], F32, kind="ExternalOutput").ap()
        dr["out_n"] = nc.dram_tensor("out_n", [D, ntok], BF16, kind="ExternalOutput").ap()
    else:
        dr["out_n"] = nc.dram_tensor("out_n", [D, ntok], F32, kind="ExternalOutput").ap()
    P = Prog(nc)
    emit_ffn(P, nc, ntok, mode, dr)
    P.emit()
    return nc


def gvec(g):
    return np.ascontiguousarray(np.asarray(g, np.float32).reshape(NKC, 128).T)


TT1 = 512
LC = 64
C1_TRI = 0
C1_RST = 256
C1_DQ = 768
C1_EXL = 768 + 6 * 512
C1_N = C1_EXL + 16


def emit_mix0(P, nc, ntok, dr):
    ntile = ntok // TT1
    w_in = P.sbuf([128, NKC, 2048], BF16, "w_in")
    g_mix = P.sbuf([128, NKC], F32, "g_mix")
    cst = P.sbuf([128, C1_N], F32, "cst")
    ident = P.sbuf([128, 128], BF16, "ident")
    ones = P.sbuf([128, 128], BF16, "ones")
    lbp = P.sbuf([128, 4], F32, "lbp")
    lb = P.sbuf([128, 2], F32, "lb")
    oml = P.sbuf([128, 2], F32, "oml")
    wn = P.sbuf([128, 2], F32, "wn")
    P.memset("pool", ones[:], 1.0, ["ones"])
    P.dma("sp", g_mix[:], dr["g_mix"], [], ["g_mix"])
    P.dma("sp", cst[:], dr["cst"], [], ["cst"])
    P.dma("sp", lbp[:], dr["lbp"], [], ["lbp"])
    P.dma("sp", wn[:], dr["wn"], [], ["wn"])
    P.dma("pool", ident[:], dr["ident"], [], ["ident"])
    wi_v = dr["w_in"].rearrange("(c p) n -> p c n", p=128)
    for c in range(NKC):
        P.dma("pool", w_in[:, c, :], wi_v[:, c, :], [], [("w_in", c)])
    P.tt("dve", lb[:], lbp[:, 0:2], lbp[:, 2:4], ALU.subtract, ["lbp"], ["lb"])
    P.act(lb[:], lb[:], AF.Sigmoid, ["lb"], ["lb"])
    P.ts("dve", oml[:], lb[:], -1.0, 1.0, ALU.mult, ALU.add, ["lb"], ["oml"])

    tri4 = cst[0:64, C1_TRI:C1_TRI + 256]
    rst = cst[:, C1_RST:C1_RST + 512]

    def dtab(kind, r):
        o = C1_DQ + (kind * 2 + r) * 512
        return cst[:, o:o + 512]

    xt = [P.sbuf([128, NKC, TT1], F32, "xt") for _ in range(2)]
    hn = P.sbuf([128, NKC, TT1], BF16, "hn")
    sq = P.sbuf([128, NKC, TT1], BF16, "sq")
    rstd = P.sbuf([128, TT1], F32, "rstd")
    t_a = P.sbuf([128, TT1], F32, "t_a")
    t_f = P.sbuf([128, TT1], F32, "t_f")
    t_g = P.sbuf([128, TT1], F32, "t_g")
    t_c = P.sbuf([128, TT1], F32, "t_c")
    t_n = P.sbuf([128, TT1], F32, "t_n")
    t_k = P.sbuf([128, TT1], F32, "t_k")
    ecum = [P.sbuf([128, TT1], F32, "ecum") for _ in range(2)]
    qe = [P.sbuf([128, TT1], BF16, "qe") for _ in range(4)]
    ke = [P.sbuf([128, TT1], BF16, "ke") for _ in range(4)]
    kl = [P.sbuf([128, TT1], BF16, "kl") for _ in range(4)]
    gs = [P.sbuf([128, TT1], F32, "gs") for _ in range(4)]
    v_tok = P.sbuf([128, 4, 512], BF16, "v_tok")
    kl_tok = P.sbuf([128, 4, 4, 128], BF16, "kl_tok")
    atm = P.sbuf([128, 4, 64], BF16, "atm")
    S = P.sbuf([128, 4, 128], F32, "S")
    S_bf = [P.sbuf([128, 4, 128], BF16, "S_bf") for _ in range(2)]
    osq = P.sbuf([128, TT1], BF16, "osq")
    ors = P.sbuf([128, TT1], F32, "ors")
    otmp = P.sbuf([128, TT1], F32, "otmp")
    of = [P.sbuf([128, 4, TT1], BF16, "of") for _ in range(2)]

    ps_in = [P.psum([128, TT1], F32, "ps_in") for _ in range(2)]
    ps_o = [P.psum([128, TT1], F32, "ps_o") for _ in range(4)]
    ps_ds = P.psum([128, 4, 128], F32, "ps_ds")
    ps_misc = P.psum([128, 512], F32, "ps_misc")
    ps_at = ps_misc[0:64, 0:256].rearrange("p (u c) -> p u c", c=LC)
    ps_kt = ps_misc[:, 256:512].bitcast(BF16).rearrange("p (s d) -> p s d", d=128)

    P.memset("dve", S[:], 0.0, ["S"])
    P.memset("pool", S_bf[0][:], 0.0, [("S_bf", 0)])

    xT_v = dr["xT"].rearrange("(c p) t -> p c t", p=128)
    if "o_dst" in dr:
        o_dst = dr["o_dst"]
    else:
        o_v = dr["oT_out"].rearrange("(u p) t -> p u t", p=128)
        o_dst = lambda i, u: o_v[:, u, i * TT1:(i + 1) * TT1]

    def load(i):
        b = i % 2
        sl = slice(i * TT1, (i + 1) * TT1)
        for c in range(NKC):
            P.dma("sp", xt[b][:, c, :], xT_v[:, c, sl], [], [("xt", b, c)])

    pin = [0]

    def proj_fm(blk):
        p = pin[0] % 2
        pin[0] += 1
        for k in range(NKC):
            P.mm(ps_in[p][:], w_in[:, k, blk * 128:(blk + 1) * 128], hn[:, k, :], k == 0, k == NKC - 1,
                 [("w_in", k)] + [("hn", k)], [("ps_in", p)])
        return p

    sbi = [0]
    load(0)
    for i in range(ntile):
        b = i % 2
        sl = slice(i * TT1, (i + 1) * TT1)
        if i + 1 < ntile:
            load(i + 1)
        for c in range(NKC):
            P.act(sq[:, c, :], xt[b][:, c, :], AF.Square, [("xt", b, c)], [("sq", c)])
        p = pin[0] % 2
        pin[0] += 1
        for c in range(NKC):
            P.mm(ps_in[p][:], ones[:], sq[:, c, :], c == 0, c == NKC - 1, ["ones", ("sq", c)], [("ps_in", p)])
        P.act(rstd[:], ps_in[p][:], AF.Sqrt, [("ps_in", p)], ["rstd"], bias=EPS, scale=1.0 / D)
        P.recip(rstd[:], rstd[:], ["rstd"], ["rstd"])
        for c in range(NKC):
            P.stt(hn[:, c, :], xt[b][:, c, :], g_mix[:, c:c + 1], rstd[:], ALU.mult, ALU.mult,
                  [("xt", b, c), "rstd", "g_mix"], [("hn", c)])
        for s in range(4):
            p = pin[0] % 2
            pin[0] += 1
            for k in range(NKC):
                P.mm(ps_in[p][:], hn[:, k, s * 128:(s + 1) * 128], w_in[:, k, 1536:2048], k == 0, k == NKC - 1,
                     [("w_in", k), ("hn", k)], [("ps_in", p)])
            P.copy("act", v_tok[:, s, :], ps_in[p][:], [("ps_in", p)], [("v_tok", s)])
        for u in range(4):
            if u < 2:
                p = proj_fm(u)
                P.act(t_a[:], ps_in[p][:], AF.Silu, [("ps_in", p)], ["t_a"])
                p = proj_fm(4 + u)
                P.act(t_f[:], ps_in[p][:], AF.Sigmoid, [("ps_in", p)], ["t_f"])
                P.ts("dve", t_f[:], t_f[:], oml[:, u:u + 1], lb[:, u:u + 1], ALU.mult, ALU.add,
                     ["t_f", "oml", "lb"], ["t_f"])
                P.act(t_g[:], t_f[:], AF.Ln, ["t_f"], ["t_g"])
                P.op("dve", lambda h, o=t_c[:], m=rst, g=t_g[:]: h.tensor_tensor_scan(o, m, g, 0.0, ALU.mult, ALU.add),
                     ["cst", "t_g"], ["t_c"])
                P.act(ecum[u][:], t_c[:], AF.Exp, ["t_c"], [("ecum", u)])
                P.act(t_n[:], t_c[:], AF.Exp, ["t_c"], ["t_n"], scale=-1.0)
                P.ts("dve", t_k[:], t_f[:], -1.0, 1.0, ALU.mult, ALU.add, ["t_f"], ["t_k"])
                P.tt("dve", qe[u][:], t_a[:], ecum[u][:], ALU.mult, ["t_a", ("ecum", u)], [("qe", u)])
                P.tt("dve", ke[u][:], t_k[:], t_n[:], ALU.mult, ["t_k", "t_n"], [("ke", u)])
                ev = ecum[u][:].rearrange("p (n c) -> p n c", c=LC)[:, :, LC - 1:LC].to_broadcast([128, TT1 // LC, LC])
                P.tt("dve", kl[u][:].rearrange("p (n c) -> p n c", c=LC), ke[u][:].rearrange("p (n c) -> p n c", c=LC),
                     ev, ALU.mult, [("ke", u), ("ecum", u)], [("kl", u)])
            else:
                r = u - 2
                p = proj_fm(u)
                P.tt("dve", qe[u][:], ps_in[p][:], dtab(0, r), ALU.mult, [("ps_in", p), "cst"], [("qe", u)])
                p = proj_fm(4 + u)
                P.tt("dve", ke[u][:], ps_in[p][:], dtab(1, r), ALU.mult, [("ps_in", p), "cst"], [("ke", u)])
                P.tt("dve", kl[u][:], ps_in[p][:], dtab(2, r), ALU.mult, [("ps_in", p), "cst"], [("kl", u)])
            p = proj_fm(8 + u)
            P.act(gs[u][:], ps_in[p][:], AF.Silu, [("ps_in", p)], [("gs", u)])
            for s in range(4):
                P.transpose(ps_kt[:, s, :], kl[u][:, s * 128:(s + 1) * 128], ident[:],
                            [("kl", u), "ident"], ["ps_kt"])
            P.copy("act", kl_tok[:, :, u, :], ps_kt, ["ps_kt"], [("kl_tok", u)])
        for n in range(TT1 // LC):
            c0 = n * LC
            s = n // 2
            r0 = (n % 2) * LC
            for u in range(4):
                P.mm(ps_ds[:, u, :], kl_tok[r0:r0 + LC, s, u, :], v_tok[r0:r0 + LC, s, u * 128:(u + 1) * 128], True, True,
                     [("kl_tok", u), ("v_tok", s)], ["ps_ds"])
            for u in range(4):
                P.mm(ps_at[:, u, :], ke[u][:, c0:c0 + LC], qe[u][:, c0:c0 + LC], True, True,
                     [("ke", u), ("qe", u)], ["ps_at"])
            P.tt("dve", atm[r0:r0 + LC], ps_at, cst[r0:r0 + LC, C1_TRI:C1_TRI + 256].rearrange("p (u c) -> p u c", c=LC),
                 ALU.mult, ["ps_at", "cst"], [("atm", n % 2)])
            sb = sbi[0] % 2
            for u in range(4):
                P.mm(ps_o[u][:, c0:c0 + LC], v_tok[r0:r0 + LC, s, u * 128:(u + 1) * 128], atm[r0:r0 + LC, u, :], True, False,
                     [("v_tok", s), ("atm", n % 2)], [("ps_o", u)])
                P.mm(ps_o[u][:, c0:c0 + LC], S_bf[sb][:, u, :], qe[u][:, c0:c0 + LC], False, True,
                     [("S_bf", sb), ("qe", u)], [("ps_o", u)])
            for u in range(4):
                if u < 2:
                    ex = ecum[u][:, c0 + LC - 1:c0 + LC]
                    rk = [("ecum", u)]
                else:
                    ex = cst[:, C1_EXL + (u - 2) * 8:C1_EXL + (u - 2) * 8 + 1]
                    rk = ["cst"]
                P.stt(S[:, u, :], S[:, u, :], ex, ps_ds[:, u, :], ALU.mult, ALU.add, ["S", "ps_ds"] + rk, ["S"])
            sbi[0] += 1
            P.copy("act", S_bf[sbi[0] % 2][:], S[:], ["S"], [("S_bf", sbi[0] % 2)])
        ob = i % 2
        for u in range(4):
            P.act(osq[:], ps_o[u][:], AF.Square, [("ps_o", u)], ["osq"])
            p = pin[0] % 2
            pin[0] += 1
            P.mm(ps_in[p][:], ones[:], osq[:], True, True, ["ones", "osq"], [("ps_in", p)])
            P.act(ors[:], ps_in[p][:], AF.Sqrt, [("ps_in", p)], ["ors"], bias=EPS, scale=1.0 / 128)
            P.recip(ors[:], ors[:], ["ors"], ["ors"])
            P.tt("dve", otmp[:], ps_o[u][:], ors[:], ALU.mult, [("ps_o", u), "ors"], ["otmp"])
            wc = 0 if u < 2 else 1
            P.stt(of[ob][:, u, :], otmp[:], wn[:, wc:wc + 1], gs[u][:], ALU.mult, ALU.mult,
                  ["otmp", "wn", ("gs", u)], [("of", ob, u)])
            P.dma("sp", o_dst(i, u), of[ob][:, u, :], [("of", ob, u)], [("oT_out", i, u)])
        if "after_tile" in dr:
            dr["after_tile"](i)


def build_mix0(ntok):
    nc = bass.Bass("TRN2", target_bir_lowering=False)
    dr = {}
    dr["xT"] = nc.dram_tensor("xT", [D, ntok], F32, kind="ExternalInput").ap()
    dr["w_in"] = nc.dram_tensor("w_in", [D, 2048], F32, kind="ExternalInput").ap()
    dr["g_mix"] = nc.dram_tensor("g_mix", [128, NKC], F32, kind="ExternalInput").ap()
    dr["cst"] = nc.dram_tensor("cst", [128, C1_N], F32, kind="ExternalInput").ap()
    dr["ident"] = nc.dram_tensor("ident", [128, 128], F32, kind="ExternalInput").ap()
    dr["lbp"] = nc.dram_tensor("lbp", [128, 4], F32, kind="ExternalInput").ap()
    dr["wn"] = nc.dram_tensor("wn", [128, 2], F32, kind="ExternalInput").ap()
    dr["oT_out"] = nc.dram_tensor("oT_out", [512, ntok], BF16, kind="ExternalOutput").ap()
    P = Prog(nc)
    emit_mix0(P, nc, ntok, dr)
    P.emit()
    return nc


def mix0_consts(hh):
    c = np.zeros((128, C1_N), np.float32)
    j = np.arange(64)[:, None]
    i = np.arange(64)[None, :]
    tri = (j <= i).astype(np.float32)
    c[0:64, C1_TRI:C1_TRI + 256] = np.tile(tri, (1, 4))
    c[64:128, C1_TRI:C1_TRI + 256] = np.tile(tri, (1, 4))
    pos = np.arange(512) % LC
    c[:, C1_RST:C1_RST + 512] = (pos != 0).astype(np.float32)[None, :]
    for r in range(2):
        hidx = 2 * hh + r
        lg = np.log(np.float32(1.0) - np.exp2(np.float32(-5.0 - hidx))).astype(np.float32)
        qd = np.exp((pos + 1.0).astype(np.float32) * lg).astype(np.float32)
        kd = (np.exp(-(pos + 1.0).astype(np.float32) * lg) * np.float32(128 ** -0.5)).astype(np.float32)
        ld = (np.exp((LC - 1.0 - pos).astype(np.float32) * lg) * np.float32(128 ** -0.5)).astype(np.float32)
        c[:, C1_DQ + (0 * 2 + r) * 512:C1_DQ + (0 * 2 + r) * 512 + 512] = qd[None, :]
        c[:, C1_DQ + (1 * 2 + r) * 512:C1_DQ + (1 * 2 + r) * 512 + 512] = kd[None, :]
        c[:, C1_DQ + (2 * 2 + r) * 512:C1_DQ + (2 * 2 + r) * 512 + 512] = ld[None, :]
        c[:, C1_EXL + r * 8:C1_EXL + r * 8 + 8] = np.exp(np.float32(LC) * lg)
    return c


NEGM = -30000.0
DUM_SLC = 0
DUM_WIN = 1
HD = 64
NR = 8
WIN_IN = 512 + 128 + 128 + 128 + 24
C2_BC = 0
C2_BT = 4
C2_AW = 68
C2_DW = 68 + 255
C2_SL = 68 + 510
C2_NT = 68 + 510 + 8
C2_N = 68 + 510 + 8 + 16
B2_ID = 0
B2_TU = 128
B2_TL = 256
B2_CM = 384
OVS = 130
B2_OV = 384 + 2560
B2_SEL = B2_OV + 4 * OVS
B2_N = B2_SEL + 24 * 64


def emit_mix1(P, nc, ntok, dr, slopes, stop=99):
    ntile = ntok // 512
    NKCH = ntok // 128
    NCB = ntok // 16 - 1
    NCC = ntok // 2048
    w_in = P.sbuf([128, NKC, WIN_IN], BF16, "w_in")
    cst = P.sbuf([128, C2_N], F32, "cst")
    cb = P.sbuf([128, B2_N], BF16, "cb")
    w1kv = P.sbuf([128, 32, 64], BF16, "w1kv")
    poskv = P.sbuf([128, 32], BF16, "poskv")
    w2kv = P.sbuf([64, 128], BF16, "w2kv")
    kz = P.sbuf([128, ntok], BF16, "kz")
    ks_x = P.sbuf([67, ntok], BF16, "ks_x")
    kw_x = P.sbuf([67, ntok], BF16, "kw_x")
    vs_tok = P.sbuf([128, NKCH, 128], BF16, "vs_tok")
    vw_tok = P.sbuf([128, NKCH, 128], BF16, "vw_tok")
    kc_x = P.sbuf([67, NCC * 128], BF16, "kc_x")
    vc_tok = P.sbuf([128, NCC, 128], BF16, "vc_tok")
    hid = P.sbuf([64, 2, NCC * 128], BF16, "hid")
    cbias = P.sbuf([64, 2], F32, "cbias")
    hn = [P.sbuf([128, NKC, 512], BF16, "hn") for _ in range(2)]
    q_x = [P.sbuf([67, 512], BF16, "q_x") for _ in range(NR)]
    gsig = P.sbuf([24, 512], F32, "gsig")
    g_hl = P.sbuf([56, 512], BF16, "g_hl")
    bias_c = P.sbuf([128, NR, NCC], F32, "bias_c")
    bias_t = P.sbuf([128, NR, NKCH], F32, "bias_t")
    Ec = [P.sbuf([128, NCC, 512], BF16, "Ec") for _ in range(2)]
    NEB = 6
    Eb = [P.sbuf([128, 512], BF16, "Eb") for _ in range(NEB)]
    oacc = P.sbuf([64, NR, 512], F32, "oacc")
    rcp = P.sbuf([64, 512], F32, "rcp")
    wgt = P.sbuf([64, 512], F32, "wgt")
    tmpo = P.sbuf([64, 512], F32, "tmpo")
    obf = [P.sbuf([64, NR, 512], BF16, "obf") for _ in range(2)]
    pacc = P.sbuf([128, 4, 128], F32, "pacc")
    rc1 = P.sbuf([128, 1], F32, "rc1")
    sc = P.sbuf([128, 128], F32, "sc")
    sc2 = P.sbuf([128, 128], F32, "sc2")
    m8 = P.sbuf([128, 16], F32, "m8")
    nm = P.sbuf([128, 128], BF16, "nm")
    negmT = P.sbuf([128, 512], BF16, "negmT")

    NPS = 4
    ps_p = [P.psum([128, 512], F32, "ps_g") for _ in range(NPS)]
    ps_s = ps_p
    ps_o = [P.psum([128, 512], F32, "ps_o") for _ in range(2)]
    ps_u = [P.psum([128, 512], F32, "ps_u") for _ in range(2)]

    P.dma("sp", cst[:], dr["cst"], [], ["cst"])
    for j0 in range(0, B2_N, 1024):
        j1 = min(B2_N, j0 + 1024)
        P.dma("pool", cb[:, j0:j1], dr["cb"][:, j0:j1], [], ["cb"])
    wi_v = dr["w_in"].rearrange("(c p) n -> p c n", p=128)
    for c in range(NKC):
        P.dma("pool", w_in[:, c, :], wi_v[:, c, :], [], [("w_in", c)])
    P.dma("pool", w1kv[:].rearrange("p a b -> p (a b)"), dr["w1kv"], [], ["w1kv"])
    P.dma("pool", poskv[:], dr["poskv"], [], ["poskv"])
    P.dma("pool", w2kv[:], dr["w2kv"], [], ["w2kv"])
    for r in range(NR):
        P.dma("pool", q_x[r][64:67, :], dr["qbias"][:, r, :], [], [("q_b", r)])
    ident = cb[:, B2_ID:B2_ID + 128]
    tri_u = cb[:, B2_TU:B2_TU + 128]
    tri_l = cb[:, B2_TL:B2_TL + 128]
    P.memset("dve", ks_x[64:67, :], 1.0, ["ks_b"])
    P.memset("dve", kw_x[64:67, :], 1.0, ["kw_b"])
    P.memset("dve", kc_x[:], 0.0, ["kc_x"])
    P.memset("dve", kc_x[64:67, :], 1.0, ["kc_x"])
    P.memset("pool", vs_tok[:, :, 64:128], 1.0, ["vs_ones"])
    P.memset("pool", vw_tok[:, :, 64:128], 1.0, ["vw_ones"])
    P.memset("pool", vc_tok[:], 0.0, ["vc_tok"])
    P.memset("pool", vc_tok[:, :, 64:128], 1.0, ["vc_tok"])
    P.memset("pool", g_hl[:], 0.0, ["g_hl"])
    P.memset("pool", hid[:], 0.0, ["hid"])

    if stop <= 0:
        return
    if "hn_src" in dr:
        hn_src = dr["hn_src"]
    else:
        hn_v = dr["hnT"].rearrange("(c p) t -> p c t", p=128)
        hn_src = lambda i, c: hn_v[:, c, i * 512:(i + 1) * 512]

    def load(slot, i):
        for c in range(NKC):
            P.dma("sp", hn[slot][:, c, :], hn_src(i, c), [], [("hn", slot, c)])

    ppi = [0]

    def nextp():
        p = ppi[0] % NPS
        ppi[0] += 1
        return p

    nload = [0]
    load(0, 0)
    for i in range(ntile):
        b = nload[0] % 2
        nload[0] += 1
        if i + 1 < ntile:
            load(nload[0] % 2, i + 1)
        else:
            load(nload[0] % 2, 0)
        sl = slice(i * 512, (i + 1) * 512)
        hk = [("hn", b, c) for c in range(NKC)]
        p = nextp()
        for k in range(NKC):
            P.mm(ps_p[p][:], w_in[:, k, 512:640], hn[b][:, k, :], k == 0, k == NKC - 1, [("w_in", k), ("hn", b, k)], [("ps_p", p)])
        P.copy("act", kz[:, sl], ps_p[p][:], [("ps_p", p)], [("kz", i)])
        if stop <= 0.3:
            continue
        p = nextp()
        for k in range(NKC):
            P.mm(ps_p[p][:], w_in[:, k, 640:768], hn[b][:, k, :], k == 0, k == NKC - 1, [("w_in", k), ("hn", b, k)], [("ps_p", p)])
        P.copy("dve", ks_x[0:64, sl], ps_p[p][0:64, :], [("ps_p", p)], [("ks_x", i)])
        P.copy("dve", kw_x[0:64, sl], ps_p[p][64:128, :], [("ps_p", p)], [("kw_x", i)])
        if stop <= 0.6:
            continue
        p = nextp()
        for s in range(4):
            for k in range(NKC):
                P.mm(ps_p[p][:, s * 128:(s + 1) * 128], hn[b][:, k, s * 128:(s + 1) * 128], w_in[:, k, 768:896], k == 0, k == NKC - 1,
                     [("w_in", k), ("hn", b, k)], [("ps_p", p)])
        pv = ps_p[p][:].rearrange("p (s c) -> p s c", c=128)
        P.copy("dve", vs_tok[:, 4 * i:4 * i + 4, 0:64], pv[:, :, 0:64], [("ps_p", p)], [("vs_tok", i)])
        P.copy("dve", vw_tok[:, 4 * i:4 * i + 4, 0:64], pv[:, :, 64:128], [("ps_p", p)], [("vw_tok", i)])

    if stop <= 1:
        return
    kzall = [("kz", i) for i in range(ntile)]
    for kv in range(2):
        base = 64 * kv
        p = nextp()
        for pp in range(32):
            P.mm(ps_p[p][0:64, 0:1], w1kv[base:base + 64, pp, :], poskv[base:base + 64, pp:pp + 1], pp == 0, pp == 31,
                 ["w1kv", "poskv"], [("ps_p", p)])
        P.copy("dve", cbias[:, kv:kv + 1], ps_p[p][0:64, 0:1], [("ps_p", p)], [("cbias", kv)])
        for c0 in range(0, NCB, 512):
            cn = min(512, NCB - c0)
            p = nextp()
            for pp in range(32):
                rhs = kz[base:base + 64, pp + 16 * c0: pp + 16 * c0 + 16 * (cn - 1) + 1: 16]
                P.mm(ps_p[p][0:64, 0:cn], w1kv[base:base + 64, pp, :], rhs, pp == 0, pp == 31, ["w1kv"] + kzall, [("ps_p", p)])
            P.act(hid[:, kv, c0:c0 + cn], ps_p[p][0:64, 0:cn], AF.Silu, [("ps_p", p), ("cbias", kv)], ["hid"],
                  bias=cbias[:, kv:kv + 1])
    for c0 in range(0, NCB, 512):
        cn = min(512, NCB - c0)
        p = nextp()
        P.mm(ps_p[p][0:64, 0:cn], w2kv[:, 0:64], hid[:, 0, c0:c0 + cn], True, True, ["w2kv", "hid"], [("ps_p", p)])
        P.copy("dve", kc_x[0:64, c0:c0 + cn], ps_p[p][0:64, 0:cn], [("ps_p", p)], ["kc_x"])
    for m in range(NCC):
        p = nextp()
        P.mm(ps_p[p][:, 0:64], hid[:, 1, m * 128:(m + 1) * 128], w2kv[:, 64:128], True, True, ["w2kv", "hid"], [("ps_p", p)])
        rows = 128 if (m + 1) * 128 <= NCB else NCB - m * 128
        P.copy("dve", vc_tok[0:rows, m, 0:64], ps_p[p][0:rows, 0:64], [("ps_p", p)], ["vc_tok"])
    for j0 in range(0, ntok, 2048):
        P.dma("pool", kz[:, j0:j0 + 2048], dr["zexp"][:, j0:j0 + 2048], [], kzall)

    if stop <= 2:
        return
    sp_c = P.sbuf([128, NR, NCC], F32, "sp_c")
    sp_t = P.sbuf([128, NR, NKCH], F32, "sp_t")
    nt0 = P.sbuf([128, NR, 16], F32, "nt0")
    for r in range(NR):
        slp = cst[:, C2_SL + r:C2_SL + r + 1]
        P.ts("dve", sp_c[:, r, :], cst[:, C2_BC:C2_BC + NCC], slp, None, ALU.mult, None, ["cst"], ["sp_c"])
        P.ts("dve", sp_t[:, r, :], cst[:, C2_BT:C2_BT + NKCH], slp, None, ALU.mult, None, ["cst"], ["sp_t"])
        P.ts("dve", nt0[:, r, :], cst[:, C2_NT:C2_NT + 16], slp, None, ALU.mult, None, ["cst"], ["nt0"])
    if "o_dst" in dr:
        o_dst1 = dr["o_dst"]
    else:
        o_v = dr["oT_out"].rearrange("(r p) t -> p r t", p=64)
        o_dst1 = lambda n_: o_v[:, :, n_ * 512:(n_ + 1) * 512]
    psi = [0]
    poi = [0]
    ebi = [0]

    def gate_w(r, br, po):
        P.ts("dve", rcp[:], ps_o[po][64:128, :], 1e-30, None, ALU.add, None, [("ps_o", po)], ["rcp"])
        P.recip(rcp[:], rcp[:], ["rcp"], ["rcp"])
        p = nextp()
        P.mm(ps_p[p][0:64, :], cb[0:56, B2_SEL + (3 * r + br) * 64:B2_SEL + (3 * r + br + 1) * 64], g_hl[:], True, True,
             ["cb", "g_hl"], [("ps_p", p)])
        P.tt("dve", wgt[:], rcp[:], ps_p[p][0:64, :], ALU.mult, ["rcp", ("ps_p", p)], ["wgt"])

    import os
    dbg_lo, dbg_hi = [int(v) for v in os.environ.get('DBG_TILES', '0,99').split(',')]
    for n in range(ntile):
        b = nload[0] % 2
        nload[0] += 1
        if n + 1 < ntile:
            load(nload[0] % 2, n + 1)
        if n < dbg_lo or n >= dbg_hi:
            continue
        t0 = 512 * n
        sl = slice(t0, t0 + 512)
        for r in range(NR):
            p = nextp()
            for k in range(NKC):
                P.mm(ps_p[p][0:64, :], w_in[:, k, r * 64:(r + 1) * 64], hn[b][:, k, :], k == 0, k == NKC - 1,
                     [("w_in", k), ("hn", b, k)], [("ps_p", p)])
            P.op("act", lambda h, o=q_x[r][0:64, :], i_=ps_p[p][0:64, :]: h.mul(o, i_, HD ** -0.5), [("ps_p", p)], [("q_x", r)],
                 banks=P._banks(ps_p[p][0:64, :]))
        if stop <= 2.3:
            continue
        p = nextp()
        for k in range(NKC):
            P.mm(ps_p[p][0:24, :], w_in[:, k, 896:920], hn[b][:, k, :], k == 0, k == NKC - 1,
                 [("w_in", k), ("hn", b, k)], [("ps_p", p)])
        P.act(gsig[:], ps_p[p][0:24, :], AF.Sigmoid, [("ps_p", p)], ["gsig"])
        P.copy("dve", g_hl[0:24, :], gsig[:], ["gsig"], ["g_hl"])
        P.tt("dve", g_hl[32:56, :], gsig[:], g_hl[0:24, :], ALU.subtract, ["gsig", "g_hl"], ["g_hl"])
        if stop <= 2.6:
            continue
        P.tt("dve", bias_c[:], sp_c[:], nt0[:, :, n:n + 1].to_broadcast([128, NR, NCC]), ALU.add, ["sp_c", "nt0"],
             [("bias_c", r) for r in range(NR)])
        P.tt("dve", bias_t[:], sp_t[:], nt0[:, :, n:n + 1].to_broadcast([128, NR, NKCH]), ALU.add, ["sp_t", "nt0"],
             [("bias_t", r) for r in range(NR)])
        if stop <= 3:
            continue
        mlist = [m for m in range(NCC) if 2048 * m + 31 <= t0 + 511]
        for r in range(NR):
            e = Ec[r % 2]
            po = poi[0] % 2
            poi[0] += 1
            for mi, m in enumerate(mlist):
                ps = nextp()
                o = n - 4 * m
                mixed = 0 <= o <= 4
                P.mm(ps_s[ps][:], kc_x[:, m * 128:(m + 1) * 128], q_x[r][:], True, not mixed,
                     ["kc_x", ("q_x", r), ("q_b", r)], [("ps_p", ps)])
                if mixed:
                    P.mm(ps_s[ps][:], ident, cb[:, B2_CM + o * 512:B2_CM + (o + 1) * 512], False, True, ["cb"], [("ps_p", ps)])
                P.act(e[:, m, :], ps_s[ps][:], AF.Exp, [("ps_p", ps), ("bias_c", r)], [("Ec", r % 2, m)],
                      bias=bias_c[:, r, m:m + 1])
                P.mm(ps_o[po][:], vc_tok[:, m, :], e[:, m, :], mi == 0, mi == len(mlist) - 1,
                     ["vc_tok", ("Ec", r % 2, m)], [("ps_o", po)])
            if stop <= 3.3:
                continue
            gate_w(r, 0, po)
            P.tt("dve", oacc[:, r, :], ps_o[po][0:64, :], wgt[:], ALU.mult, [("ps_o", po), "wgt"], [("oacc", r)])
            if stop <= 3.6:
                continue
            dbg_u = int(os.environ.get('DBG_U', '0'))
            for qs in range(4):
                ub = ps_u[qs // 2][:, (qs % 2) * OVS:(qs % 2) * OVS + OVS]
                for mi, m in enumerate(mlist):
                    if dbg_u == 2:
                        continue
                    P.mm(ub, e[:, m, qs * 128:(qs + 1) * 128], cb[:, B2_OV + m * OVS:B2_OV + m * OVS + OVS],
                         mi == 0, mi == len(mlist) - 1, [("Ec", r % 2, m), "cb"], [("ps_u", qs)])
                if dbg_u == 1:
                    continue
                P.ts("dve", rc1[:], ub[:, 128:129], 1e-30, None, ALU.add, None, [("ps_u", qs)], ["rc1"])
                P.recip(rc1[:], rc1[:], ["rc1"], ["rc1"])
                if r == 0:
                    P.ts("dve", pacc[:, qs, :], ub[:, 0:128], rc1[:, 0:1], None, ALU.mult, None,
                         [("ps_u", qs), "rc1"], [("pacc", qs)])
                else:
                    P.stt(pacc[:, qs, :], ub[:, 0:128], rc1[:, 0:1], pacc[:, qs, :], ALU.mult, ALU.add,
                          [("ps_u", qs), "rc1", ("pacc", qs)], [("pacc", qs)])
        if stop <= 4:
            continue
        for qs in range(4):
            off = 8 * n + 2 * qs
            aw = cst[:, C2_AW + 127 - off:C2_AW + 255 - off]
            dw = cst[:, C2_DW + 127 - off:C2_DW + 255 - off]
            P.tt("dve", sc[:], pacc[:, qs, :], aw, ALU.mult, [("pacc", qs), "cst"], ["sc"])
            P.tt("dve", sc[:], sc[:], dw, ALU.add, ["sc", "cst"], ["sc"])
            P.memset("dve", sc[:, 0:1], 1e9, ["sc"])
            P.op("dve", lambda h, o=m8[:, 0:8], i_=sc[:]: h.max(o, i_), ["sc"], ["m8"])
            P.op("dve", lambda h, o=sc2[:], a=m8[:, 0:8], v=sc[:]: h.match_replace(o, a, v, -3.0e38), ["sc", "m8"], ["sc2"])
            P.op("dve", lambda h, o=m8[:, 8:16], i_=sc2[:]: h.max(o, i_), ["sc2"], ["m8"])
            P.ts("dve", sc2[:], sc[:], m8[:, 15:16], None, ALU.is_ge, None, ["sc", "m8"], ["sc2"])
            P.ts("dve", nm[:], sc2[:], -1.0, -NEGM, ALU.add, ALU.mult, ["sc2"], ["nm"])
            P.transpose(ps_u[qs // 2][:, 264:392].bitcast(BF16)[:, (qs % 2) * 128:(qs % 2) * 128 + 128], nm[:], ident,
                        ["nm", "cb"], [("ps_nm", qs)])
        for hb_ in range(2):
            P.copy("act", negmT[:, hb_ * 256:(hb_ + 1) * 256], ps_u[hb_][:, 264:392].bitcast(BF16),
                   [("ps_nm", 2 * hb_), ("ps_nm", 2 * hb_ + 1)], [("negmT", hb_)])
        if stop <= 5:
            continue
        for br in (2, 1):
            if br == 2:
                klist = [kc for kc in ([4 * n + a for a in range(4)] + [4 * n - 4 + a for a in range(4)]) if kc >= 0]
                kx, vt, kkey, vkey, vones = kw_x, vw_tok, "kw_x", "vw_tok", "vw_ones"
            else:
                klist = list(range(4 * n + 4))
                kx, vt, kkey, vkey, vones = ks_x, vs_tok, "ks_x", "vs_tok", "vs_ones"
            for rp in range(0, NR, 2):
                heads = (rp, rp + 1)
                pend = None

                def emit_pv(pp):
                    ki_, kc_, cs_, ebs_ = pp
                    for r in heads:
                        po = r % 2
                        P.mm(ps_o[po][:, cs_], vt[:, kc_, :], Eb[ebs_[r]][:, cs_], ki_ == 0, ki_ == len(klist) - 1,
                             [(vkey, kc_ // 4), vones, ("Eb", ebs_[r])], [("ps_o", po)])

                for ki, kc in enumerate(klist):
                    if kc >= 4 * n:
                        a = kc - 4 * n
                        c_lo, c_hi = 128 * a, 512
                        dmask = (tri_u, c_lo)
                    else:
                        a = kc - (4 * n - 4)
                        if br == 2:
                            c_lo, c_hi = 0, 128 * a + 128
                            dmask = (tri_l, 128 * a)
                        else:
                            c_lo, c_hi = 0, 512
                            dmask = None
                    cs = slice(c_lo, c_hi)
                    ebs = {}
                    for r in heads:
                        ps = nextp()
                        eb = ebi[0] % NEB
                        ebi[0] += 1
                        ebs[r] = eb
                        P.mm(ps_s[ps][:, cs], kx[:, kc * 128:(kc + 1) * 128], q_x[r][:, cs], True, False,
                             [(kkey, kc // 4), kkey[:2] + "_b", ("q_x", r), ("q_b", r)], [("ps_p", ps)])
                        if br == 1:
                            P.mm(ps_s[ps][:, cs], kz[:, kc * 128:(kc + 1) * 128], negmT[:, cs], False, dmask is None,
                                 kzall + [("negmT", 0), ("negmT", 1)], [("ps_p", ps)])
                        if dmask is not None:
                            P.mm(ps_s[ps][:, dmask[1]:dmask[1] + 128], ident, dmask[0], False, True, ["cb"], [("ps_p", ps)])
                        P.act(Eb[eb][:, cs], ps_s[ps][:, cs], AF.Exp, [("ps_p", ps), ("bias_t", r)], [("Eb", eb)],
                              bias=bias_t[:, r, kc:kc + 1])
                    for _ in range(DUM_WIN if br == 2 else DUM_SLC):
                        P.mm(ps_u[0][:, :], ident, cb[:, B2_CM:B2_CM + 512], True, True, ["cb"], ["ps_junk"])
                    if pend is not None:
                        emit_pv(pend)
                    pend = (ki, kc, cs, ebs)
                emit_pv(pend)
                for r in heads:
                    po = r % 2
                    gate_w(r, br, po)
                    P.tt("dve", tmpo[:], ps_o[po][0:64, :], wgt[:], ALU.mult, [("ps_o", po), "wgt"], ["tmpo"])
                    P.tt("dve", oacc[:, r, :], oacc[:, r, :], tmpo[:], ALU.add, [("oacc", r), "tmpo"], [("oacc", r)])
        ob = n % 2
        for r in range(NR):
            P.copy("act", obf[ob][:, r, :], oacc[:, r, :], [("oacc", r)], [("obf", ob, r)])
        P.dma("sp", o_dst1(n), obf[ob][:], [("obf", ob, r) for r in range(NR)], [("oT_out", n)])
        if "after_tile" in dr:
            dr["after_tile"](n)


def build_mix1(ntok, slopes, stop=99):
    nc = bass.Bass("TRN2", target_bir_lowering=False)
    dr = {}
    dr["hnT"] = nc.dram_tensor("hnT", [D, ntok], BF16, kind="ExternalInput").ap()
    dr["w_in"] = nc.dram_tensor("w_in", [D, WIN_IN], F32, kind="ExternalInput").ap()
    dr["cst"] = nc.dram_tensor("cst", [128, C2_N], F32, kind="ExternalInput").ap()
    dr["cb"] = nc.dram_tensor("cb", [128, B2_N], F32, kind="ExternalInput").ap()
    dr["w1kv"] = nc.dram_tensor("w1kv", [128, 2048], F32, kind="ExternalInput").ap()
    dr["poskv"] = nc.dram_tensor("poskv", [128, 32], F32, kind="ExternalInput").ap()
    dr["w2kv"] = nc.dram_tensor("w2kv", [64, 128], F32, kind="ExternalInput").ap()
    dr["qbias"] = nc.dram_tensor("qbias", [3, NR, 512], F32, kind="ExternalInput").ap()
    dr["zexp"] = nc.dram_tensor("zexp", [128, ntok], F32, kind="ExternalInput").ap()
    dr["oT_out"] = nc.dram_tensor("oT_out", [512, ntok], BF16, kind="ExternalOutput").ap()
    P = Prog(nc)
    emit_mix1(P, nc, ntok, dr, slopes, stop)
    P.emit()
    return nc


def _bf(x):
    return np.asarray(x, np.float32).astype(ml_dtypes.bfloat16).astype(np.float32)


def mix1_slopes(g):
    h = np.arange(8 * g, 8 * g + 8, dtype=np.float32)
    return np.exp2(np.float32(-8.0) * (h + np.float32(1.0)) / np.float32(16.0)).astype(np.float32)


def mix1_consts(g, ntok):
    slopes = mix1_slopes(g)
    cst = np.zeros((128, C2_N), np.float32)
    ki = np.arange(128)
    ncc = ntok // 2048
    for m in range(ncc):
        cst[:, C2_BC + m] = 16.0 * (128 * m + ki) + 31.0
    for kc in range(ntok // 128):
        cst[:, C2_BT + kc] = 128.0 * kc + ki
    hb = (ki // 64)[:, None]
    jj = (np.arange(255) - 127)[None, :]
    allowed = jj <= hb
    forced = (jj == hb) | (jj == hb - 1)
    cst[:, C2_AW:C2_AW + 255] = allowed.astype(np.float32)
    cst[:, C2_DW:C2_DW + 255] = np.where(forced, np.float32(1e9), np.where(allowed, np.float32(0.0), np.float32(-1e30)))
    cst[:, C2_SL:C2_SL + 8] = slopes[None, :]
    cst[:, C2_NT:C2_NT + 16] = -512.0 * np.arange(16)[None, :]
    cb = np.zeros((128, B2_N), np.float32)
    cb[:, B2_ID:B2_ID + 128] = np.eye(128)
    k = ki[:, None]
    q = np.arange(128)[None, :]
    cb[:, B2_TU:B2_TU + 128] = np.where(q >= k, 0.0, NEGM)
    cb[:, B2_TL:B2_TL + 128] = np.where(q < k, 0.0, NEGM)
    qi = np.arange(512)[None, :]
    for o in range(5):
        cb[:, B2_CM + o * 512:B2_CM + (o + 1) * 512] = np.where(512 * o + qi >= 16 * k + 31, 0.0, NEGM)
    ncb = ntok // 16 - 1
    ns = ntok // 64
    for m in range(ncc):
        c = 128 * m + ki
        c0 = c * 16
        for j in range(128):
            if j >= ns:
                continue
            lo = np.maximum(c0, j * 64)
            hi = np.minimum(c0 + 32, j * 64 + 64)
            cb[:, B2_OV + m * OVS + j] = np.where(c < ncb, np.maximum(hi - lo, 0) / 32.0, 0.0)
        cb[:, B2_OV + m * OVS + 128] = (c < ncb).astype(np.float32)
    for idx in range(24):
        cb[idx, B2_SEL + idx * 64:B2_SEL + (idx + 1) * 64] = 1.0
        cb[32 + idx, B2_SEL + idx * 64:B2_SEL + (idx + 1) * 64] = 1.0
    cb = _bf(cb)
    i = np.arange(512, dtype=np.float64)
    qb = np.zeros((3, NR, 512), np.float32)
    for r in range(NR):
        a = -np.float64(slopes[r]) * i
        a1 = _bf(a)
        a2 = _bf(a - a1)
        a3 = _bf(a - a1 - a2)
        qb[0, r], qb[1, r], qb[2, r] = a1, a2, a3
    z = (np.arange(ntok)[None, :] // 64 == np.arange(128)[:, None]).astype(np.float32)
    return {"cst": cst, "cb": cb, "qbias": qb, "zexp": z}, slopes


def mix1_weights(w_in, cpk, cpv, w1k, w2k, w1v, w2v, g):
    cols = [w_in[:, 512 * g:512 * g + 512]]
    for base in (1024, 1152, 1280, 1536, 1408, 1664):
        cols.append(w_in[:, base + 64 * g: base + 64 * g + 64])
    cols.append(w_in[:, 1792 + 24 * g:1792 + 24 * g + 24])
    wc = np.ascontiguousarray(np.concatenate(cols, axis=1))
    w1 = np.concatenate([w1k.reshape(32, 64, 64).transpose(1, 0, 2).reshape(64, 2048),
                         w1v.reshape(32, 64, 64).transpose(1, 0, 2).reshape(64, 2048)], axis=0)
    pos = np.concatenate([cpk.T, cpv.T], axis=0)
    w2 = np.concatenate([w2k, w2v], axis=1)
    return {"w_in": wc, "w1kv": np.ascontiguousarray(w1), "poskv": np.ascontiguousarray(pos), "w2kv": np.ascontiguousarray(w2)}


_PROGS = {}


def _prog(key, fn):
    if key not in _PROGS:
        _PROGS[key] = fn()
    return _PROGS[key]


def _mix0_inputs(xT, w_in, lbs, hno, rno, g_mix, hh):
    hsel = [2 * hh, 2 * hh + 1]
    col = lambda grp, h: w_in[:, grp * 512 + h * 128: grp * 512 + (h + 1) * 128]
    Q = [col(0, h) for h in hsel] + [col(4, h) for h in hsel]
    K = [col(1, h) for h in hsel] + [col(5, h) for h in hsel]
    G = [col(3, h) for h in hsel] + [col(7, h) for h in hsel]
    V = [col(2, h) for h in hsel] + [col(6, h) for h in hsel]
    wc = np.ascontiguousarray(np.concatenate(Q + K + G + V, axis=1))
    lbp = np.stack([lbs[0, hsel[0] * 128:(hsel[0] + 1) * 128], lbs[0, hsel[1] * 128:(hsel[1] + 1) * 128],
                    lbs[1, hsel[0] * 128:(hsel[0] + 1) * 128], lbs[1, hsel[1] * 128:(hsel[1] + 1) * 128]], axis=1)
    return {"xT": xT, "w_in": wc, "g_mix": gvec(g_mix), "cst": mix0_consts(hh), "ident": np.eye(128, dtype=np.float32),
            "lbp": np.ascontiguousarray(lbp.astype(np.float32)),
            "wn": np.ascontiguousarray(np.stack([hno, rno], axis=1).astype(np.float32))}


def kernel_unfused(x, mix_norm, ffn_norm, final_norm, even_w_in, hgrn_lower_bounds, hgrn_out_norm,
           ret_out_norm, even_w_out, odd_w_in, cmp_pos_k, cmp_pos_v, cmp_w1_k, cmp_w2_k,
           cmp_w1_v, cmp_w2_v, odd_w_out, ffn_w_gate_up, ffn_w_down):
    f = lambda a: np.asarray(a, np.float32)
    x = f(x)
    cores = list(range(8))
    xT = [np.ascontiguousarray(x[b].T) for b in range(B)]
    TH = T // 2
    ncA = _prog("mix0", lambda: build_mix0(T))
    inA = [_mix0_inputs(xT[c // 2], f(even_w_in)[0], f(hgrn_lower_bounds), f(hgrn_out_norm)[0], f(ret_out_norm)[0],
                        f(mix_norm)[0], c % 2) for c in cores]
    rA = run_bass_kernel_spmd(ncA, inA, core_ids=cores).results
    o0T = []
    for b in range(B):
        full = np.empty((D, T), ml_dtypes.bfloat16)
        for hh in range(2):
            o = rA[2 * b + hh]["oT_out"]
            full[256 * hh:256 * hh + 256] = o[0:256]
            full[512 + 256 * hh:512 + 256 * hh + 256] = o[256:512]
        o0T.append(full)
    ncB = _prog("ffn_mid", lambda: build_ffn(TH, "mid"))
    inB = []
    for c in cores:
        b, s = c // 2, c % 2
        inB.append({"oT": np.ascontiguousarray(o0T[b][:, s * TH:(s + 1) * TH]),
                    "xT": np.ascontiguousarray(xT[b][:, s * TH:(s + 1) * TH]),
                    "w_out": f(even_w_out)[0], "g_ffn": gvec(f(ffn_norm)[0]), "g_nxt": gvec(f(mix_norm)[1]),
                    "w_gu": f(ffn_w_gate_up)[0], "w_dn": f(ffn_w_down)[0]})
    rB = run_bass_kernel_spmd(ncB, inB, core_ids=cores).results
    inC = []
    slopes = None
    for c in cores:
        b, g = c // 2, c % 2
        consts, sl = mix1_consts(g, T)
        d = dict(consts)
        d.update(mix1_weights(f(odd_w_in)[0], f(cmp_pos_k)[0], f(cmp_pos_v)[0], f(cmp_w1_k)[0], f(cmp_w2_k)[0],
                              f(cmp_w1_v)[0], f(cmp_w2_v)[0], g))
        d["hnT"] = np.ascontiguousarray(np.concatenate([rB[2 * b]["out_n"], rB[2 * b + 1]["out_n"]], axis=1))
        inC.append(d)
    ncC = _prog("mix1", lambda: build_mix1(T, None))
    rC = run_bass_kernel_spmd(ncC, inC, core_ids=cores).results
    ncD = _prog("ffn_fin", lambda: build_ffn(TH, "fin"))
    inD = []
    for c in cores:
        b, s = c // 2, c % 2
        o1 = np.concatenate([rC[2 * b]["oT_out"][:, s * TH:(s + 1) * TH], rC[2 * b + 1]["oT_out"][:, s * TH:(s + 1) * TH]], axis=0)
        inD.append({"oT": np.ascontiguousarray(o1), "xT": rB[c]["out_h"],
                    "w_out": f(odd_w_out)[0], "g_ffn": gvec(f(ffn_norm)[1]), "g_nxt": gvec(f(final_norm)),
                    "w_gu": f(ffn_w_gate_up)[1], "w_dn": f(ffn_w_down)[1]})
    rD = run_bass_kernel_spmd(ncD, inD, core_ids=cores).results
    out = np.empty((B, T, D), np.float32)
    for c in cores:
        b, s = c // 2, c % 2
        out[b, s * TH:(s + 1) * TH, :] = rD[c]["out_n"].T
    return out


PAIRS = [[0, 1], [2, 3], [4, 5], [6, 7]]
TH = T // 2


def build_fused(nph=4):
    nc = bass.Bass("TRN2", target_bir_lowering=False)
    ext = lambda name, shape, dt=F32: nc.dram_tensor(name, list(shape), dt, kind="ExternalInput").ap()
    P = Prog(nc)
    NCH = 8
    o0c = [nc.dram_tensor("o0c%d" % k, [512, 1024], BF16) for k in range(NCH)]
    g0c = [nc.dram_tensor("g0c%d" % k, [1024, 1024], BF16) for k in range(NCH)]
    d0 = {"xT": ext("xT", [D, T]), "w_in": ext("w_in0", [D, 2048]), "g_mix": ext("g_mix0", [128, NKC]),
          "cst": ext("cst0", [128, C1_N]), "ident": ext("ident", [128, 128]), "lbp": ext("lbp", [128, 4]),
          "wn": ext("wn", [128, 2])}
    o0v = [t.ap().rearrange("(u p) t -> p u t", p=128) for t in o0c]
    d0["o_dst"] = lambda i, u: o0v[i // 2][:, u, (i % 2) * 512:(i % 2) * 512 + 512]

    def after0(i):
        if i % 2 == 1:
            k = i // 2
            P.allgather(g0c[k].ap().opt(), o0c[k].ap().opt(), PAIRS,
                        [("oT_out", ii, u) for ii in (2 * k, 2 * k + 1) for u in range(4)], [("g0c", k)])
    d0["after_tile"] = after0
    emit_mix0(P, nc, T, d0)
    if nph == 1:
        dbg = nc.dram_tensor("dbg", [1024, 1024], BF16, kind="ExternalOutput").ap()
        P.dma("sp", dbg, g0c[7].ap(), [("g0c", 7)], ["dbg"])
        P.end_phase(last=True)
        return nc
    P.end_phase()
    h1 = nc.dram_tensor("h1", [D, TH], F32)
    hn1c = [nc.dram_tensor("hn1c%d" % k, [D, 512], BF16) for k in range(NCH)]
    ghn = [nc.dram_tensor("ghn%d" % k, [2 * D, 512], BF16) for k in range(NCH)]
    d1 = {"xT": ext("xh", [D, TH]), "w_out": ext("w_out0", [D, D]), "g_ffn": ext("g_ffn0", [128, NKC]),
          "g_nxt": ext("g_nxt0", [128, NKC]), "w_gu": ext("w_gu0", [D, 2 * FH]), "w_dn": ext("w_dn0", [FH, D]),
          "sel": ext("sel", [128, 2])}
    g0v = [t.ap().rearrange("(c p) t -> p c t", p=128) for t in g0c]

    def osrc(half):
        def f(i):
            tg = half * TH + i * TT_F
            return g0v[tg // 1024][:, :, tg % 1024:tg % 1024 + TT_F]
        return f
    d1["o_srcs"] = [osrc(0), osrc(1)]
    d1["wo_perm"] = [(2 * (c // 4) + (c % 4)) if (c % 4) < 2 else (4 + 2 * (c // 4) + (c % 4) - 2) for c in range(8)]
    h1v = h1.ap().rearrange("(c p) t -> p c t", p=128)
    d1["h_dst"] = lambda i, m: h1v[:, m, i * TT_F:(i + 1) * TT_F]
    hn1v = [t.ap().rearrange("(c p) t -> p c t", p=128) for t in hn1c]
    d1["n_dst"] = lambda i, c: hn1v[i // 2][:, c, (i % 2) * TT_F:(i % 2) * TT_F + TT_F]

    def after1(i):
        if i % 2 == 1:
            k = i // 2
            P.allgather(ghn[k].ap().opt(), hn1c[k].ap().opt(), PAIRS,
                        [("out_n", ii, c) for ii in (2 * k, 2 * k + 1) for c in range(NKC)], [("ghn", k)])
    d1["after_tile"] = after1
    emit_ffn(P, nc, TH, "mid", d1)
    if nph == 2:
        dbg = nc.dram_tensor("dbg", [2 * D, 512], BF16, kind="ExternalOutput").ap()
        P.dma("sp", dbg, ghn[7].ap(), [("ghn", 7)], ["dbg"])
        P.end_phase(last=True)
        return nc
    P.end_phase()
    o1c = [nc.dram_tensor("o1c%d" % k, [512, 1024], BF16) for k in range(NCH)]
    g1c = [nc.dram_tensor("g1c%d" % k, [1024, 1024], BF16) for k in range(NCH)]
    d2 = {"w_in": ext("w_in1", [D, WIN_IN]), "cst": ext("cst1", [128, C2_N]), "cb": ext("cb1", [128, B2_N]),
          "w1kv": ext("w1kv", [128, 2048]), "poskv": ext("poskv", [128, 32]), "w2kv": ext("w2kv", [64, 128]),
          "qbias": ext("qbias", [3, NR, 512]), "zexp": ext("zexp", [128, T])}
    ghv = [t.ap().rearrange("(r c p) t -> p r c t", p=128, c=NKC) for t in ghn]
    d2["hn_src"] = lambda i, c: ghv[i % 8][:, i // 8, c, :]
    o1v = [t.ap().rearrange("(r p) t -> p r t", p=64) for t in o1c]
    d2["o_dst"] = lambda n: o1v[n // 2][:, :, (n % 2) * 512:(n % 2) * 512 + 512]

    def after2(n):
        if n % 2 == 1:
            k = n // 2
            P.allgather(g1c[k].ap().opt(), o1c[k].ap().opt(), PAIRS, [("oT_out", 2 * k), ("oT_out", 2 * k + 1)], [("g1c", k)])
    d2["after_tile"] = after2
    emit_mix1(P, nc, T, d2, None)
    if nph == 3:
        dbg = nc.dram_tensor("dbg", [1024, 1024], BF16, kind="ExternalOutput").ap()
        P.dma("sp", dbg, g1c[7].ap(), [("g1c", 7)], ["dbg"])
        P.end_phase(last=True)
        return nc
    P.end_phase()
    d3 = {"xT": h1.ap(), "w_out": ext("w_out1", [D, D]), "g_ffn": ext("g_ffn1", [128, NKC]),
          "g_nxt": ext("g_fin", [128, NKC]), "w_gu": ext("w_gu1", [D, 2 * FH]), "w_dn": ext("w_dn1", [FH, D]),
          "sel": d1["sel"]}
    g1v = [t.ap().rearrange("(c p) t -> p c t", p=128) for t in g1c]

    def osrc1(half):
        def f(i):
            tg = half * TH + i * TT_F
            return g1v[tg // 1024][:, :, tg % 1024:tg % 1024 + TT_F]
        return f
    d3["o_srcs"] = [osrc1(0), osrc1(1)]
    d3["out_n"] = nc.dram_tensor("out_n", [D, TH], F32, kind="ExternalOutput").ap()
    emit_ffn(P, nc, TH, "fin", d3)
    P.end_phase(last=True)
    return nc


def kernel(x, mix_norm, ffn_norm, final_norm, even_w_in, hgrn_lower_bounds, hgrn_out_norm,
           ret_out_norm, even_w_out, odd_w_in, cmp_pos_k, cmp_pos_v, cmp_w1_k, cmp_w2_k,
           cmp_w1_v, cmp_w2_v, odd_w_out, ffn_w_gate_up, ffn_w_down):
    f = lambda a: np.asarray(a, np.float32)
    x = f(x)
    cores = list(range(8))
    xT = [np.ascontiguousarray(x[b].T) for b in range(B)]
    nc = _prog("fused", build_fused)
    ins = _fused_inputs(x, xT, mix_norm, ffn_norm, final_norm, even_w_in, hgrn_lower_bounds, hgrn_out_norm,
                        ret_out_norm, even_w_out, odd_w_in, cmp_pos_k, cmp_pos_v, cmp_w1_k, cmp_w2_k,
                        cmp_w1_v, cmp_w2_v, odd_w_out, ffn_w_gate_up, ffn_w_down)
    res = run_bass_kernel_spmd(nc, ins, core_ids=cores).results
    out = np.empty((B, T, D), np.float32)
    for c in cores:
        b, j = c // 2, c % 2
        out[b, j * TH:(j + 1) * TH, :] = res[c]["out_n"].T
    return out


def _fused_inputs(x, xT, mix_norm, ffn_norm, final_norm, even_w_in, hgrn_lower_bounds, hgrn_out_norm,
                  ret_out_norm, even_w_out, odd_w_in, cmp_pos_k, cmp_pos_v, cmp_w1_k, cmp_w2_k,
                  cmp_w1_v, cmp_w2_v, odd_w_out, ffn_w_gate_up, ffn_w_down):
    f = lambda a: np.asarray(a, np.float32)
    cores = list(range(8))
    ins = []
    for c in cores:
        b, j = c // 2, c % 2
        m0 = _mix0_inputs(xT[b], f(even_w_in)[0], f(hgrn_lower_bounds), f(hgrn_out_norm)[0], f(ret_out_norm)[0],
                          f(mix_norm)[0], j)
        d = {"xT": m0["xT"], "w_in0": m0["w_in"], "g_mix0": m0["g_mix"], "cst0": m0["cst"], "ident": m0["ident"],
             "lbp": m0["lbp"], "wn": m0["wn"]}
        d.update({"xh": np.ascontiguousarray(xT[b][:, j * TH:(j + 1) * TH]), "w_out0": f(even_w_out)[0],
                  "g_ffn0": gvec(f(ffn_norm)[0]), "g_nxt0": gvec(f(mix_norm)[1]), "w_gu0": f(ffn_w_gate_up)[0],
                  "w_dn0": f(ffn_w_down)[0],
                  "sel": np.ascontiguousarray(np.tile(np.array([[1.0 - j, float(j)]], np.float32), (128, 1)))})
        consts, _ = mix1_consts(j, T)
        mw = mix1_weights(f(odd_w_in)[0], f(cmp_pos_k)[0], f(cmp_pos_v)[0], f(cmp_w1_k)[0], f(cmp_w2_k)[0],
                          f(cmp_w1_v)[0], f(cmp_w2_v)[0], j)
        d.update({"w_in1": mw["w_in"], "cst1": consts["cst"], "cb1": consts["cb"], "w1kv": mw["w1kv"], "poskv": mw["poskv"],
                  "w2kv": mw["w2kv"], "qbias": consts["qbias"], "zexp": consts["zexp"]})
        d.update({"w_out1": f(odd_w_out)[0], "g_ffn1": gvec(f(ffn_norm)[1]), "g_fin": gvec(f(final_norm)),
                  "w_gu1": f(ffn_w_gate_up)[1], "w_dn1": f(ffn_w_down)[1]})
        ins.append(d)
    return ins
```
